# Optimizing a Trainium2 kernel written in Bass

```python
import jax, jax.numpy as jnp
from jax import lax
import numpy as np

D_MODEL = 2048
BATCH = 4
SEQ = 2048
DEPTH = 2

D_MIX = D_MODEL
D_BRANCH = D_MIX // 4
RWKV_HEAD = 64
RWKV_HEADS = D_BRANCH // RWKV_HEAD
RWKV_LORA_W = 64
RWKV_LORA_A = 64
RWKV_GN_EPS = 64e-5
ATT_HEAD = 64
ATT_Q_HEADS = D_BRANCH // ATT_HEAD
ATT_KV_HEADS = 2
ATT_GROUP = ATT_Q_HEADS // ATT_KV_HEADS
WINDOW = 128
ATT_BLOCK = WINDOW
NEG_INF = -1e30
POOL_WINDOWS = (2, 4, 8, 16)
POOL_GROUPS = len(POOL_WINDOWS)
POOL_CH = D_BRANCH // POOL_GROUPS
SGU_CHUNK = 128
SGU_GROUPS = 4
SGU_CH = D_BRANCH // SGU_GROUPS
LN_EPS = 1e-5
NORM_EPS = 1e-6

A_COLS = 3 * D_BRANCH + RWKV_LORA_W + RWKV_LORA_A
B_COLS = D_BRANCH + 2 * ATT_KV_HEADS * ATT_HEAD
C_COLS = D_BRANCH
D_COLS = 2 * D_BRANCH
G_COLS = D_MIX
D_IN = A_COLS + B_COLS + C_COLS + D_COLS + G_COLS
SPLITS = (A_COLS, A_COLS + B_COLS, A_COLS + B_COLS + C_COLS, A_COLS + B_COLS + C_COLS + D_COLS)

kernel_name = 'hybrid_parallel_rwkv7_swa_pool_sgu'


def rms_norm(x, w):
    xf = x.astype(jnp.float32)
    y = xf * lax.rsqrt(jnp.mean(xf * xf, axis=-1, keepdims=True) + NORM_EPS)
    return (y * w.astype(jnp.float32)).astype(x.dtype)


def token_shift(z, mu):
    z_prev = jnp.pad(z, ((0, 0), (1, 0), (0, 0)))[:, :-1]
    return z + mu * (z_prev - z)


def rwkv7_mixer(z, w0, w_up, a0, a_up, k_k, k_a, r_k, ln_w, ln_b):
    B_, T, _ = z.shape
    H, N = RWKV_HEADS, RWKV_HEAD
    zf = z.astype(jnp.float32)
    r, k, v, wd, ad = jnp.split(zf, [D_BRANCH, 2 * D_BRANCH, 3 * D_BRANCH, 3 * D_BRANCH + RWKV_LORA_W], axis=-1)
    w_log = -jax.nn.softplus(-(w0 + jnp.tanh(wd) @ w_up)) - 0.5
    decay = jnp.exp(-jnp.exp(w_log))
    a = jax.nn.sigmoid(a0 + ad @ a_up)
    heads = lambda t: t.reshape(B_, T, H, N)
    kk = heads(k * k_k)
    kk = kk / jnp.maximum(jnp.sqrt(jnp.sum(kk * kk, axis=-1, keepdims=True)), 1e-12)
    k = k * (1.0 + (a - 1.0) * k_a)
    r, k, v, decay, a = heads(r), heads(k), heads(v), heads(decay), heads(a)
    a_vec = -kk
    b_vec = kk * a

    def step(S, inp):
        r_t, w_t, k_t, v_t, a_t, b_t = inp
        sa = jnp.einsum('bhij,bhj->bhi', S, a_t)
        S = S * w_t[:, :, None, :] + sa[..., None] * b_t[:, :, None, :] + v_t[..., None] * k_t[:, :, None, :]
        y = jnp.einsum('bhij,bhj->bhi', S, r_t)
        return S, y

    xs = tuple(jnp.moveaxis(t, 1, 0) for t in (r, decay, k, v, a_vec, b_vec))
    S0 = jnp.zeros((B_, H, N, N), jnp.float32)
    _, y = lax.scan(step, S0, xs)
    y = jnp.moveaxis(y, 0, 1)
    mu = jnp.mean(y, axis=-1, keepdims=True)
    var = jnp.mean(jnp.square(y - mu), axis=-1, keepdims=True)
    y = ((y - mu) * lax.rsqrt(var + RWKV_GN_EPS)).reshape(B_, T, D_BRANCH) * ln_w + ln_b
    bonus = jnp.sum(r * k * r_k, axis=-1, keepdims=True) * v
    y = y + bonus.reshape(B_, T, D_BRANCH)
    return y.astype(z.dtype)


def alibi_slopes(n_heads):
    return jnp.exp2(-8.0 * (jnp.arange(n_heads) + 1) / n_heads).astype(jnp.float32)


def swa_sink_attention(z, sinks):
    B_, T, _ = z.shape
    NB = T // ATT_BLOCK
    KV, G, HD, BLK = ATT_KV_HEADS, ATT_GROUP, ATT_HEAD, ATT_BLOCK
    zf = z.astype(jnp.float32)
    q, k, v = jnp.split(zf, [D_BRANCH, D_BRANCH + KV * HD], axis=-1)
    q = q.reshape(B_, NB, BLK, KV, G, HD)
    k = k.reshape(B_, NB, BLK, KV, HD)
    v = v.reshape(B_, NB, BLK, KV, HD)

    def with_prev(t):
        prev = jnp.pad(t, ((0, 0), (1, 0), (0, 0), (0, 0), (0, 0)))[:, :-1]
        return jnp.concatenate([prev, t], axis=2)

    kw, vw = with_prev(k), with_prev(v)
    s = jnp.einsum('bnqkgd,bnskd->bnkgqs', q, kw) * (HD ** -0.5)
    qi = jnp.arange(BLK)[:, None]
    sj = jnp.arange(2 * BLK)[None, :]
    dist = qi + BLK - sj
    kpos = jnp.arange(NB)[:, None, None] * BLK + sj[None] - BLK
    valid = (dist >= 0)[None] & (dist < WINDOW)[None] & (kpos >= 0)
    slopes = alibi_slopes(ATT_Q_HEADS).reshape(KV, G)
    s = s - slopes[:, :, None, None] * dist.astype(jnp.float32)
    s = jnp.where(valid[None, :, None, None], s, NEG_INF)
    sink = jnp.broadcast_to(sinks.astype(jnp.float32).reshape(KV, G)[None, None, :, :, None, None], s.shape[:-1] + (1,))
    p = jax.nn.softmax(jnp.concatenate([s, sink], axis=-1), axis=-1)[..., :-1]
    o = jnp.einsum('bnkgqs,bnskd->bnqkgd', p, vw)
    return o.reshape(B_, T, D_BRANCH).astype(z.dtype)


def pool_mixer(z, pool_w, pool_scale):
    B_, T, _ = z.shape
    zg = z.astype(jnp.float32).reshape(B_, T, POOL_GROUPS, POOL_CH)
    csum = jnp.cumsum(zg, axis=1)
    pos = jnp.arange(T) + 1
    outs = []
    for g, w in enumerate(POOL_WINDOWS):
        c = csum[:, :, g]
        lag = jnp.pad(c, ((0, 0), (w, 0), (0, 0)))[:, :T]
        cnt = jnp.minimum(pos, w).astype(jnp.float32)[None, :, None]
        outs.append((c - lag) / cnt - zg[:, :, g])
    pooled = jnp.stack(outs, axis=2)
    y = jnp.einsum('btgc,gcd->btgd', pooled, pool_w)
    return (y.reshape(B_, T, D_BRANCH) * pool_scale).astype(z.dtype)


def spatial_gating(z, norm_w, sgu_w, sgu_b):
    B_, T, _ = z.shape
    NC = T // SGU_CHUNK
    zf = jax.nn.gelu(z.astype(jnp.float32), approximate=False)
    u, v = jnp.split(zf, 2, axis=-1)
    mu = jnp.mean(v, axis=-1, keepdims=True)
    var = jnp.mean(jnp.square(v - mu), axis=-1, keepdims=True)
    v = (v - mu) * lax.rsqrt(var + LN_EPS) * norm_w
    v = v.reshape(B_, NC, SGU_CHUNK, SGU_GROUPS, SGU_CH)
    causal = jnp.tril(jnp.ones((SGU_CHUNK, SGU_CHUNK), dtype=bool))
    w_s = jnp.where(causal[None], sgu_w, 0)
    s = jnp.einsum('gij,bcjgd->bcigd', w_s, v) + jnp.swapaxes(sgu_b, 0, 1)[:, :, None]
    return (u * s.reshape(B_, T, D_BRANCH)).astype(z.dtype)


def hybrid_layer(x, pre_w, post_w, w_in, mu, w0, w_up, a0, a_up, k_k, k_a, r_k, ln_w, ln_b,
                 sinks, pool_w, pool_scale, sgu_norm_w, sgu_w, sgu_b, w_out):
    h = rms_norm(x, pre_w)
    proj = h @ w_in
    za, zb, zc, zd, gate = jnp.split(proj, SPLITS, axis=-1)
    ya = rwkv7_mixer(token_shift(za, mu), w0, w_up, a0, a_up, k_k, k_a, r_k, ln_w, ln_b)
    yb = swa_sink_attention(zb, sinks)
    yc = pool_mixer(zc, pool_w, pool_scale)
    yd = spatial_gating(zd, sgu_norm_w, sgu_w, sgu_b)
    y = jnp.concatenate([ya, yb, yc, yd], axis=-1) * jax.nn.silu(gate)
    return x + rms_norm(y @ w_out, post_w)


def setup_inputs(seed: int = 0) -> dict:
    key = jax.random.key(seed)
    ks = jax.random.split(key, 24)
    L = DEPTH
    nrm = lambda k, shape, s: jax.random.normal(k, shape, jnp.float32) * s
    return {
        'x': nrm(ks[0], (BATCH, SEQ, D_MODEL), 1.0),
        'pre_norm_w': 1.0 + nrm(ks[1], (L, D_MODEL), 0.02),
        'post_norm_w': 1.0 + nrm(ks[2], (L, D_MODEL), 0.02),
        'w_in': nrm(ks[3], (L, D_MODEL, D_IN), D_MODEL ** -0.5),
        'shift_mu': jax.random.uniform(ks[4], (L, A_COLS), jnp.float32),
        'rwkv_w0': jax.random.uniform(ks[5], (L, D_BRANCH), jnp.float32, -4.0, 1.0),
        'rwkv_w_up': nrm(ks[6], (L, RWKV_LORA_W, D_BRANCH), 0.1),
        'rwkv_a0': nrm(ks[7], (L, D_BRANCH), 0.1),
        'rwkv_a_up': nrm(ks[8], (L, RWKV_LORA_A, D_BRANCH), 0.1),
        'rwkv_k_k': 0.85 + nrm(ks[9], (L, D_BRANCH), 0.02),
        'rwkv_k_a': 1.0 + nrm(ks[10], (L, D_BRANCH), 0.02),
        'rwkv_r_k': nrm(ks[11], (L, RWKV_HEADS, RWKV_HEAD), 0.1),
        'rwkv_ln_w': 1.0 + nrm(ks[12], (L, D_BRANCH), 0.02),
        'rwkv_ln_b': nrm(ks[13], (L, D_BRANCH), 0.02),
        'attn_sinks': nrm(ks[14], (L, ATT_Q_HEADS), 0.5),
        'pool_w': nrm(ks[15], (L, POOL_GROUPS, POOL_CH, POOL_CH), POOL_CH ** -0.5),
        'pool_scale': 1.0 + nrm(ks[16], (L, D_BRANCH), 0.02),
        'sgu_norm_w': 1.0 + nrm(ks[17], (L, D_BRANCH), 0.02),
        'sgu_w': nrm(ks[18], (L, SGU_GROUPS, SGU_CHUNK, SGU_CHUNK), 0.5 * SGU_CHUNK ** -0.5),
        'sgu_b': 1.0 + nrm(ks[19], (L, SGU_GROUPS, SGU_CHUNK), 0.02),
        'w_out': nrm(ks[20], (L, D_MIX, D_MODEL), D_MIX ** -0.5),
    }


def reference(x, pre_norm_w, post_norm_w, w_in, shift_mu, rwkv_w0, rwkv_w_up, rwkv_a0, rwkv_a_up,
              rwkv_k_k, rwkv_k_a, rwkv_r_k, rwkv_ln_w, rwkv_ln_b, attn_sinks, pool_w, pool_scale,
              sgu_norm_w, sgu_w, sgu_b, w_out):
    h = x
    for l in range(DEPTH):
        h = hybrid_layer(h, pre_norm_w[l], post_norm_w[l], w_in[l], shift_mu[l], rwkv_w0[l], rwkv_w_up[l],
                         rwkv_a0[l], rwkv_a_up[l], rwkv_k_k[l], rwkv_k_a[l], rwkv_r_k[l], rwkv_ln_w[l],
                         rwkv_ln_b[l], attn_sinks[l], pool_w[l], pool_scale[l], sgu_norm_w[l], sgu_w[l],
                         sgu_b[l], w_out[l])
    return h
```

```python
import contextlib
import types
import numpy as np
import ml_dtypes
import concourse.bass as bass
import concourse.mybir as mybir
from concourse.bass_utils import run_bass_kernel_spmd

F32 = mybir.dt.float32
BF16 = mybir.dt.bfloat16
AF = mybir.ActivationFunctionType
ALU = mybir.AluOpType
AX = mybir.AxisListType

D = 2048
SEQ = 2048
TB = 512
NCOL = 6144
C0 = -0.6065306597126334
GN_EPS = 64e-5
STAGGER = 0
NFILL = 0
LN_EPS = 1e-5
NORM_EPS = 1e-6

G_X, G_A0, G_Q, G_GB, G_C, G_GC, G_U, G_VD, G_GD = 0, 1, 5, 6, 7, 8, 9, 10, 11


def perm_index():
    idx = []
    rng = lambda a, n: list(range(a, a + n))
    idx += rng(1536, 128)
    idx += rng(2176, 64) + rng(2176, 64)
    idx += rng(2240, 64) + rng(2240, 64)
    idx += rng(2304, 128)
    for p in range(4):
        idx += rng(128 * p, 128) + rng(512 + 128 * p, 128) + rng(1024 + 128 * p, 128) + rng(3968 + 128 * p, 128)
    idx += rng(1664, 512)
    idx += rng(3968 + 512, 512)
    idx += rng(2432, 512)
    idx += rng(3968 + 1024, 512)
    idx += rng(2944, 512)
    idx += rng(3456, 512)
    idx += rng(3968 + 1536, 512)
    assert len(idx) == NCOL
    return np.array(idx)


CB = {}
_o = 0
for _n, _w in [("ident", 128), ("sl4", 512), ("suui2", 512), ("bd2", 256), ("bones", 128),
               ("onesA", 128), ("onesB", 128), ("o512", 128), ("hsel", 2), ("E0", 1024), ("E1", 1024), ("ui4", 512)]:
    CB[_n] = (_o, _o + _w)
    _o += _w
NCB = _o
NCBS = CB["E0"][0]
NPC = 61


def make_consts():
    cb = np.zeros((128, NCB), np.float32)
    p = np.arange(128)[:, None]
    f = np.arange(128)[None, :]
    cb[:, slice(*CB["ident"])] = (p == f)
    cb[:, slice(*CB["sl4"])] = np.tile((f < p), (1, 4))
    cb[:, slice(*CB["suui2"])] = np.tile(np.concatenate([(p < f), (p <= f)], 1), (1, 2))
    cb[:, slice(*CB["ui4"])] = np.tile((p <= f), (1, 4))
    cb[:, slice(*CB["bd2"])] = np.tile(((p // 64) == (f // 64)), (1, 2))
    cb[:, slice(*CB["bones"])] = ((p // 64) == (f // 64))
    slopes = 2.0 ** (-(np.arange(8) + 1.0))
    for g in range(2):
        E = np.zeros((128, 8, 128), np.float64)
        for hh in range(4):
            sl = slopes[4 * g + hh]
            dist_cur = (f - p).astype(np.float64)
            E[:, hh * 2 + 1, :] = np.where(dist_cur >= 0, np.exp(-sl * np.maximum(dist_cur, 0)), 0.0)
            dist_prev = dist_cur + 128
            E[:, hh * 2 + 0, :] = np.where(dist_prev < 128, np.exp(-sl * dist_prev), 0.0)
        cb[:, slice(*CB["E%d" % g])] = E.reshape(128, 1024)
    cb[:, CB["onesA"][0]:CB["onesA"][0] + 64] = 1.0
    cb[:, CB["onesB"][0] + 64:CB["onesB"][1]] = 1.0
    cb[:, slice(*CB["o512"])] = 1.0 / 512.0
    cb[0:64, CB["hsel"][0]] = 1.0
    cb[64:128, CB["hsel"][0] + 1] = 1.0
    cf = np.zeros((128, 512 + 2), np.float32)
    rm = np.ones(512, np.float32)
    rm[::128] = 0.0
    cf[:, 0:512] = rm[None]
    cf[0:64, 512] = 1.0
    cf[64:128, 513] = 1.0
    pos = np.arange(512)
    ic = np.zeros((128, 4, 512), np.float32)
    for g, w in enumerate((2, 4, 8, 16)):
        ic[:, g, :] = (1.0 / np.minimum(pos + 1, w))[None]
    return cb.astype(ml_dtypes.bfloat16), cf, ic.reshape(128, 2048)


def freeze(fn):
    if fn.__closure__ is None:
        return fn
    cells = []
    for c in fn.__closure__:
        try:
            cells.append(types.CellType(c.cell_contents))
        except ValueError:
            cells.append(c)
    g = types.FunctionType(fn.__code__, fn.__globals__, fn.__name__, fn.__defaults__, tuple(cells))
    g.__kwdefaults__ = fn.__kwdefaults__
    return g


class Buf:
    __slots__ = ("name", "w", "r")

    def __init__(self, name):
        self.name = name
        self.w = None
        self.r = []


class Tl:
    def __init__(self, t, name, psum=False):
        self.t = t
        self.b = Buf(name)
        self.psum = psum

    def __getitem__(self, k):
        return self.t[k]


class Sched:
    ENG = ["pe", "act", "dve", "pool", "sp"]

    def __init__(self, nc, stack, n_dma_sems=16):
        self.nc = nc
        self.stack = stack
        self.sem = {}
        self.cnt = {}
        for k in ["pe", "act", "dve", "pool"]:
            self.sem[k] = stack.enter_context(nc.semaphore("s_" + k))
            self.cnt[k] = 0
        self.dma_sems = []
        for i in range(n_dma_sems):
            key = "dma%d" % i
            self.sem[key] = stack.enter_context(nc.semaphore("s_" + key))
            self.cnt[key] = 0
            self.dma_sems.append(key)
        self.dma_rr = 0
        self.seen = {}
        self.prog = {e: [] for e in self.ENG}
        self.uid = 0
        self.pending = {}
        self.scopes = {}
        self.sw_sems = []

    def _wait(self, e, tok):
        if tok is None:
            return
        k, v = tok
        if e == "pe" and k == "pe":
            return
        if self.seen.get((e, k), 0) >= v:
            return
        self.prog[e].append(("wait", k, v))
        self.seen[(e, k)] = v

    def deps(self, e, reads, writes):
        for t in reads:
            self._wait(e, t.b.w)
            if t.psum:
                for tok in t.b.r:
                    if tok[0] != e:
                        self._wait(e, tok)
        for t in writes:
            self._wait(e, t.b.w)
            for tok in t.b.r:
                self._wait(e, tok)

    def commit(self, tok, reads, writes):
        for t in reads:
            if t.psum:
                t.b.r = [tok]
            else:
                t.b.r.append(tok)
        for t in writes:
            t.b.w = tok
            t.b.r = []

    def op(self, e, fn, r=(), w=()):
        self.deps(e, r, w)
        fn = freeze(fn)
        self.cnt[e] += 1
        self.prog[e].append(("ins", fn, e, 1))
        tok = (e, self.cnt[e])
        self.commit(tok, r, w)
        return tok

    def mm(self, fn, r=(), w=(), last=True):
        e = "pe"
        self.deps(e, r, w)
        fn = freeze(fn)
        if last:
            self.cnt[e] += 1
            self.prog[e].append(("ins", fn, e, 1))
            tok = (e, self.cnt[e])
        else:
            self.prog[e].append(("ins", fn, None, 0))
            tok = (e, self.cnt[e] + 1)
        self.commit(tok, r, w)
        return tok

    def dma(self, q, out, in_, r=(), w=(), fresh=False, **kw):
        if fresh:
            key = "swd%d" % len(self.sem)
            self.sem[key] = self.stack.enter_context(self.nc.semaphore("s_" + key))
            self.cnt[key] = 0
            self.sw_sems.append(key)
        else:
            key = self.dma_sems[self.dma_rr % len(self.dma_sems)]
            self.dma_rr += 1
        self._wait(q, (key, self.cnt[key]) if self.cnt[key] else None)
        self.deps(q, r, w)
        fn = lambda eng, out=out, in_=in_, kw=kw: eng.dma_start(out=out, in_=in_, **kw)
        self.cnt[key] += 16
        self.prog[q].append(("ins", fn, key, 16))
        tok = (key, self.cnt[key])
        self.commit(tok, r, w)
        return tok

    def release(self, tiles):
        for t in tiles:
            for tok in ([t.b.w] if t.b.w else []) + list(t.b.r):
                k, v = tok
                if self.pending.get(k, 0) < v:
                    self.pending[k] = v

    def barrier(self):
        for e in self.ENG:
            for k in ["pe", "act", "dve", "pool"] + self.dma_sems + self.sw_sems:
                if self.cnt[k]:
                    self._wait(e, (k, self.cnt[k]))

    def emit(self):
        nc = self.nc

        def replay(name):
            def run(eng):
                for it in self.prog[name]:
                    if it[0] == "wait":
                        eng.wait_ge(self.sem[it[1]], it[2])
                    else:
                        ins = it[1](eng)
                        if it[2] is not None:
                            ins.then_inc(self.sem[it[2]], it[3])
            return run

        with nc.Block() as block:
            block.tensor(replay("pe"))
            block.scalar(replay("act"))
            block.vector(replay("dve"))
            block.gpsimd(replay("pool"))
            block.sync(replay("sp"))


class _Stop(Exception):
    pass


def build(n_blocks=4, n_layers=2, dbg=None, stop_after=None):
    nc = bass.Bass("TRN2", target_bir_lowering=False)
    stack = contextlib.ExitStack()
    S = Sched(nc, stack)
    L = 2

    def din(name, shape, dt=F32):
        return nc.dram_tensor(name, list(shape), dt, kind="ExternalInput").ap()

    x_d = din("x", [SEQ, D])
    win_d = din("w_in_p", [L, D, NCOL])
    wout_d = din("w_out", [L, D, D])
    pcol_d = din("pcol", [128, L * NPC])
    postw_d = din("post_norm_w", [L, D])
    prew_d = din("pre_norm_w", [L, D])
    lnw_d = din("rwkv_ln_w", [L, 512])
    lnb_d = din("rwkv_ln_b", [L, 512])
    sgub_d = din("sgu_b", [L, 512])
    wup_d = din("rwkv_w_up", [L, 64, 512])
    aup_d = din("rwkv_a_up", [L, 64, 512])
    poolw_d = din("pool_w", [L, 4, 128, 128])
    sguwT_d = din("sgu_wT", [L, 4, 128, 128])
    cb_d = din("cbf", [128, NCB], BF16)
    cf_d = din("cf32", [128, 514])
    ic_d = din("invcnt", [128, 2048])
    out_d = nc.dram_tensor("out", [SEQ, D], F32, kind="ExternalOutput").ap()
    dbg_d = {}
    if dbg:
        for nm, shp in dbg.items():
            dbg_d[nm] = nc.dram_tensor("dbg_" + nm, list(shp), F32, kind="ExternalOutput").ap()
    wsc_in = nc.dram_tensor("wsc_in", [L, 12, 128, 16 * 512], BF16).ap()
    wsc_out = nc.dram_tensor("wsc_out", [L, 4, 128, 16 * 512], BF16).ap()
    wsc_in_b = [[Tl(None, "wsi%d_%d" % (l, g)) for g in range(12)] for l in range(L)]
    wsc_out_b = [[Tl(None, "wso%d_%d" % (l, g)) for g in range(4)] for l in range(L)]

    def sb(stk, name, shape, dt=F32):
        S.uid += 1
        nm = "%s_%d" % (name, S.uid)
        t = Tl(stk.enter_context(nc.sbuf_tensor(nm, list(shape), dt)), nm)
        t.b.r = list(S.pending.items())
        S.scopes.setdefault(id(stk), []).append(t)
        return t

    def rel(stk):
        S.release(S.scopes.pop(id(stk), []))

    def ps(stk, name, shape, dt=F32):
        S.uid += 1
        nm = "%s_%d" % (name, S.uid)
        return Tl(stk.enter_context(nc.psum_tensor(nm, list(shape), dt)), nm, psum=True)

    def ckpt(name):
        if stop_after == name:
            raise _Stop()

    top = stack
    xb = [sb(top, "xblk%d" % i, [128, D]) for i in range(4)]
    hT = sb(top, "hT", [128, 16, TB], BF16)
    yT = sb(top, "yT", [128, 16, TB], BF16)
    wbuf = [sb(top, "wbuf%d" % i, [128, 16, 512], BF16) for i in range(2)]
    cb = sb(top, "cb", [128, NCBS], BF16)
    cf = sb(top, "cf", [128, 514])
    pcol = sb(top, "pcol", [128, L * NPC])
    omka = sb(top, "omka", [128, L * 4])
    esink = sb(top, "esink", [128, L * 4])
    csc = {nm: nc.dram_tensor("csc_" + nm, [128, L * 512], BF16).ap() for nm in ("wup", "aup", "poolw", "wsT")}
    csc_b = {nm: Tl(None, "csc_" + nm) for nm in csc}
    Hc = sb(top, "Hc", [128, L, 4, 128])
    carryA = sb(top, "carryA", [128, L, 13])
    kxm = sb(top, "kxm", [128, L, 4, 640], BF16)
    vpad = sb(top, "vpad", [128, L, 5, 4, 128], BF16)
    poolh = sb(top, "poolh", [128, L, 4, 16])
    qTp = [sb(top, "qTp%d" % c, [128, TB], BF16) for c in range(4)]
    PS = [ps(top, "ps%d" % i, [128, 512]) for i in range(8)]

    def C_(name, a=None, b=None):
        lo, hi = CB[name]
        if a is None:
            return cb[:, lo:hi]
        return cb[:, lo + a:lo + b]

    def pc(l, off, n=1):
        return pcol[:, l * NPC + off: l * NPC + off + n]

    PRE, MU, W0, A0, KK, KA, RK, PSC, SNW, SNK = 0, 16, 29, 33, 37, 41, 45, 49, 53, 57

    eng_rr = [0]

    def ew(fn, r, w, engines=("dve", "pool")):
        e = engines[eng_rr[0] % len(engines)]
        eng_rr[0] += 1
        return S.op(e, fn, r, w)

    S.dma("sp", cb[:], cb_d[:, 0:NCBS], w=[cb])
    S.dma("sp", cf[:], cf_d[:, :], w=[cf])
    S.dma("sp", pcol[:], pcol_d[:, :], w=[pcol])
    for t in (Hc, carryA, kxm, vpad, poolh):
        S.op("pool", lambda e, t=t: e.memset(t[:], 0.0), w=[t])
    with contextlib.ExitStack() as st:
        wup = sb(st, "wup", [128, L, 512], BF16)
        aup = sb(st, "aup", [128, L, 512], BF16)
        poolw = sb(st, "poolw", [128, L, 4, 128], BF16)
        wsT = sb(st, "wsT", [128, L, 4, 128], BF16)
        stg = sb(st, "stg", [128, L, 512])
        stg2 = sb(st, "stg2", [128, L, 512])
        S.op("pool", lambda e: e.memset(stg[:], 0.0), w=[stg])
        S.op("pool", lambda e: e.memset(stg2[:], 0.0), w=[stg2])
        for l in range(L):
            S.dma("sp", stg[0:64, l, :], wup_d[l, :, :], w=[stg])
            S.dma("sp", stg2[64:128, l, :], aup_d[l, :, :], w=[stg2])
        S.op("dve", lambda e: e.tensor_copy(out=wup[:], in_=stg[:]), r=[stg], w=[wup])
        S.op("dve", lambda e: e.tensor_copy(out=aup[:], in_=stg2[:]), r=[stg2], w=[aup])
        stg3 = sb(st, "stg3", [128, L, 4, 128])
        stg4 = sb(st, "stg4", [128, L, 4, 128])
        for l in range(L):
            S.dma("sp", stg3[:, l, :, :], poolw_d[l].rearrange("g c d -> c g d"), w=[stg3])
            S.dma("sp", stg4[:, l, :, :], sguwT_d[l].rearrange("g j i -> j g i"), w=[stg4])
        S.op("dve", lambda e: e.tensor_copy(out=poolw[:], in_=stg3[:]), r=[stg3], w=[poolw])
        ui4t = sb(st, "ui4t", [128, 512], BF16)
        S.dma("sp", ui4t[:], cb_d[:, CB["ui4"][0]:CB["ui4"][1]], w=[ui4t])
        for l in range(L):
            S.op("dve", lambda e, l=l: e.tensor_tensor(out=wsT[:, l, :, :].rearrange("p g i -> p (g i)"),
                                                       in0=stg4[:, l, :, :].rearrange("p g i -> p (g i)"),
                                                       in1=ui4t[:, :], op=ALU.mult), r=[stg4, ui4t], w=[wsT])
        S.dma("sp", csc["wup"], wup[:].rearrange("p l n -> p (l n)"), r=[wup], w=[csc_b["wup"]])
        S.dma("sp", csc["aup"], aup[:].rearrange("p l n -> p (l n)"), r=[aup], w=[csc_b["aup"]])
        S.dma("sp", csc["poolw"], poolw[:].rearrange("p l g n -> p (l g n)"), r=[poolw], w=[csc_b["poolw"]])
        S.dma("sp", csc["wsT"], wsT[:].rearrange("p l g n -> p (l g n)"), r=[wsT], w=[csc_b["wsT"]])
        for l in range(L):
            S.op("dve", lambda e, l=l: e.tensor_scalar(out=omka[:, l * 4:(l + 1) * 4], in0=pc(l, KA, 4), scalar1=-1.0,
                                                       scalar2=1.0, op0=ALU.mult, op1=ALU.add), r=[pcol], w=[omka])
            S.op("act", lambda e, l=l: e.activation(out=esink[:, l * 4:(l + 1) * 4], in_=pc(l, SNK, 4), func=AF.Exp),
                 r=[pcol], w=[esink])
        rel(st)

    try:
        ckpt("setup")
    except _Stop:
        S.barrier(); S.emit(); stack.close(); return nc
    def prepass(layers):
        HW = NCOL // 2
        with contextlib.ExitStack() as st:
            f32s = [sb(st, "pf%d" % i, [128, HW]) for i in range(2)]
            b16s = [sb(st, "pb%d" % i, [128, HW], BF16) for i in range(2)]
            it = 0
            for l in layers:
                for k in range(16):
                    for hf in range(2):
                        f, b_ = f32s[it % 2], b16s[it % 2]
                        it += 1
                        S.dma("sp", f[:], win_d[l, k * 128:(k + 1) * 128, hf * HW:(hf + 1) * HW], w=[f])
                        S.op("dve", lambda e, f=f, b_=b_: e.tensor_copy(out=b_[:, 0:1024], in_=f[:, 0:1024]), r=[f], w=[b_])
                        S.op("pool", lambda e, f=f, b_=b_: e.tensor_copy(out=b_[:, 1024:2048], in_=f[:, 1024:2048]), r=[f], w=[b_])
                        S.op("act", lambda e, f=f, b_=b_: e.copy(out=b_[:, 2048:HW], in_=f[:, 2048:HW]), r=[f], w=[b_])
                        S.dma("sp", wsc_in[l, hf * 6:(hf + 1) * 6, :, k * 512:(k + 1) * 512].rearrange("g p n -> p g n"),
                              b_[:].rearrange("p (g n) -> p g n", g=6), r=[b_], w=wsc_in_b[l][hf * 6:(hf + 1) * 6])
                for k in range(16):
                    f, b_ = f32s[it % 2], b16s[it % 2]
                    it += 1
                    S.dma("sp", f[:, 0:D], wout_d[l, k * 128:(k + 1) * 128, :], w=[f])
                    S.op("dve", lambda e, f=f, b_=b_: e.tensor_copy(out=b_[:, 0:1024], in_=f[:, 0:1024]), r=[f], w=[b_])
                    S.op("pool", lambda e, f=f, b_=b_: e.tensor_copy(out=b_[:, 1024:2048], in_=f[:, 1024:2048]), r=[f], w=[b_])
                    S.dma("sp", wsc_out[l, :, :, k * 512:(k + 1) * 512].rearrange("g p n -> p g n"),
                          b_[:, 0:D].rearrange("p (g n) -> p g n", g=4), r=[b_], w=wsc_out_b[l])
            S.barrier()

    try:
        ckpt("prepass")
    except _Stop:
        S.barrier(); S.emit(); stack.close(); return nc

    wslot = [0]

    converted = set()

    def load_w(kind, l, g):
        t = wbuf[wslot[0] % 2]
        wslot[0] += 1
        tl = (wsc_in_b if kind == "in" else wsc_out_b)[l][g]
        scr = (wsc_in if kind == "in" else wsc_out)[l, g, :, :]
        if (kind, l, g) not in converted:
            converted.add((kind, l, g))
            srcw = win_d if kind == "in" else wout_d
            src = srcw[l].rearrange("(k p) n -> p k n", p=128)[:, :, g * 512:(g + 1) * 512]
            S.dma("pool", t[:], src, w=[t], fresh=True)
            S.dma("sp", scr, t[:].rearrange("p k n -> p (k n)"), r=[t], w=[tl])
        else:
            S.dma("sp", t[:].rearrange("p k n -> p (k n)"), scr, r=[tl], w=[t])
        return t

    pj_rr = [0]

    def proj_chunk(wt, c):
        P = PS[pj_rr[0] % 2]
        pj_rr[0] += 1
        for k in range(16):
            S.mm(lambda e, k=k, P=P: e.matmul(P[:, :], lhsT=wt[:, k, c * 128:(c + 1) * 128], rhs=hT[:, k, :],
                                             start=(k == 0), stop=(k == 15)), r=[wt, hT], w=[P], last=(k == 15))
        return P

    def dump(name, tile_ap, tl):
        if name in dbg_d:
            S.dma("sp", dbg_d[name], tile_ap, r=[tl])

    wt_next = None
    x_pref = [None]
    fill_rr = [0]

    def layer_block(bi, l, last_layer):
        nonlocal wt_next
        first = (bi == 0)
        with contextlib.ExitStack() as st:
            prep = sb(st, "prep", [128, D])
            xn = sb(st, "xn", [128, 2, D], BF16)
            sm = sb(st, "sm", [128, 32])
            S.dma("sp", prep[:], prew_d[l:l + 1, :].partition_broadcast(128), w=[prep])
            for i in range(4):
                if l == 0:
                    S.dma("sp", xb[i][:, :], x_d[bi * TB + i * 128: bi * TB + (i + 1) * 128, :], w=[xb[i]])
                xv = xn[:, i % 2, :]
                S.op("act", lambda e, i=i, xv=xv: e.activation(out=xv, in_=xb[i][:, :], func=AF.Square,
                                                               accum_out=sm[:, i * 8:i * 8 + 1]), r=[xb[i]], w=[xn, sm])
                S.op("dve", lambda e, i=i: e.tensor_scalar(out=sm[:, i * 8 + 1:i * 8 + 2], in0=sm[:, i * 8:i * 8 + 1], scalar1=1.0 / D, scalar2=NORM_EPS,
                                                      op0=ALU.mult, op1=ALU.add), r=[sm], w=[sm])
                S.op("act", lambda e, i=i: e.sqrt(out=sm[:, i * 8 + 2:i * 8 + 3], in_=sm[:, i * 8 + 1:i * 8 + 2]), r=[sm], w=[sm])
                S.op("dve", lambda e, i=i: e.reciprocal(out=sm[:, i * 8 + 3:i * 8 + 4], in_=sm[:, i * 8 + 2:i * 8 + 3]), r=[sm], w=[sm])
                S.op("dve", lambda e, i=i, xv=xv: e.scalar_tensor_tensor(out=xv, in0=xb[i][:, :], scalar=sm[:, i * 8 + 3:i * 8 + 4],
                                                                         in1=prep[:], op0=ALU.mult, op1=ALU.mult),
                     r=[xb[i], sm, prep], w=[xn])
                for half in range(2):
                    P = PS[2 + half + 2 * (i % 2)]
                    Pb = P[:, :].bitcast(BF16)
                    for kk in range(8):
                        k = half * 8 + kk
                        S.mm(lambda e, k=k, kk=kk, Pb=Pb, xv=xv: e.transpose(Pb[:, kk * 128:(kk + 1) * 128],
                                                                          xv[:, k * 128:(k + 1) * 128], C_("ident")),
                             r=[xn, cb], w=[P], last=(kk == 7))
                    S.op("act", lambda e, i=i, half=half, Pb=Pb: e.copy(
                        out=hT[:, half * 8:(half + 1) * 8, i * 128:(i + 1) * 128],
                        in_=Pb.rearrange("p (k t) -> p k t", k=8)), r=[P], w=[hT])
            rel(st)
        ckpt("N")

        with contextlib.ExitStack() as st:
            HRT = []
            for hr in range(2):
                sh = st
                lnw = sb(sh, "lnw", [128, 256])
                lnb = sb(sh, "lnb", [128, 256])
                AR = [sb(sh, "AR%d" % i, [128, 4, 2, 128], BF16) for i in range(2)]
                BT = [sb(sh, "BT%d" % i, [128, TB], BF16) for i in range(2)]
                KT = [sb(sh, "KT%d" % i, [128, TB], BF16) for i in range(2)]
                rkp = [sb(sh, "rkp%d" % i, [128, TB], BF16) for i in range(2)]
                vSb = [sb(sh, "vSb%d" % i, [128, TB], BF16) for i in range(2)]
                sgA = [sb(sh, "sgA%d" % i, [128, TB], BF16) for i in range(2)]
                rho = [sb(sh, "rho%d" % i, [128, 4]) for i in range(2)]
                sC = [sb(sh, "sC%d" % i, [128, 4]) for i in range(2)]
                HRT.append((lnw, lnb, AR, BT, KT, rkp, vSb, sgA, rho, sC))
            with contextlib.ExitStack() as pre:
                wupl = sb(pre, "wupl", [128, 512], BF16)
                aupl = sb(pre, "aupl", [128, 512], BF16)
                S.dma("sp", wupl[:], csc["wup"][:, l * 512:(l + 1) * 512], r=[csc_b["wup"]], w=[wupl])
                S.dma("sp", aupl[:], csc["aup"][:, l * 512:(l + 1) * 512], r=[csc_b["aup"]], w=[aupl])
                lora_t = sb(pre, "lora", [128, TB], BF16)
                zs = [sb(pre, "zs%d" % i, [128, TB]) for i in range(2)]
                zs_rr = [0]
                zds = [sb(pre, "zd%d" % i, [128, TB]) for i in range(2)]

                zd0s = [sb(pre, "zdc%d" % i, [128, 2]) for i in range(2)]

                def shifted(P, ci, dst_ap, dst_tl, eng2="dve"):
                    z = zs[zs_rr[0] % 2]
                    zd = zds[zs_rr[0] % 2]
                    zd0 = zd0s[zs_rr[0] % 2]
                    zs_rr[0] += 1
                    S.op("act", lambda e: e.copy(out=z[:, 0:TB], in_=P[:, :]), r=[P], w=[z])
                    S.op("dve", lambda e: e.tensor_tensor(out=zd[:, 1:TB], in0=z[:, 0:TB - 1], in1=z[:, 1:TB], op=ALU.subtract), r=[z], w=[zd])
                    S.op("dve", lambda e: e.scalar_tensor_tensor(out=dst_ap[:, 1:TB], in0=zd[:, 1:TB], scalar=pc(l, MU + ci), in1=z[:, 1:TB],
                                                                 op0=ALU.mult, op1=ALU.add), r=[z, zd, pcol], w=[dst_tl])
                    S.op("pool", lambda e: e.tensor_tensor(out=zd0[:, 0:1], in0=carryA[:, l, ci:ci + 1], in1=z[:, 0:1], op=ALU.subtract), r=[carryA, z], w=[zd0])
                    S.op("pool", lambda e: e.tensor_copy(out=carryA[:, l, ci:ci + 1], in_=z[:, TB - 1:TB]), r=[z, zd0], w=[carryA])
                    S.op("pool", lambda e: e.tensor_scalar(out=dst_ap[:, 0:1], in0=zd0[:, 0:1], scalar1=pc(l, MU + ci), scalar2=z[:, 0:1],
                                                           op0=ALU.mult, op1=ALU.add), r=[zd0, z, pcol], w=[dst_tl])

                NT = 6
                tmp = [sb(pre, "rt%d" % i, [128, TB]) for i in range(NT)]
                wt = x_pref[0] if x_pref[0] is not None else load_w("in", l, G_X)
                x_pref[0] = None
                wt_next = load_w("in", l, G_A0)
                P = proj_chunk(wt, 0)
                lraw = tmp[0]
                shifted(P, 12, lraw[:, :], lraw)
                S.op("act", lambda e: e.activation(out=lora_t[0:64, :], in_=lraw[0:64, :], func=AF.Tanh), r=[lraw], w=[lora_t])
                S.op("dve", lambda e: e.tensor_copy(out=lora_t[64:128, :], in_=lraw[64:128, :]), r=[lraw], w=[lora_t])
                for g in range(2):
                    P = proj_chunk(wt, 1 + g)
                    S.op("act", lambda e, g=g, P=P: e.activation(out=kxm[:, l, 2 * g, 128:640], in_=P[:, :], func=AF.Identity,
                                                                 scale=cf[:, 512:513]), r=[P, cf], w=[kxm])
                    S.op("dve", lambda e, g=g, P=P: e.tensor_scalar(out=kxm[:, l, 2 * g + 1, 128:640], in0=P[:, :],
                                                                    scalar1=cf[:, 513:514], scalar2=None, op0=ALU.mult),
                         r=[P, cf], w=[kxm])
                P = proj_chunk(wt, 3)
                vvT = sb(pre, "vvT", [128, TB], BF16)
                S.op("act", lambda e, P=P: e.copy(out=vvT[:, :], in_=P[:, :]), r=[P], w=[vvT])
                Pt = PS[2]
                Ptb = Pt[:, :].bitcast(BF16)
                for i in range(4):
                    S.mm(lambda e, i=i: e.transpose(Ptb[:, i * 128:(i + 1) * 128], vvT[:, i * 128:(i + 1) * 128], C_("ident")),
                         r=[vvT, cb], w=[Pt], last=(i == 3))
                Ptv = Ptb[:, 0:512].rearrange("p (i c) -> p i c", i=4)
                for vi, (src0, dst0) in enumerate([(0, 0), (0, 64), (64, 0), (64, 64)]):
                    ew(lambda e, vi=vi, src0=src0, dst0=dst0: e.tensor_copy(out=vpad[:, l, 1:5, vi, dst0:dst0 + 64],
                                                                            in_=Ptv[:, :, src0:src0 + 64]),
                       r=[Pt], w=[vpad], engines=("dve",))

                ckpt("AX")
                tmp2 = [sb(pre, "ru%d" % i, [128, TB]) for i in range(6)]
                sqbs = [sb(pre, "sqb%d" % i, [128, TB], BF16) for i in range(2)]
                for hr in range(2):
                    pairs = [2 * hr, 2 * hr + 1]
                    lnw, lnb, AR, BT, KT, rkp, vSb, sgA, rho, sC = HRT[hr]
                    S.dma("sp", lnw[:], lnw_d[l:l + 1, hr * 256:(hr + 1) * 256].partition_broadcast(128), w=[lnw])
                    S.dma("sp", lnb[:], lnb_d[l:l + 1, hr * 256:(hr + 1) * 256].partition_broadcast(128), w=[lnb])
                    for pi, p in enumerate(pairs):
                        wt = wt_next
                        nxt = p + 1
                        wt_next = load_w("in", l, G_A0 + nxt) if nxt < 4 else load_w("in", l, G_Q)
                        rS, kS, sgw, cs, av, t5 = tmp if pi == 0 else tmp2
                        P = proj_chunk(wt, 0)
                        shifted(P, p, rS[:, :], rS)
                        P = proj_chunk(wt, 1)
                        shifted(P, 4 + p, kS[:, :], kS)
                        P = proj_chunk(wt, 2)
                        shifted(P, 8 + p, vSb[pi][:, :], vSb[pi])
                        P = proj_chunk(wt, 3)
                        S.op("act", lambda e, P=P, pi=pi: e.activation(out=sgA[pi][:, :], in_=P[:, :], func=AF.Silu),
                             r=[P], w=[sgA[pi]])
                        PA, PB = PS[2], PS[3]
                        S.mm(lambda e, p=p: e.matmul(PA[:, :], lhsT=wupl[:, p * 128:(p + 1) * 128], rhs=lora_t[:, :],
                                                     start=True, stop=True), r=[wupl, lora_t], w=[PA])
                        S.mm(lambda e, p=p: e.matmul(PB[:, :], lhsT=aupl[:, p * 128:(p + 1) * 128], rhs=lora_t[:, :],
                                                     start=True, stop=True), r=[aupl, lora_t], w=[PB])
                        S.op("act", lambda e, p=p: e.activation(out=sgw[:, :], in_=PA[:, :], func=AF.Sigmoid,
                                                                bias=pc(l, W0 + p)), r=[PA, pcol], w=[sgw])
                        S.op("act", lambda e, p=p: e.activation(out=av[:, :], in_=PB[:, :], func=AF.Sigmoid,
                                                                bias=pc(l, A0 + p)), r=[PB, pcol], w=[av])
                        S.op("dve", lambda e: e.tensor_tensor_scan(out=cs[:, :], data0=cf[:, 0:512], data1=sgw[:, :], initial=0.0,
                                                                   op0=ALU.mult, op1=ALU.add), r=[cf, sgw], w=[cs])
                        S.op("act", lambda e, pi=pi: e.activation(out=rho[pi][:, :], in_=cs[:, 63:512:128], func=AF.Exp, scale=C0),
                             r=[cs], w=[rho[pi]])
                        cc = t5
                        S.op("dve", lambda e: e.tensor_tensor(out=cc[:, :].rearrange("p (c t) -> p c t", c=4),
                                                              in0=cs[:, :].rearrange("p (c t) -> p c t", c=4),
                                                              in1=cs[:, 63:512:128].unsqueeze(2).to_broadcast([128, 4, 128]),
                                                              op=ALU.subtract), r=[cs], w=[cc])
                        S.op("act", lambda e, pi=pi: e.activation(out=sC[pi][:, :], in_=cc[:, 127:512:128], func=AF.Exp, scale=C0),
                             r=[cc], w=[sC[pi]])
                        S.op("pool", lambda e: e.tensor_tensor(out=sgw[:, :], in0=cc[:, :], in1=sgw[:, :], op=ALU.subtract),
                             r=[cc, sgw], w=[sgw])
                        S.op("act", lambda e: e.activation(out=sgw[:, :], in_=sgw[:, :], func=AF.Exp, scale=C0), r=[sgw], w=[sgw])
                        S.op("act", lambda e: e.activation(out=cs[:, :], in_=cc[:, :], func=AF.Exp, scale=C0), r=[cc], w=[cs])
                        S.op("act", lambda e: e.activation(out=cc[:, :], in_=cc[:, :], func=AF.Exp, scale=-C0), r=[cc], w=[cc])
                        eprev, epos, eneg = sgw, cs, cc
                        S.op("pool", lambda e, pi=pi: e.tensor_tensor(out=AR[pi][:, :, 1, :],
                                                                      in0=rS[:, :].rearrange("p (c t) -> p c t", c=4),
                                                                      in1=epos[:, :].rearrange("p (c t) -> p c t", c=4), op=ALU.mult),
                             r=[rS, epos], w=[AR[pi]])
                        S.op("dve", lambda e, p=p: e.tensor_scalar(out=cs[:, :], in0=av[:, :], scalar1=pc(l, KA + p),
                                                                   scalar2=omka[:, l * 4 + p:l * 4 + p + 1], op0=ALU.mult, op1=ALU.add),
                             r=[av, pcol, omka], w=[cs])
                        S.op("pool", lambda e: e.tensor_tensor(out=cs[:, :], in0=cs[:, :], in1=kS[:, :], op=ALU.mult), r=[cs, kS], w=[cs])
                        kmod = cs
                        S.op("dve", lambda e, p=p, pi=pi: e.scalar_tensor_tensor(out=rkp[pi][:, :], in0=rS[:, :], scalar=pc(l, RK + p),
                                                                                in1=kmod[:, :], op0=ALU.mult, op1=ALU.mult),
                             r=[rS, pcol, kmod], w=[rkp[pi]])
                        S.op("pool", lambda e, pi=pi: e.tensor_tensor(out=KT[pi][:, :], in0=kmod[:, :], in1=eneg[:, :], op=ALU.mult),
                             r=[kmod, eneg], w=[KT[pi]])
                        kraw = rS
                        S.op("dve", lambda e, p=p: e.tensor_scalar(out=kraw[:, :], in0=kS[:, :], scalar1=pc(l, KK + p), scalar2=None,
                                                                   op0=ALU.mult), r=[kS, pcol], w=[kraw])
                        S.op("act", lambda e, sqb=sqbs[pi]: e.activation(out=sqb[:, :], in_=kraw[:, :], func=AF.Square), r=[kraw], w=[sqbs[pi]])
                    for pi, p in enumerate(pairs):
                        rS, kS, sgw, cs, av, t5 = tmp if pi == 0 else tmp2
                        PA, PB = PS[2], PS[3]
                        cc = t5
                        eprev, epos, eneg = sgw, cs, cc
                        kmod = cs
                        kraw = rS
                        S.mm(lambda e, sqb=sqbs[pi]: e.matmul(PA[:, :], lhsT=C_("bones"), rhs=sqb[:, :], start=True, stop=True),
                             r=[cb, sqbs[pi]], w=[PA])
                        nrm = kS
                        S.op("act", lambda e: e.sqrt(out=nrm[:, :], in_=PA[:, :]), r=[PA], w=[nrm])
                        S.op("dve", lambda e: e.tensor_scalar(out=nrm[:, :], in0=nrm[:, :], scalar1=1e-12, scalar2=None, op0=ALU.max),
                             r=[nrm], w=[nrm])
                        S.op("dve", lambda e: e.reciprocal(out=nrm[:, :], in_=nrm[:, :]), r=[nrm], w=[nrm])
                        kk = kraw
                        S.op("pool", lambda e: e.tensor_tensor(out=kk[:, :], in0=kraw[:, :], in1=nrm[:, :], op=ALU.mult), r=[kraw, nrm], w=[kk])
                        S.op("dve", lambda e, pi=pi: e.scalar_tensor_tensor(out=AR[pi][:, :, 0, :],
                                                                           in0=kk[:, :].rearrange("p (c t) -> p c t", c=4), scalar=-1.0,
                                                                           in1=eprev[:, :].rearrange("p (c t) -> p c t", c=4),
                                                                           op0=ALU.mult, op1=ALU.mult), r=[kk, eprev], w=[AR[pi]])
                        S.op("pool", lambda e: e.tensor_tensor(out=av[:, :], in0=av[:, :], in1=eneg[:, :], op=ALU.mult), r=[av, eneg], w=[av])
                        S.op("dve", lambda e, pi=pi: e.tensor_tensor(out=BT[pi][:, :], in0=kk[:, :], in1=av[:, :], op=ALU.mult),
                             r=[kk, av], w=[BT[pi]])
                rel(pre)
            ckpt("Apre")
            wtQ = wt_next

            def chunk_loop(hr, bk):
                sh = st
                pairs = [2 * hr, 2 * hr + 1]
                lnw, lnb, AR, BT, KT, rkp, vSb, sgA, rho, sC = HRT[hr]
                BTm = [[sb(sh, "BTm%d%d" % (i, e_), [128, 128], BF16) for e_ in range(2)] for i in range(2)]
                KTm = [[sb(sh, "KTm%d%d" % (i, e_), [128, 128], BF16) for e_ in range(2)] for i in range(2)]
                ATm = [[sb(sh, "ATm%d%d" % (i, e_), [128, 128], BF16) for e_ in range(2)] for i in range(2)]
                LA = sb(sh, "LA", [128, 4, 2, 128], BF16)
                KA = sb(sh, "KA", [128, 4, 2, 128], BF16)
                ZL0 = sb(sh, "ZL0", [128, 4, 2, 128], BF16)
                ZLp = [[sb(sh, "ZLp%d%d" % (i, j), [128, 2, 2, 128], BF16) for j in range(2)] for i in range(2)]
                LTp = [[sb(sh, "LTp%d%d" % (i, j), [128, 2, 128], BF16) for j in range(2)] for i in range(2)]
                TK = sb(sh, "TK", [128, 2, 4, 128], BF16)
                Pp = sb(sh, "Pp", [128, 2, 128], BF16)
                Ul = sb(sh, "Ul", [128, 2, 128], BF16)
                GT = sb(sh, "GT", [128, 2, 128], BF16)
                MtT = sb(sh, "MtT", [128, 2, 128], BF16)
                Hb = sb(sh, "Hb", [128, 2, 128], BF16)
                gs = sb(sh, "gs", [128, 32])
                rkt = sb(sh, "rkt", [128, 4])
                sq = sb(sh, "sq", [128, 256])
                yn = sb(sh, "yn", [128, 256])
                yab = sb(sh, "yab", [128, 256], BF16)
                for ch in range(4):
                    csl = slice(ch * 128, (ch + 1) * 128)
                    for pi in range(2):
                        for e_ in range(2):
                            hmc = cf[:, 512 + e_:513 + e_]
                            S.op("act", lambda e: e.activation(out=BTm[pi][e_][:, :], in_=BT[pi][:, csl], func=AF.Identity, scale=hmc), r=[BT[pi], cf], w=[BTm[pi][e_]])
                            S.op("pool", lambda e: e.tensor_scalar(out=KTm[pi][e_][:, :], in0=KT[pi][:, csl], scalar1=hmc, scalar2=1.0, op0=ALU.mult, op1=ALU.mult), r=[KT[pi], cf], w=[KTm[pi][e_]])
                            S.op("act", lambda e: e.activation(out=ATm[pi][e_][:, :], in_=AR[pi][:, ch, 0, :], func=AF.Identity, scale=hmc), r=[AR[pi], cf], w=[ATm[pi][e_]])
                    PLA = [PS[bk[2]], PS[bk[3]]]
                    PKA = [PS[bk[4]], PS[bk[5]]]
                    PL = PS[bk[6]]
                    for hh in range(4):
                        pi, e_ = hh // 2, hh % 2
                        o2 = slice(e_ * 256, (e_ + 1) * 256)
                        hs = slice(hh * 128, (hh + 1) * 128)
                        rhs2 = AR[pi][:, ch, :, :].rearrange("p a t -> p (a t)")
                        S.mm(lambda e: e.matmul(PLA[pi][:, o2], lhsT=BTm[pi][e_][:, :], rhs=rhs2, start=True, stop=True), r=[BTm[pi][e_], AR[pi]], w=[PLA[pi]], last=(e_ == 1))
                        S.mm(lambda e: e.matmul(PKA[pi][:, o2], lhsT=KTm[pi][e_][:, :], rhs=rhs2, start=True, stop=True), r=[KTm[pi][e_], AR[pi]], w=[PKA[pi]], last=(e_ == 1))
                        S.mm(lambda e: e.matmul(PL[:, hs], lhsT=ATm[pi][e_][:, :], rhs=BT[pi][:, csl], start=True, stop=True), r=[ATm[pi][e_], BT[pi]], w=[PL], last=(hh == 3))
                    for pi in range(2):
                        o_la = LA[:, 2 * pi:2 * pi + 2, :, :].rearrange("p h a t -> p (h a t)")
                        o_ka = KA[:, 2 * pi:2 * pi + 2, :, :].rearrange("p h a t -> p (h a t)")
                        S.op("dve", lambda e: e.tensor_tensor(out=o_la, in0=PLA[pi][:, :], in1=C_("suui2"), op=ALU.mult), r=[PLA[pi], cb], w=[LA])
                        S.op("dve", lambda e: e.tensor_tensor(out=o_ka, in0=PKA[pi][:, :], in1=C_("suui2"), op=ALU.mult), r=[PKA[pi], cb], w=[KA])
                    S.op("dve", lambda e: e.tensor_tensor(out=ZL0[:, :, 1, :], in0=PL[:, :].rearrange("p (h t) -> p h t", h=4),
                                                          in1=C_("sl4").rearrange("p (h t) -> p h t", h=4), op=ALU.mult), r=[PL, cb], w=[ZL0])
                    yield
                    Pt = PS[bk[7]]
                    Ptb = Pt[:, :].bitcast(BF16)
                    for pi in range(2):
                        srcs = [(AR[pi], AR[pi][:, ch, 0, :]), (BT[pi], BT[pi][:, csl]), (KT[pi], KT[pi][:, csl]), (vSb[pi], vSb[pi][:, csl])]
                        for qi, (tl_, ap_) in enumerate(srcs):
                            o0 = (pi * 4 + qi) * 128
                            S.mm(lambda e, ap_=ap_, o0=o0: e.transpose(Ptb[:, o0:o0 + 128], ap_, C_("ident")), r=[tl_, cb], w=[Pt],
                                 last=(pi == 1 and qi == 3))
                    S.op("dve", lambda e: e.tensor_copy(out=TK[:].rearrange("p a q c -> p (a q c)"), in_=Ptb[:, 0:1024]), r=[Pt], w=[TK])
                    PZ = PS[bk[2]]
                    for hh in range(4):
                        pi, e_ = hh // 2, hh % 2
                        S.mm(lambda e, hh=hh, pi=pi, e_=e_: e.matmul(PZ[:, hh * 128 + 64: hh * 128 + 128], lhsT=KA[:, hh, 0, :],
                                                                      rhs=TK[:, pi, 3, e_ * 64:(e_ + 1) * 64], start=True, stop=True),
                             r=[KA, TK], w=[PZ], last=(hh == 3))
                    PZv = PZ[:, :].rearrange("p (h c) -> p h c", h=4)
                    S.op("act", lambda e: e.copy(out=ZL0[:, :, 0, 64:128], in_=PZv[:, :, 64:128]), r=[PZ], w=[ZL0])
                    S.op("dve", lambda e: e.tensor_copy(out=ZL0[:, :, 0, 0:64].rearrange("p (a b) c -> p a b c", a=2),
                                                         in_=TK[:, :, 0, :].rearrange("p a (b c) -> p a b c", b=2)), r=[TK], w=[ZL0])
                    yield
                    for step in range(7):
                        for pi in range(2):
                            PZL, PLTq = PS[bk[2 + pi]], PS[bk[4]]
                            lo_ = pi * 256
                            if step == 0:
                                zlT, ltT = ZL0, LA
                                zl = [ZL0[:, 2 * pi + e_, :, :] for e_ in range(2)]
                                ltA = [LA[:, 2 * pi + e_, 0, :] for e_ in range(2)]
                            else:
                                cur = (step - 1) % 2
                                zlT, ltT = ZLp[pi][cur], LTp[pi][cur]
                                zl = [zlT[:, e_, :, :] for e_ in range(2)]
                                ltA = [ltT[:, e_, :] for e_ in range(2)]
                            for e_ in range(2):
                                rhs_ = zl[e_].rearrange("p a t -> p (a t)")
                                S.mm(lambda e: e.matmul(PZL[:, e_ * 256:(e_ + 1) * 256], lhsT=ltA[e_], rhs=rhs_, start=True, stop=True), r=[ltT, zlT], w=[PZL], last=(e_ == 1))
                            if step < 6:
                                for e_ in range(2):
                                    lk_ = zl[e_][:, 1, :]
                                    S.mm(lambda e: e.matmul(PLTq[:, lo_ + e_ * 128:lo_ + (e_ + 1) * 128], lhsT=lk_, rhs=ltA[e_], start=True, stop=True), r=[zlT, ltT], w=[PLTq], last=(e_ == 1))
                            for f_ in range(NFILL):
                                Pf = PS[fill_rr[0] % 2]
                                fill_rr[0] += 1
                                S.mm(lambda e: e.matmul(Pf[:, :], lhsT=C_("ident"), rhs=hT[:, 0, :], start=True, stop=True), r=[cb, hT], w=[Pf], last=False)
                        if hr == 0:
                            if step == 0:
                                Pq = PS[pj_rr[0] % 2]
                                pj_rr[0] += 1
                            kb = [0, 3, 6, 9, 11, 13, 15, 16]
                            for k in range(kb[step], kb[step + 1]):
                                S.mm(lambda e: e.matmul(Pq[:, :], lhsT=wtQ[:, k, ch * 128:(ch + 1) * 128], rhs=hT[:, k, :], start=(k == 0), stop=(k == 15)),
                                     r=[wtQ, hT], w=[Pq], last=(k == 15))
                            if step == 6:
                                S.op("act", lambda e: e.copy(out=qTp[ch][:, :], in_=Pq[:, :]), r=[Pq], w=[qTp[ch]])
                        for pi in range(2):
                            PZL, PLTq = PS[bk[2 + pi]], PS[bk[4]]
                            lo_ = pi * 256
                            nx = step % 2
                            PZLv = PZL[:, :].rearrange("p (e a t) -> p e a t", e=2, a=2)
                            if step == 0:
                                zprevT, zprev = ZL0, ZL0[:, 2 * pi:2 * pi + 2, 0, :]
                            else:
                                zprevT = ZLp[pi][(step - 1) % 2]
                                zprev = zprevT[:, :, 0, :]
                            if step < 6:
                                o_z = ZLp[pi][nx][:, :, 0, :]
                                o_l = ZLp[pi][nx][:, :, 1, :]
                                o_lt = LTp[pi][nx][:].rearrange("p h t -> p (h t)")
                                S.op("dve", lambda e: e.tensor_tensor(out=o_z, in0=PZLv[:, :, 0, :], in1=zprev, op=ALU.add), r=[PZL, zprevT], w=[ZLp[pi][nx]])
                                S.op("act", lambda e: e.copy(out=o_l, in_=PZLv[:, :, 1, :]), r=[PZL], w=[ZLp[pi][nx]])
                                S.op("dve", lambda e: e.tensor_copy(out=o_lt, in_=PLTq[:, lo_:lo_ + 256]), r=[PLTq], w=[LTp[pi][nx]])
                            else:
                                o_p = Pp[:, pi, :].rearrange("p (b c) -> p b c", b=2)
                                o_u = Ul[:, pi, :].rearrange("p (b c) -> p b c", b=2)
                                S.op("dve", lambda e: e.tensor_tensor(out=o_p, in0=PZLv[:, :, 0, 0:64], in1=zprev[:, :, 0:64], op=ALU.add), r=[PZL, zprevT], w=[Pp])
                                S.op("dve", lambda e: e.tensor_tensor(out=o_u, in0=PZLv[:, :, 0, 64:128], in1=zprev[:, :, 64:128], op=ALU.add), r=[PZL, zprevT], w=[Ul])
                        yield
                    yield
                    PG = PS[bk[2]]
                    for hh in range(4):
                        pi = hh // 2
                        S.mm(lambda e, hh=hh, pi=pi: e.matmul(PG[:, hh * 128:(hh + 1) * 128], lhsT=Pp[:, pi, :], rhs=LA[:, hh, 1, :], start=True, stop=True),
                             r=[Pp, LA], w=[PG], last=(hh == 3))
                    for pi in range(2):
                        for e_ in range(2):
                            hh = pi * 2 + e_
                            rows = slice(e_ * 64, (e_ + 1) * 64)
                            S.op("dve", lambda e, pi=pi, hh=hh, rows=rows: e.tensor_tensor(out=GT[rows, pi, :], in0=PG[rows, hh * 128:(hh + 1) * 128],
                                                                                          in1=AR[pi][rows, ch, 1, :], op=ALU.add), r=[PG, AR[pi]], w=[GT])
                    PM = PS[bk[6]]
                    for pi in range(2):
                        S.mm(lambda e, pi=pi: e.matmul(PM[:, pi * 128:(pi + 1) * 128], lhsT=Pp[:, pi, :], rhs=TK[:, pi, 1, :], start=True, stop=True),
                             r=[Pp, TK], w=[PM], last=(pi == 1))
                    S.op("dve", lambda e: e.tensor_tensor(out=MtT[:].rearrange("p a c -> p (a c)"), in0=PM[:, 0:256], in1=C_("bd2"), op=ALU.mult), r=[PM, cb], w=[MtT])
                    for pi in range(2):
                        p = pairs[pi]
                        S.op("act", lambda e, pi=pi, p=p: e.activation(out=Hb[:, pi, :], in_=Hc[:, l, p, :], func=AF.Identity, scale=rho[pi][:, ch:ch + 1]), r=[Hc, rho[pi]], w=[Hb])
                    PY = PS[bk[3]]
                    for pi in range(2):
                        S.mm(lambda e, pi=pi: e.matmul(PY[:, pi * 128:(pi + 1) * 128], lhsT=GT[:, pi, :], rhs=Hb[:, pi, :], start=True, stop=False),
                             r=[GT, Hb], w=[PY], last=False)
                        for e_ in range(2):
                            hh = pi * 2 + e_
                            cs_ = slice(pi * 128 + e_ * 64, pi * 128 + (e_ + 1) * 64)
                            S.mm(lambda e, pi=pi, e_=e_, hh=hh, cs_=cs_: e.matmul(PY[:, cs_], lhsT=LA[:, hh, 1, :], rhs=Ul[:, pi, e_ * 64:(e_ + 1) * 64], start=False, stop=False),
                                 r=[LA, Ul], w=[PY], last=False)
                            S.mm(lambda e, pi=pi, e_=e_, hh=hh, cs_=cs_: e.matmul(PY[:, cs_], lhsT=KA[:, hh, 1, :], rhs=TK[:, pi, 3, e_ * 64:(e_ + 1) * 64], start=False, stop=(e_ == 1)),
                                 r=[KA, TK], w=[PY], last=(pi == 1 and e_ == 1))
                    PC_ = PS[bk[4]]
                    for pi in range(2):
                        o_ = slice(pi * 128, (pi + 1) * 128)
                        S.mm(lambda e, pi=pi, o_=o_: e.matmul(PC_[:, o_], lhsT=MtT[:, pi, :], rhs=Hb[:, pi, :], start=True, stop=False), r=[MtT, Hb], w=[PC_], last=False)
                        S.mm(lambda e, pi=pi, o_=o_: e.matmul(PC_[:, o_], lhsT=C_("ident"), rhs=Hb[:, pi, :], start=False, stop=False), r=[cb, Hb], w=[PC_], last=False)
                        S.mm(lambda e, pi=pi, o_=o_: e.matmul(PC_[:, o_], lhsT=TK[:, pi, 1, :], rhs=Ul[:, pi, :], start=False, stop=False), r=[TK, Ul], w=[PC_], last=False)
                        S.mm(lambda e, pi=pi, o_=o_: e.matmul(PC_[:, o_], lhsT=TK[:, pi, 2, :], rhs=TK[:, pi, 3, :], start=False, stop=True), r=[TK], w=[PC_], last=(pi == 1))
                    for pi in range(2):
                        p = pairs[pi]
                        S.op("dve", lambda e, pi=pi, p=p: e.scalar_tensor_tensor(out=Hc[:, l, p, :], in0=PC_[:, pi * 128:(pi + 1) * 128], scalar=sC[pi][:, ch:ch + 1],
                                                                                in1=C_("bd2", 0, 128), op0=ALU.mult, op1=ALU.mult), r=[PC_, sC[pi], cb], w=[Hc])
                    pass
                    PR = PS[bk[5]]
                    for pi in range(2):
                        S.mm(lambda e, pi=pi: e.matmul(PR[:, pi * 2:(pi + 1) * 2], lhsT=rkp[pi][:, csl], rhs=C_("hsel"), start=True, stop=True), r=[rkp[pi], cb], w=[PR], last=(pi == 1))
                    PYv = PY[:, 0:256].rearrange("p (h i) -> p h i", h=4)
                    S.op("act", lambda e: e.activation(out=sq[:, :], in_=PY[:, 0:256], func=AF.Square), r=[PY], w=[sq])
                    S.op("act", lambda e: e.copy(out=rkt[:, 0:4], in_=PR[:, 0:4]), r=[PR], w=[rkt])
                    S.op("dve", lambda e: e.tensor_reduce(out=gs[:, 0:4], in_=PYv, axis=AX.X, op=ALU.add), r=[PY], w=[gs])
                    S.op("dve", lambda e: e.tensor_reduce(out=gs[:, 4:8], in_=sq[:, :].rearrange("p (h i) -> p h i", h=4), axis=AX.X, op=ALU.add), r=[sq], w=[gs])
                    S.op("dve", lambda e: e.tensor_scalar(out=gs[:, 8:12], in0=gs[:, 0:4], scalar1=1.0 / 64, scalar2=None, op0=ALU.mult), r=[gs], w=[gs])
                    S.op("dve", lambda e: e.tensor_tensor(out=gs[:, 12:16], in0=gs[:, 8:12], in1=gs[:, 8:12], op=ALU.mult), r=[gs], w=[gs])
                    S.op("dve", lambda e: e.scalar_tensor_tensor(out=gs[:, 16:20], in0=gs[:, 4:8], scalar=1.0 / 64, in1=gs[:, 12:16], op0=ALU.mult, op1=ALU.subtract), r=[gs], w=[gs])
                    S.op("dve", lambda e: e.tensor_scalar(out=gs[:, 16:20], in0=gs[:, 16:20], scalar1=GN_EPS, scalar2=None, op0=ALU.add), r=[gs], w=[gs])
                    S.op("act", lambda e: e.sqrt(out=gs[:, 20:24], in_=gs[:, 16:20]), r=[gs], w=[gs])
                    S.op("dve", lambda e: e.reciprocal(out=gs[:, 24:28], in_=gs[:, 20:24]), r=[gs], w=[gs])
                    ynv = yn[:, :].rearrange("p (h i) -> p h i", h=4)
                    S.op("dve", lambda e: e.tensor_tensor(out=ynv, in0=PYv, in1=gs[:, 8:12].unsqueeze(2).to_broadcast([128, 4, 64]), op=ALU.subtract), r=[PY, gs], w=[yn])
                    S.op("dve", lambda e: e.tensor_tensor(out=ynv, in0=ynv, in1=gs[:, 24:28].unsqueeze(2).to_broadcast([128, 4, 64]), op=ALU.mult), r=[yn, gs], w=[yn])
                    S.op("dve", lambda e: e.tensor_tensor(out=yn[:, :], in0=yn[:, :], in1=lnw[:, :], op=ALU.mult), r=[yn, lnw], w=[yn])
                    S.op("dve", lambda e: e.tensor_tensor(out=yn[:, :], in0=yn[:, :], in1=lnb[:, :], op=ALU.add), r=[yn, lnb], w=[yn])
                    S.op("dve", lambda e: e.tensor_tensor(out=sq[:, :].rearrange("p (a b c) -> p a b c", a=2, b=2),
                                                          in0=TK[:, :, 3, :].rearrange("p a (b c) -> p a b c", b=2),
                                                          in1=rkt[:, 0:4].rearrange("p (a b) -> p a b", a=2).unsqueeze(3).to_broadcast([128, 2, 2, 64]), op=ALU.mult),
                         r=[TK, rkt], w=[sq])
                    S.op("dve", lambda e: e.tensor_tensor(out=yab[:, :], in0=yn[:, :], in1=sq[:, :], op=ALU.add), r=[yn, sq], w=[yab])
                    Pt2 = PS[bk[6]]
                    Pt2b = Pt2[:, :].bitcast(BF16)
                    for pi in range(2):
                        S.mm(lambda e, pi=pi: e.transpose(Pt2b[:, pi * 128:(pi + 1) * 128], yab[:, pi * 128:(pi + 1) * 128], C_("ident")), r=[yab, cb], w=[Pt2], last=(pi == 1))
                    for pi in range(2):
                        p = pairs[pi]
                        S.op("dve", lambda e, pi=pi, p=p: e.tensor_tensor(out=yT[:, p, csl], in0=Pt2b[:, pi * 128:(pi + 1) * 128], in1=sgA[pi][:, csl], op=ALU.mult),
                             r=[Pt2, sgA[pi]], w=[yT])
                    yield

            gens = [chunk_loop(0, [0, 1, 2, 3, 4, 5, 6, 7]), chunk_loop(1, [0, 1, 5, 6, 7, 2, 3, 4])]
            live = [gens[0]]
            started = 1
            nyield = 0
            while live:
                for g_ in list(live):
                    try:
                        next(g_)
                    except StopIteration:
                        live.remove(g_)
                nyield += 1
                if started < 2 and (nyield >= STAGGER or not live):
                    live.append(gens[1])
                    started = 2
            rel(st)
        ckpt("A")

        with contextlib.ExitStack() as st:
            qT = qTp
            sgB = [sb(st, "sgB%d" % c, [128, TB], BF16) for c in range(4)]
            wt = wt_next
            wt_next = load_w("in", l, G_GB)
            wt = wt_next
            wt_next = load_w("in", l, G_C)
            for c in range(4):
                P = proj_chunk(wt, c)
                S.op("act", lambda e, c=c, P=P: e.activation(out=sgB[c][:, :], in_=P[:, :], func=AF.Silu), r=[P], w=[sgB[c]])
            Et = sb(st, "Et", [128, 2048], BF16)
            S.dma("sp", Et[:], cb_d[:, NCBS:NCBS + 2048], w=[Et])
            pexp = sb(st, "pexp", [128, 1024], BF16)
            pT = sb(st, "pT", [128, 8, 128], BF16)
            t1 = sb(st, "t1", [128, 256])
            t2 = sb(st, "t2", [128, 256])
            for i in range(4):
                tsl = slice(i * 128, (i + 1) * 128)
                for g in range(2):
                    Pa, Pb_ = (PS[2], PS[3]) if g == 0 else (PS[6], PS[7])
                    for hh in range(4):
                        par = hh % 2
                        qc = 2 * g + hh // 2
                        for pcur in range(2):
                            Pd = Pa if hh < 2 else Pb_
                            o0 = ((hh % 2) * 2 + pcur) * 128
                            ks = slice((i + pcur) * 128, (i + pcur + 1) * 128)
                            S.mm(lambda e, Pd=Pd, o0=o0, ks=ks, g=g, par=par, qc=qc: e.matmul(Pd[:, o0:o0 + 128], lhsT=kxm[:, l, 2 * g + par, ks], rhs=qT[qc][:, tsl],
                                                                                              start=True, stop=True), r=[kxm, qT[qc]], w=[Pd], last=(pcur == 1 and hh % 2 == 1))
                    S.op("act", lambda e: e.activation(out=pexp[:, 0:512], in_=Pa[:, :], func=AF.Exp, scale=0.125), r=[Pa], w=[pexp])
                    S.op("act", lambda e: e.activation(out=pexp[:, 512:1024], in_=Pb_[:, :], func=AF.Exp, scale=0.125), r=[Pb_], w=[pexp])
                    S.op("dve", lambda e, g=g: e.tensor_tensor(out=pT[:].rearrange("p a q -> p (a q)"), in0=pexp[:, :], in1=Et[:, g * 1024:(g + 1) * 1024], op=ALU.mult), r=[pexp, Et], w=[pT])
                    PN, PD_ = PS[4], PS[5]
                    skip_prev = first and i == 0
                    for cq in range(2):
                        o_ = slice(cq * 128, (cq + 1) * 128)
                        terms = []
                        for par in range(2):
                            hh = 2 * cq + par
                            terms.append((i + 1, 2 * g + par, hh * 2 + 1, "onesA" if par == 0 else "onesB"))
                            if not skip_prev:
                                terms.append((i, 2 * g + par, hh * 2 + 0, "onesA" if par == 0 else "onesB"))
                        for ti, (vt, vv_, pidx, on) in enumerate(terms):
                            S.mm(lambda e, vt=vt, vv_=vv_, pidx=pidx, o_=o_, ti=ti: e.matmul(PN[:, o_], lhsT=vpad[:, l, vt, vv_, :], rhs=pT[:, pidx, :], start=(ti == 0), stop=(ti == len(terms) - 1)),
                                 r=[vpad, pT], w=[PN], last=(ti == len(terms) - 1))
                        for ti, (vt, vv_, pidx, on) in enumerate(terms):
                            S.mm(lambda e, on=on, pidx=pidx, o_=o_, ti=ti: e.matmul(PD_[:, o_], lhsT=C_(on), rhs=pT[:, pidx, :], start=(ti == 0), stop=(ti == len(terms) - 1)),
                                 r=[cb, pT], w=[PD_], last=(ti == len(terms) - 1))
                    es = esink[:, l * 4 + 2 * g: l * 4 + 2 * g + 2]
                    S.op("dve", lambda e, es=es: e.tensor_tensor(out=t1[:, :].rearrange("p (a q) -> p a q", a=2), in0=PD_[:, 0:256].rearrange("p (a q) -> p a q", a=2),
                                                                 in1=es.unsqueeze(2).to_broadcast([128, 2, 128]), op=ALU.add), r=[PD_, esink], w=[t1])
                    S.op("dve", lambda e: e.reciprocal(out=t1[:, :], in_=t1[:, :]), r=[t1], w=[t1])
                    S.op("dve", lambda e: e.tensor_tensor(out=t2[:, :], in0=PN[:, 0:256], in1=t1[:, :], op=ALU.mult), r=[PN, t1], w=[t2])
                    for cq in range(2):
                        c = 2 * g + cq
                        S.op("pool", lambda e, c=c, cq=cq: e.tensor_tensor(out=yT[:, 4 + c, tsl], in0=t2[:, cq * 128:(cq + 1) * 128], in1=sgB[c][:, tsl], op=ALU.mult),
                             r=[t2, sgB[c]], w=[yT])
            S.op("pool", lambda e: e.tensor_copy(out=kxm[:, l, :, 0:128], in_=kxm[:, l, :, 512:640]), r=[kxm], w=[kxm])
            S.op("dve", lambda e: e.tensor_copy(out=vpad[:, l, 0, :, :], in_=vpad[:, l, 4, :, :]), r=[vpad], w=[vpad])
            rel(st)

        ckpt("B")
        with contextlib.ExitStack() as st:
            zc = sb(st, "zc", [128, 4, 528])
            ta = sb(st, "ta", [128, 528])
            tb_ = sb(st, "tb", [128, 528])
            sgC = [sb(st, "sgC%d" % c, [128, TB], BF16) for c in range(4)]
            pooled = sb(st, "pooled", [128, TB], BF16)
            poolwl = sb(st, "poolwl", [128, 4, 128], BF16)
            S.dma("sp", poolwl[:].rearrange("p g n -> p (g n)"), csc["poolw"][:, l * 512:(l + 1) * 512], r=[csc_b["poolw"]], w=[poolwl])
            if first:
                icn = sb(st, "icn", [128, 4, 512])
                S.dma("sp", icn[:].rearrange("p g t -> p (g t)"), ic_d[:, :], w=[icn])
            wt = wt_next
            wt_next = load_w("in", l, G_GC)
            S.op("pool", lambda e: e.tensor_copy(out=zc[:, :, 0:16], in_=poolh[:, l, :, :]), r=[poolh], w=[zc])
            for g in range(4):
                P = proj_chunk(wt, g)
                S.op("act", lambda e, g=g, P=P: e.copy(out=zc[:, g, 16:528], in_=P[:, :]), r=[P], w=[zc])
            S.op("pool", lambda e: e.tensor_copy(out=poolh[:, l, :, :], in_=zc[:, :, 512:528]), r=[zc], w=[poolh])
            wt = wt_next
            wt_next = load_w("in", l, G_U)
            for c in range(4):
                P = proj_chunk(wt, c)
                S.op("act", lambda e, c=c, P=P: e.activation(out=sgC[c][:, :], in_=P[:, :], func=AF.Silu), r=[P], w=[sgC[c]])
            for g, wdw in enumerate((2, 4, 8, 16)):
                src_t, src = zc, (lambda a, b, g=g: zc[:, g, a:b])
                sh_ = 1
                bufs = [ta, tb_]
                bi_ = 0
                while sh_ < wdw:
                    dst = bufs[bi_ % 2]
                    bi_ += 1
                    lo_ = 2 * sh_ - 1
                    S.op("pool", lambda e, dst=dst, src=src, sh_=sh_, lo_=lo_: e.tensor_tensor(out=dst[:, lo_:528], in0=src(lo_, 528), in1=src(lo_ - sh_, 528 - sh_), op=ALU.add),
                         r=[src_t], w=[dst])
                    src_t, src = dst, (lambda a, b, dst=dst: dst[:, a:b])
                    sh_ *= 2
                if first:
                    S.op("dve", lambda e, g=g, src=src: e.tensor_tensor(out=src(16, 528), in0=src(16, 528), in1=icn[:, g, :], op=ALU.mult), r=[src_t, icn], w=[src_t])
                    S.op("dve", lambda e, g=g, src=src: e.tensor_tensor(out=pooled[:, :], in0=src(16, 528), in1=zc[:, g, 16:528], op=ALU.subtract), r=[src_t, zc], w=[pooled])
                else:
                    S.op("dve", lambda e, g=g, src=src, wdw=wdw: e.scalar_tensor_tensor(out=pooled[:, :], in0=src(16, 528), scalar=1.0 / wdw, in1=zc[:, g, 16:528],
                                                                                       op0=ALU.mult, op1=ALU.subtract), r=[src_t, zc], w=[pooled])
                PA = PS[2 + g % 2]
                S.mm(lambda e, g=g, PA=PA: e.matmul(PA[:, :], lhsT=poolwl[:, g, :], rhs=pooled[:, :], start=True, stop=True), r=[poolwl, pooled], w=[PA])
                S.op("dve", lambda e, g=g, PA=PA: e.scalar_tensor_tensor(out=yT[:, 8 + g, :], in0=PA[:, :], scalar=pc(l, PSC + g), in1=sgC[g][:, :], op0=ALU.mult, op1=ALU.mult),
                     r=[PA, pcol, sgC[g]], w=[yT])
            rel(st)

        ckpt("C")
        with contextlib.ExitStack() as st:
            uT = [sb(st, "uT%d" % c, [128, TB], BF16) for c in range(4)]
            vT = [sb(st, "vT%d" % c, [128, TB]) for c in range(4)]
            vb = [sb(st, "vb%d" % c, [128, TB], BF16) for c in range(4)]
            vq = [sb(st, "vq%d" % c, [128, TB], BF16) for c in range(4)]
            sgD = [sb(st, "sgD%d" % c, [128, TB], BF16) for c in range(4)]
            mean = sb(st, "mean", [128, TB])
            rstd = sb(st, "rstd", [128, TB])
            vtok = sb(st, "vtok", [128, 4, 4, 128], BF16)
            brep = sb(st, "brep", [128, 4, 128])
            sT = sb(st, "sT", [128, TB])
            wsTl = sb(st, "wsTl", [128, 4, 128], BF16)
            S.dma("sp", wsTl[:].rearrange("p g n -> p (g n)"), csc["wsT"][:, l * 512:(l + 1) * 512], r=[csc_b["wsT"]], w=[wsTl])
            S.dma("sp", brep[:].rearrange("p g i -> p (g i)"), sgub_d[l:l + 1, :].partition_broadcast(128), w=[brep])
            wt = wt_next
            wt_next = load_w("in", l, G_VD)
            for c in range(4):
                P = proj_chunk(wt, c)
                S.op("act", lambda e, c=c, P=P: e.activation(out=uT[c][:, :], in_=P[:, :], func=AF.Gelu), r=[P], w=[uT[c]])
            wt = wt_next
            wt_next = load_w("in", l, G_GD)
            for c in range(4):
                P = proj_chunk(wt, c)
                S.op("act", lambda e, c=c, P=P: e.activation(out=vT[c][:, :], in_=P[:, :], func=AF.Gelu), r=[P], w=[vT[c]])
                S.op("dve", lambda e, c=c: e.tensor_copy(out=vb[c][:, :], in_=vT[c][:, :]), r=[vT[c]], w=[vb[c]])
                S.op("pool", lambda e, c=c: e.tensor_tensor(out=vq[c][:, :], in0=vT[c][:, :], in1=vT[c][:, :], op=ALU.mult), r=[vT[c]], w=[vq[c]])
            wt = wt_next
            wt_next = load_w("out", l, 0)
            for c in range(4):
                P = proj_chunk(wt, c)
                S.op("act", lambda e, c=c, P=P: e.activation(out=sgD[c][:, :], in_=P[:, :], func=AF.Silu), r=[P], w=[sgD[c]])
            PM, PQ = PS[2], PS[3]
            for c in range(4):
                S.mm(lambda e, c=c: e.matmul(PM[:, :], lhsT=C_("o512"), rhs=vb[c][:, :], start=(c == 0), stop=(c == 3)), r=[cb, vb[c]], w=[PM], last=(c == 3))
            for c in range(4):
                S.mm(lambda e, c=c: e.matmul(PQ[:, :], lhsT=C_("o512"), rhs=vq[c][:, :], start=(c == 0), stop=(c == 3)), r=[cb, vq[c]], w=[PQ], last=(c == 3))
            S.op("act", lambda e: e.copy(out=mean[:, :], in_=PM[:, :]), r=[PM], w=[mean])
            S.op("pool", lambda e: e.tensor_tensor(out=rstd[:, :], in0=mean[:, :], in1=mean[:, :], op=ALU.mult), r=[mean], w=[rstd])
            S.op("dve", lambda e: e.tensor_tensor(out=rstd[:, :], in0=PQ[:, :], in1=rstd[:, :], op=ALU.subtract), r=[PQ, rstd], w=[rstd])
            S.op("dve", lambda e: e.tensor_scalar(out=rstd[:, :], in0=rstd[:, :], scalar1=LN_EPS, scalar2=None, op0=ALU.add), r=[rstd], w=[rstd])
            S.op("act", lambda e: e.sqrt(out=rstd[:, :], in_=rstd[:, :]), r=[rstd], w=[rstd])
            S.op("dve", lambda e: e.reciprocal(out=rstd[:, :], in_=rstd[:, :]), r=[rstd], w=[rstd])
            for c in range(4):
                S.op("pool", lambda e, c=c: e.tensor_tensor(out=vT[c][:, :], in0=vT[c][:, :], in1=mean[:, :], op=ALU.subtract), r=[vT[c], mean], w=[vT[c]])
                S.op("dve", lambda e, c=c: e.tensor_tensor(out=vb[c][:, :], in0=vT[c][:, :], in1=rstd[:, :], op=ALU.mult), r=[vT[c], rstd], w=[vb[c]])
            for i in range(4):
                Pt = PS[4 + i % 2]
                Ptb = Pt[:, :].bitcast(BF16)
                for c in range(4):
                    S.mm(lambda e, i=i, c=c, Ptb=Ptb: e.transpose(Ptb[:, c * 128:(c + 1) * 128], vb[c][:, i * 128:(i + 1) * 128], C_("ident")), r=[vb[c], cb], w=[Pt], last=(c == 3))
                S.op("act", lambda e, i=i, Ptb=Ptb: e.copy(out=vtok[:, i, :, :].rearrange("p c d -> p (c d)"), in_=Ptb[:, 0:512]), r=[Pt], w=[vtok])
            for g in range(4):
                Pg = PS[2 + g % 2]
                for i in range(4):
                    S.mm(lambda e, g=g, i=i, Pg=Pg: e.matmul(Pg[:, i * 128:(i + 1) * 128], lhsT=vtok[:, i, g, :], rhs=wsTl[:, g, :], start=True, stop=True),
                         r=[vtok, wsTl], w=[Pg], last=(i == 3))
                S.op("dve", lambda e, g=g, Pg=Pg: e.scalar_tensor_tensor(out=sT[:, :].rearrange("p (i t) -> p i t", i=4), in0=Pg[:, :].rearrange("p (i t) -> p i t", i=4),
                                                                        scalar=pc(l, SNW + g), in1=brep[:, g:g + 1, :].to_broadcast([128, 4, 128]), op0=ALU.mult, op1=ALU.add),
                     r=[Pg, pcol, brep], w=[sT])
                S.op("pool", lambda e, g=g: e.tensor_tensor(out=sT[:, :], in0=sT[:, :], in1=uT[g][:, :], op=ALU.mult), r=[sT, uT[g]], w=[sT])
                S.op("dve", lambda e, g=g: e.tensor_tensor(out=yT[:, 12 + g, :], in0=sT[:, :], in1=sgD[g][:, :], op=ALU.mult), r=[sT, sgD[g]], w=[yT])
            rel(st)

        ckpt("D")
        with contextlib.ExitStack() as st:
            o = sb(st, "o", [128, 4, D])
            postw = sb(st, "postw", [128, D])
            junk = sb(st, "junk", [128, 512], BF16)
            ssq = sb(st, "ssq", [128, 16])
            sm = sb(st, "smo", [128, 32])
            S.dma("sp", postw[:], postw_d[l:l + 1, :].partition_broadcast(128), w=[postw])
            for mb in range(4):
                wt = wt_next
                if mb < 3:
                    wt_next = load_w("out", l, mb + 1)
                elif not (bi == n_blocks - 1 and l == n_layers - 1):
                    x_pref[0] = load_w("in", (l + 1) if l + 1 < n_layers else 0, G_X)
                for i in range(4):
                    P = PS[pj_rr[0] % 2]
                    pj_rr[0] += 1
                    for k in range(16):
                        S.mm(lambda e, k=k, P=P, i=i, wt=wt: e.matmul(P[:, :], lhsT=yT[:, k, i * 128:(i + 1) * 128], rhs=wt[:, k, :], start=(k == 0), stop=(k == 15)),
                             r=[yT, wt], w=[P], last=(k == 15))
                    S.op("act", lambda e, i=i, mb=mb, P=P: e.copy(out=o[:, i, mb * 512:(mb + 1) * 512], in_=P[:, :]), r=[P], w=[o])
                    S.op("act", lambda e, i=i, mb=mb: e.activation(out=junk[:, :], in_=o[:, i, mb * 512:(mb + 1) * 512], func=AF.Square, accum_out=ssq[:, i * 4 + mb:i * 4 + mb + 1]),
                         r=[o], w=[junk, ssq])
            for i in range(4):
                S.op("dve", lambda e, i=i: e.tensor_reduce(out=sm[:, i * 8:i * 8 + 1], in_=ssq[:, i * 4:(i + 1) * 4], axis=AX.X, op=ALU.add), r=[ssq], w=[sm])
                S.op("dve", lambda e, i=i: e.tensor_scalar(out=sm[:, i * 8 + 1:i * 8 + 2], in0=sm[:, i * 8:i * 8 + 1], scalar1=1.0 / D, scalar2=NORM_EPS, op0=ALU.mult, op1=ALU.add), r=[sm], w=[sm])
                S.op("act", lambda e, i=i: e.sqrt(out=sm[:, i * 8 + 2:i * 8 + 3], in_=sm[:, i * 8 + 1:i * 8 + 2]), r=[sm], w=[sm])
                S.op("dve", lambda e, i=i: e.reciprocal(out=sm[:, i * 8 + 3:i * 8 + 4], in_=sm[:, i * 8 + 2:i * 8 + 3]), r=[sm], w=[sm])
                S.op("dve", lambda e, i=i: e.scalar_tensor_tensor(out=o[:, i, :], in0=o[:, i, :], scalar=sm[:, i * 8 + 3:i * 8 + 4], in1=postw[:, :], op0=ALU.mult, op1=ALU.mult), r=[o, sm, postw], w=[o])
                S.op("dve", lambda e, i=i: e.tensor_tensor(out=xb[i][:, :], in0=xb[i][:, :], in1=o[:, i, :], op=ALU.add), r=[xb[i], o], w=[xb[i]])
                if last_layer:
                    S.dma("sp", out_d[bi * TB + i * 128: bi * TB + (i + 1) * 128, :], xb[i][:, :], r=[xb[i]])
            rel(st)

    try:
        for bi in range(n_blocks):
            for l in range(n_layers):
                layer_block(bi, l, l == n_layers - 1)
    except _Stop:
        S.barrier()
        S.emit()
        return nc
    S.barrier()
    S.emit()
    stack.close()
    return nc


_CACHE = {}


def host_layout(inputs):
    f = lambda a: np.ascontiguousarray(np.asarray(a, dtype=np.float32))
    L = 2
    shared = {}
    shared["w_in_p"] = f(np.asarray(inputs["w_in"])[:, :, perm_index()])
    shared["w_out"] = f(inputs["w_out"])
    pcol = np.zeros((128, L * NPC), np.float32)
    for l in range(L):
        cols = []
        cols.append(np.asarray(inputs["pre_norm_w"])[l].reshape(16, 128).T)
        cols.append(np.asarray(inputs["shift_mu"])[l].reshape(13, 128).T)
        for nm in ("rwkv_w0", "rwkv_a0", "rwkv_k_k", "rwkv_k_a"):
            cols.append(np.asarray(inputs[nm])[l].reshape(4, 128).T)
        cols.append(np.asarray(inputs["rwkv_r_k"])[l].reshape(4, 128).T)
        cols.append(np.asarray(inputs["pool_scale"])[l].reshape(4, 128).T)
        cols.append(np.asarray(inputs["sgu_norm_w"])[l].reshape(4, 128).T)
        cols.append(np.repeat(np.asarray(inputs["attn_sinks"])[l], 64).reshape(4, 128).T)
        pc_ = np.concatenate(cols, 1)
        assert pc_.shape[1] == NPC
        pcol[:, l * NPC:(l + 1) * NPC] = pc_
    shared["pcol"] = pcol
    shared["post_norm_w"] = f(inputs["post_norm_w"])
    shared["pre_norm_w"] = f(inputs["pre_norm_w"])
    shared["rwkv_ln_w"] = f(inputs["rwkv_ln_w"])
    shared["rwkv_ln_b"] = f(inputs["rwkv_ln_b"])
    shared["sgu_b"] = f(np.asarray(inputs["sgu_b"]).reshape(L, 512))
    shared["rwkv_w_up"] = f(inputs["rwkv_w_up"])
    shared["rwkv_a_up"] = f(inputs["rwkv_a_up"])
    shared["pool_w"] = f(inputs["pool_w"])
    shared["sgu_wT"] = f(np.asarray(inputs["sgu_w"]).transpose(0, 1, 3, 2))
    cbf, cf32, ic = make_consts()
    shared["cbf"] = cbf
    shared["cf32"] = cf32
    shared["invcnt"] = ic
    return shared


def kernel(**inputs):
    x = np.asarray(inputs["x"], dtype=np.float32)
    shared = host_layout(inputs)
    if "nc" not in _CACHE:
        _CACHE["nc"] = build()
    nc = _CACHE["nc"]
    in_maps = []
    for c in range(8):
        m = dict(shared)
        m["x"] = np.ascontiguousarray(x[c % 4])
        in_maps.append(m)
    res = run_bass_kernel_spmd(nc, in_maps, core_ids=list(range(8)))
    out = np.stack([res.results[c]["out"] for c in range(4)], 0)
    return out.astype(np.float32)
```

```python
import contextlib
import types
import numpy as np
import ml_dtypes
import concourse.bass as bass
import concourse.mybir as mybir
from concourse.bass_utils import run_bass_kernel_spmd

F32 = mybir.dt.float32
BF16 = mybir.dt.bfloat16
AF = mybir.ActivationFunctionType
ALU = mybir.AluOpType
AX = mybir.AxisListType

D = 2048
SEQ = 2048
TB = 512
NCOL = 6144
C0 = -0.6065306597126334
GN_EPS = 64e-5
STAGGER = 0
NFILL = 0
LN_EPS = 1e-5
NORM_EPS = 1e-6

G_X, G_A0, G_Q, G_GB, G_C, G_GC, G_U, G_VD, G_GD = 0, 1, 5, 6, 7, 8, 9, 10, 11


def perm_index():
    idx = []
    rng = lambda a, n: list(range(a, a + n))
    idx += rng(1536, 128)
    idx += rng(2176, 64) + rng(2176, 64)
    idx += rng(2240, 64) + rng(2240, 64)
    idx += rng(2304, 128)
    for p in range(4):
        idx += rng(128 * p, 128) + rng(512 + 128 * p, 128) + rng(1024 + 128 * p, 128) + rng(3968 + 128 * p, 128)
    idx += rng(1664, 512)
    idx += rng(3968 + 512, 512)
    idx += rng(2432, 512)
    idx += rng(3968 + 1024, 512)
    idx += rng(2944, 512)
    idx += rng(3456, 512)
    idx += rng(3968 + 1536, 512)
    assert len(idx) == NCOL
    return np.array(idx)


CB = {}
_o = 0
for _n, _w in [("ident", 128), ("sl4", 512), ("suui2", 512), ("bd2", 256), ("bones", 128),
               ("onesA", 128), ("onesB", 128), ("o512", 128), ("hsel", 2), ("E0", 1024), ("E1", 1024), ("ui4", 512)]:
    CB[_n] = (_o, _o + _w)
    _o += _w
NCB = _o
NCBS = CB["E0"][0]
NPC = 61


def make_consts():
    cb = np.zeros((128, NCB), np.float32)
    p = np.arange(128)[:, None]
    f = np.arange(128)[None, :]
    cb[:, slice(*CB["ident"])] = (p == f)
    cb[:, slice(*CB["sl4"])] = np.tile((f < p), (1, 4))
    cb[:, slice(*CB["suui2"])] = np.tile(np.concatenate([(p < f), (p <= f)], 1), (1, 2))
    cb[:, slice(*CB["ui4"])] = np.tile((p <= f), (1, 4))
    cb[:, slice(*CB["bd2"])] = np.tile(((p // 64) == (f // 64)), (1, 2))
    cb[:, slice(*CB["bones"])] = ((p // 64) == (f // 64))
    slopes = 2.0 ** (-(np.arange(8) + 1.0))
    for g in range(2):
        E = np.zeros((128, 8, 128), np.float64)
        for hh in range(4):
            sl = slopes[4 * g + hh]
            dist_cur = (f - p).astype(np.float64)
            E[:, hh * 2 + 1, :] = np.where(dist_cur >= 0, np.exp(-sl * np.maximum(dist_cur, 0)), 0.0)
            dist_prev = dist_cur + 128
            E[:, hh * 2 + 0, :] = np.where(dist_prev < 128, np.exp(-sl * dist_prev), 0.0)
        cb[:, slice(*CB["E%d" % g])] = E.reshape(128, 1024)
    cb[:, CB["onesA"][0]:CB["onesA"][0] + 64] = 1.0
    cb[:, CB["onesB"][0] + 64:CB["onesB"][1]] = 1.0
    cb[:, slice(*CB["o512"])] = 1.0 / 512.0
    cb[0:64, CB["hsel"][0]] = 1.0
    cb[64:128, CB["hsel"][0] + 1] = 1.0
    cf = np.zeros((128, 512 + 2), np.float32)
    rm = np.ones(512, np.float32)
    rm[::128] = 0.0
    cf[:, 0:512] = rm[None]
    cf[0:64, 512] = 1.0
    cf[64:128, 513] = 1.0
    pos = np.arange(512)
    ic = np.zeros((128, 4, 512), np.float32)
    for g, w in enumerate((2, 4, 8, 16)):
        ic[:, g, :] = (1.0 / np.minimum(pos + 1, w))[None]
    return cb.astype(ml_dtypes.bfloat16), cf, ic.reshape(128, 2048)


def freeze(fn):
    if fn.__closure__ is None:
        return fn
    cells = []
    for c in fn.__closure__:
        try:
            cells.append(types.CellType(c.cell_contents))
        except ValueError:
            cells.append(c)
    g = types.FunctionType(fn.__code__, fn.__globals__, fn.__name__, fn.__defaults__, tuple(cells))
    g.__kwdefaults__ = fn.__kwdefaults__
    return g


class Buf:
    __slots__ = ("name", "w", "r")

    def __init__(self, name):
        self.name = name
        self.w = None
        self.r = []


class Tl:
    def __init__(self, t, name, psum=False):
        self.t = t
        self.b = Buf(name)
        self.psum = psum

    def __getitem__(self, k):
        return self.t[k]


class Sched:
    ENG = ["pe", "act", "dve", "pool", "sp"]

    def __init__(self, nc, stack, n_dma_sems=16):
        self.nc = nc
        self.stack = stack
        self.sem = {}
        self.cnt = {}
        for k in ["pe", "act", "dve", "pool"]:
            self.sem[k] = stack.enter_context(nc.semaphore("s_" + k))
            self.cnt[k] = 0
        self.dma_sems = []
        for i in range(n_dma_sems):
            key = "dma%d" % i
            self.sem[key] = stack.enter_context(nc.semaphore("s_" + key))
            self.cnt[key] = 0
            self.dma_sems.append(key)
        self.dma_rr = 0
        self.seen = {}
        self.prog = {e: [] for e in self.ENG}
        self.uid = 0
        self.pending = {}
        self.scopes = {}
        self.sw_sems = []

    def _wait(self, e, tok):
        if tok is None:
            return
        k, v = tok
        if e == "pe" and k == "pe":
            return
        if self.seen.get((e, k), 0) >= v:
            return
        self.prog[e].append(("wait", k, v))
        self.seen[(e, k)] = v

    def deps(self, e, reads, writes):
        for t in reads:
            self._wait(e, t.b.w)
            if t.psum:
                for tok in t.b.r:
                    if tok[0] != e:
                        self._wait(e, tok)
        for t in writes:
            self._wait(e, t.b.w)
            for tok in t.b.r:
                self._wait(e, tok)

    def commit(self, tok, reads, writes):
        for t in reads:
            if t.psum:
                t.b.r = [tok]
            else:
                t.b.r.append(tok)
        for t in writes:
            t.b.w = tok
            t.b.r = []

    def op(self, e, fn, r=(), w=()):
        self.deps(e, r, w)
        fn = freeze(fn)
        self.cnt[e] += 1
        self.prog[e].append(("ins", fn, e, 1))
        tok = (e, self.cnt[e])
        self.commit(tok, r, w)
        return tok

    def mm(self, fn, r=(), w=(), last=True):
        e = "pe"
        self.deps(e, r, w)
        fn = freeze(fn)
        if last:
            self.cnt[e] += 1
            self.prog[e].append(("ins", fn, e, 1))
            tok = (e, self.cnt[e])
        else:
            self.prog[e].append(("ins", fn, None, 0))
            tok = (e, self.cnt[e] + 1)
        self.commit(tok, r, w)
        return tok

    def dma(self, q, out, in_, r=(), w=(), fresh=False, **kw):
        if fresh:
            key = "swd%d" % len(self.sem)
            self.sem[key] = self.stack.enter_context(self.nc.semaphore("s_" + key))
            self.cnt[key] = 0
            self.sw_sems.append(key)
        else:
            key = self.dma_sems[self.dma_rr % len(self.dma_sems)]
            self.dma_rr += 1
        self._wait(q, (key, self.cnt[key]) if self.cnt[key] else None)
        self.deps(q, r, w)
        fn = lambda eng, out=out, in_=in_, kw=kw: eng.dma_start(out=out, in_=in_, **kw)
        self.cnt[key] += 16
        self.prog[q].append(("ins", fn, key, 16))
        tok = (key, self.cnt[key])
        self.commit(tok, r, w)
        return tok

    def release(self, tiles):
        for t in tiles:
            for tok in ([t.b.w] if t.b.w else []) + list(t.b.r):
                k, v = tok
                if self.pending.get(k, 0) < v:
                    self.pending[k] = v

    def barrier(self):
        for e in self.ENG:
            for k in ["pe", "act", "dve", "pool"] + self.dma_sems + self.sw_sems:
                if self.cnt[k]:
                    self._wait(e, (k, self.cnt[k]))

    def emit(self):
        nc = self.nc

        def replay(name):
            def run(eng):
                for it in self.prog[name]:
                    if it[0] == "wait":
                        eng.wait_ge(self.sem[it[1]], it[2])
                    else:
                        ins = it[1](eng)
                        if it[2] is not None:
                            ins.then_inc(self.sem[it[2]], it[3])
            return run

        with nc.Block() as block:
            block.tensor(replay("pe"))
            block.scalar(replay("act"))
            block.vector(replay("dve"))
            block.gpsimd(replay("pool"))
            block.sync(replay("sp"))


class _Stop(Exception):
    pass


def build(n_blocks=4, n_layers=2, dbg=None, stop_after=None):
    nc = bass.Bass("TRN2", target_bir_lowering=False)
    stack = contextlib.ExitStack()
    S = Sched(nc, stack)
    L = 2

    def din(name, shape, dt=F32):
        return nc.dram_tensor(name, list(shape), dt, kind="ExternalInput").ap()

    x_d = din("x", [SEQ, D])
    win_d = din("w_in_p", [L, D, NCOL])
    wout_d = din("w_out", [L, D, D])
    pcol_d = din("pcol", [128, L * NPC])
    postw_d = din("post_norm_w", [L, D])
    prew_d = din("pre_norm_w", [L, D])
    lnw_d = din("rwkv_ln_w", [L, 512])
    lnb_d = din("rwkv_ln_b", [L, 512])
    sgub_d = din("sgu_b", [L, 512])
    wup_d = din("rwkv_w_up", [L, 64, 512])
    aup_d = din("rwkv_a_up", [L, 64, 512])
    poolw_d = din("pool_w", [L, 4, 128, 128])
    sguwT_d = din("sgu_wT", [L, 4, 128, 128])
    cb_d = din("cbf", [128, NCB], BF16)
    cf_d = din("cf32", [128, 514])
    ic_d = din("invcnt", [128, 2048])
    out_d = nc.dram_tensor("out", [SEQ, D], F32, kind="ExternalOutput").ap()
    dbg_d = {}
    if dbg:
        for nm, shp in dbg.items():
            dbg_d[nm] = nc.dram_tensor("dbg_" + nm, list(shp), F32, kind="ExternalOutput").ap()
    wsc_in = nc.dram_tensor("wsc_in", [L, 12, 128, 16 * 512], BF16).ap()
    wsc_out = nc.dram_tensor("wsc_out", [L, 4, 128, 16 * 512], BF16).ap()
    wsc_in_b = [[Tl(None, "wsi%d_%d" % (l, g)) for g in range(12)] for l in range(L)]
    wsc_out_b = [[Tl(None, "wso%d_%d" % (l, g)) for g in range(4)] for l in range(L)]

    def sb(stk, name, shape, dt=F32):
        S.uid += 1
        nm = "%s_%d" % (name, S.uid)
        t = Tl(stk.enter_context(nc.sbuf_tensor(nm, list(shape), dt)), nm)
        t.b.r = list(S.pending.items())
        S.scopes.setdefault(id(stk), []).append(t)
        return t

    def rel(stk):
        S.release(S.scopes.pop(id(stk), []))

    def ps(stk, name, shape, dt=F32):
        S.uid += 1
        nm = "%s_%d" % (name, S.uid)
        return Tl(stk.enter_context(nc.psum_tensor(nm, list(shape), dt)), nm, psum=True)

    def ckpt(name):
        if stop_after == name:
            raise _Stop()

    top = stack
    xb = [sb(top, "xblk%d" % i, [128, D]) for i in range(4)]
    hT = sb(top, "hT", [128, 16, TB], BF16)
    yT = sb(top, "yT", [128, 16, TB], BF16)
    wbuf = [sb(top, "wbuf%d" % i, [128, 16, 512], BF16) for i in range(2)]
    cb = sb(top, "cb", [128, NCBS], BF16)
    cf = sb(top, "cf", [128, 514])
    pcol = sb(top, "pcol", [128, L * NPC])
    omka = sb(top, "omka", [128, L * 4])
    esink = sb(top, "esink", [128, L * 4])
    csc = {nm: nc.dram_tensor("csc_" + nm, [128, L * 512], BF16).ap() for nm in ("wup", "aup", "poolw", "wsT")}
    csc_b = {nm: Tl(None, "csc_" + nm) for nm in csc}
    Hc = sb(top, "Hc", [128, L, 4, 128])
    carryA = sb(top, "carryA", [128, L, 13])
    kxm = sb(top, "kxm", [128, L, 4, 640], BF16)
    vpad = sb(top, "vpad", [128, L, 5, 4, 128], BF16)
    poolh = sb(top, "poolh", [128, L, 4, 16])
    qTp = [sb(top, "qTp%d" % c, [128, TB], BF16) for c in range(4)]
    PS = [ps(top, "ps%d" % i, [128, 512]) for i in range(8)]

    def C_(name, a=None, b=None):
        lo, hi = CB[name]
        if a is None:
            return cb[:, lo:hi]
        return cb[:, lo + a:lo + b]

    def pc(l, off, n=1):
        return pcol[:, l * NPC + off: l * NPC + off + n]

    PRE, MU, W0, A0, KK, KA, RK, PSC, SNW, SNK = 0, 16, 29, 33, 37, 41, 45, 49, 53, 57

    eng_rr = [0]

    def ew(fn, r, w, engines=("dve", "pool")):
        e = engines[eng_rr[0] % len(engines)]
        eng_rr[0] += 1
        return S.op(e, fn, r, w)

    S.dma("sp", cb[:], cb_d[:, 0:NCBS], w=[cb])
    S.dma("sp", cf[:], cf_d[:, :], w=[cf])
    S.dma("sp", pcol[:], pcol_d[:, :], w=[pcol])
    for t in (Hc, carryA, kxm, vpad, poolh):
        S.op("pool", lambda e, t=t: e.memset(t[:], 0.0), w=[t])
    with contextlib.ExitStack() as st:
        wup = sb(st, "wup", [128, L, 512], BF16)
        aup = sb(st, "aup", [128, L, 512], BF16)
        poolw = sb(st, "poolw", [128, L, 4, 128], BF16)
        wsT = sb(st, "wsT", [128, L, 4, 128], BF16)
        stg = sb(st, "stg", [128, L, 512])
        stg2 = sb(st, "stg2", [128, L, 512])
        S.op("pool", lambda e: e.memset(stg[:], 0.0), w=[stg])
        S.op("pool", lambda e: e.memset(stg2[:], 0.0), w=[stg2])
        for l in range(L):
            S.dma("sp", stg[0:64, l, :], wup_d[l, :, :], w=[stg])
            S.dma("sp", stg2[64:128, l, :], aup_d[l, :, :], w=[stg2])
        S.op("dve", lambda e: e.tensor_copy(out=wup[:], in_=stg[:]), r=[stg], w=[wup])
        S.op("dve", lambda e: e.tensor_copy(out=aup[:], in_=stg2[:]), r=[stg2], w=[aup])
        stg3 = sb(st, "stg3", [128, L, 4, 128])
        stg4 = sb(st, "stg4", [128, L, 4, 128])
        for l in range(L):
            S.dma("sp", stg3[:, l, :, :], poolw_d[l].rearrange("g c d -> c g d"), w=[stg3])
            S.dma("sp", stg4[:, l, :, :], sguwT_d[l].rearrange("g j i -> j g i"), w=[stg4])
        S.op("dve", lambda e: e.tensor_copy(out=poolw[:], in_=stg3[:]), r=[stg3], w=[poolw])
        ui4t = sb(st, "ui4t", [128, 512], BF16)
        S.dma("sp", ui4t[:], cb_d[:, CB["ui4"][0]:CB["ui4"][1]], w=[ui4t])
        for l in range(L):
            S.op("dve", lambda e, l=l: e.tensor_tensor(out=wsT[:, l, :, :].rearrange("p g i -> p (g i)"),
                                                       in0=stg4[:, l, :, :].rearrange("p g i -> p (g i)"),
                                                       in1=ui4t[:, :], op=ALU.mult), r=[stg4, ui4t], w=[wsT])
        S.dma("sp", csc["wup"], wup[:].rearrange("p l n -> p (l n)"), r=[wup], w=[csc_b["wup"]])
        S.dma("sp", csc["aup"], aup[:].rearrange("p l n -> p (l n)"), r=[aup], w=[csc_b["aup"]])
        S.dma("sp", csc["poolw"], poolw[:].rearrange("p l g n -> p (l g n)"), r=[poolw], w=[csc_b["poolw"]])
        S.dma("sp", csc["wsT"], wsT[:].rearrange("p l g n -> p (l g n)"), r=[wsT], w=[csc_b["wsT"]])
        for l in range(L):
            S.op("dve", lambda e, l=l: e.tensor_scalar(out=omka[:, l * 4:(l + 1) * 4], in0=pc(l, KA, 4), scalar1=-1.0,
                                                       scalar2=1.0, op0=ALU.mult, op1=ALU.add), r=[pcol], w=[omka])
            S.op("act", lambda e, l=l: e.activation(out=esink[:, l * 4:(l + 1) * 4], in_=pc(l, SNK, 4), func=AF.Exp),
                 r=[pcol], w=[esink])
        rel(st)

    try:
        ckpt("setup")
    except _Stop:
        S.barrier(); S.emit(); stack.close(); return nc
    def prepass(layers):
        HW = NCOL // 2
        with contextlib.ExitStack() as st:
            f32s = [sb(st, "pf%d" % i, [128, HW]) for i in range(2)]
            b16s = [sb(st, "pb%d" % i, [128, HW], BF16) for i in range(2)]
            it = 0
            for l in layers:
                for k in range(16):
                    for hf in range(2):
                        f, b_ = f32s[it % 2], b16s[it % 2]
                        it += 1
                        S.dma("sp", f[:], win_d[l, k * 128:(k + 1) * 128, hf * HW:(hf + 1) * HW], w=[f])
                        S.op("dve", lambda e, f=f, b_=b_: e.tensor_copy(out=b_[:, 0:1024], in_=f[:, 0:1024]), r=[f], w=[b_])
                        S.op("pool", lambda e, f=f, b_=b_: e.tensor_copy(out=b_[:, 1024:2048], in_=f[:, 1024:2048]), r=[f], w=[b_])
                        S.op("act", lambda e, f=f, b_=b_: e.copy(out=b_[:, 2048:HW], in_=f[:, 2048:HW]), r=[f], w=[b_])
                        S.dma("sp", wsc_in[l, hf * 6:(hf + 1) * 6, :, k * 512:(k + 1) * 512].rearrange("g p n -> p g n"),
                              b_[:].rearrange("p (g n) -> p g n", g=6), r=[b_], w=wsc_in_b[l][hf * 6:(hf + 1) * 6])
                for k in range(16):
                    f, b_ = f32s[it % 2], b16s[it % 2]
                    it += 1
                    S.dma("sp", f[:, 0:D], wout_d[l, k * 128:(k + 1) * 128, :], w=[f])
                    S.op("dve", lambda e, f=f, b_=b_: e.tensor_copy(out=b_[:, 0:1024], in_=f[:, 0:1024]), r=[f], w=[b_])
                    S.op("pool", lambda e, f=f, b_=b_: e.tensor_copy(out=b_[:, 1024:2048], in_=f[:, 1024:2048]), r=[f], w=[b_])
                    S.dma("sp", wsc_out[l, :, :, k * 512:(k + 1) * 512].rearrange("g p n -> p g n"),
                          b_[:, 0:D].rearrange("p (g n) -> p g n", g=4), r=[b_], w=wsc_out_b[l])
            S.barrier()

    try:
        ckpt("prepass")
    except _Stop:
        S.barrier(); S.emit(); stack.close(); return nc

    wslot = [0]

    converted = set()

    def load_w(kind, l, g):
        t = wbuf[wslot[0] % 2]
        wslot[0] += 1
        tl = (wsc_in_b if kind == "in" else wsc_out_b)[l][g]
        scr = (wsc_in if kind == "in" else wsc_out)[l, g, :, :]
        if (kind, l, g) not in converted:
            converted.add((kind, l, g))
            srcw = win_d if kind == "in" else wout_d
            src = srcw[l].rearrange("(k p) n -> p k n", p=128)[:, :, g * 512:(g + 1) * 512]
            S.dma("pool", t[:], src, w=[t], fresh=True)
            S.dma("sp", scr, t[:].rearrange("p k n -> p (k n)"), r=[t], w=[tl])
        else:
            S.dma("sp", t[:].rearrange("p k n -> p (k n)"), scr, r=[tl], w=[t])
        return t

    pj_rr = [0]

    def proj_chunk(wt, c):
        P = PS[pj_rr[0] % 2]
        pj_rr[0] += 1
        for k in range(16):
            S.mm(lambda e, k=k, P=P: e.matmul(P[:, :], lhsT=wt[:, k, c * 128:(c + 1) * 128], rhs=hT[:, k, :],
                                             start=(k == 0), stop=(k == 15)), r=[wt, hT], w=[P], last=(k == 15))
        return P

    def dump(name, tile_ap, tl):
        if name in dbg_d:
            S.dma("sp", dbg_d[name], tile_ap, r=[tl])

    wt_next = None
    x_pref = [None]
    fill_rr = [0]

    def layer_block(bi, l, last_layer):
        nonlocal wt_next
        first = (bi == 0)
        with contextlib.ExitStack() as st:
            prep = sb(st, "prep", [128, D])
            xn = sb(st, "xn", [128, 2, D], BF16)
            sm = sb(st, "sm", [128, 32])
            S.dma("sp", prep[:], prew_d[l:l + 1, :].partition_broadcast(128), w=[prep])
            for i in range(4):
                if l == 0:
                    S.dma("sp", xb[i][:, :], x_d[bi * TB + i * 128: bi * TB + (i + 1) * 128, :], w=[xb[i]])
                xv = xn[:, i % 2, :]
                S.op("act", lambda e, i=i, xv=xv: e.activation(out=xv, in_=xb[i][:, :], func=AF.Square,
                                                               accum_out=sm[:, i * 8:i * 8 + 1]), r=[xb[i]], w=[xn, sm])
                S.op("dve", lambda e, i=i: e.tensor_scalar(out=sm[:, i * 8 + 1:i * 8 + 2], in0=sm[:, i * 8:i * 8 + 1], scalar1=1.0 / D, scalar2=NORM_EPS,
                                                      op0=ALU.mult, op1=ALU.add), r=[sm], w=[sm])
                S.op("act", lambda e, i=i: e.sqrt(out=sm[:, i * 8 + 2:i * 8 + 3], in_=sm[:, i * 8 + 1:i * 8 + 2]), r=[sm], w=[sm])
                S.op("dve", lambda e, i=i: e.reciprocal(out=sm[:, i * 8 + 3:i * 8 + 4], in_=sm[:, i * 8 + 2:i * 8 + 3]), r=[sm], w=[sm])
                S.op("dve", lambda e, i=i, xv=xv: e.scalar_tensor_tensor(out=xv, in0=xb[i][:, :], scalar=sm[:, i * 8 + 3:i * 8 + 4],
                                                                         in1=prep[:], op0=ALU.mult, op1=ALU.mult),
                     r=[xb[i], sm, prep], w=[xn])
                for half in range(2):
                    P = PS[2 + half + 2 * (i % 2)]
                    Pb = P[:, :].bitcast(BF16)
                    for kk in range(8):
                        k = half * 8 + kk
                        S.mm(lambda e, k=k, kk=kk, Pb=Pb, xv=xv: e.transpose(Pb[:, kk * 128:(kk + 1) * 128],
                                                                          xv[:, k * 128:(k + 1) * 128], C_("ident")),
                             r=[xn, cb], w=[P], last=(kk == 7))
                    S.op("act", lambda e, i=i, half=half, Pb=Pb: e.copy(
                        out=hT[:, half * 8:(half + 1) * 8, i * 128:(i + 1) * 128],
                        in_=Pb.rearrange("p (k t) -> p k t", k=8)), r=[P], w=[hT])
            rel(st)
        ckpt("N")

        with contextlib.ExitStack() as st:
            HRT = []
            for hr in range(2):
                sh = st
                lnw = sb(sh, "lnw", [128, 256])
                lnb = sb(sh, "lnb", [128, 256])
                AR = [sb(sh, "AR%d" % i, [128, 4, 2, 128], BF16) for i in range(2)]
                BT = [sb(sh, "BT%d" % i, [128, TB], BF16) for i in range(2)]
                KT = [sb(sh, "KT%d" % i, [128, TB], BF16) for i in range(2)]
                rkp = [sb(sh, "rkp%d" % i, [128, TB], BF16) for i in range(2)]
                vSb = [sb(sh, "vSb%d" % i, [128, TB], BF16) for i in range(2)]
                sgA = [sb(sh, "sgA%d" % i, [128, TB], BF16) for i in range(2)]
                rho = [sb(sh, "rho%d" % i, [128, 4]) for i in range(2)]
                sC = [sb(sh, "sC%d" % i, [128, 4]) for i in range(2)]
                HRT.append((lnw, lnb, AR, BT, KT, rkp, vSb, sgA, rho, sC))
            with contextlib.ExitStack() as pre:
                wupl = sb(pre, "wupl", [128, 512], BF16)
                aupl = sb(pre, "aupl", [128, 512], BF16)
                S.dma("sp", wupl[:], csc["wup"][:, l * 512:(l + 1) * 512], r=[csc_b["wup"]], w=[wupl])
                S.dma("sp", aupl[:], csc["aup"][:, l * 512:(l + 1) * 512], r=[csc_b["aup"]], w=[aupl])
                lora_t = sb(pre, "lora", [128, TB], BF16)
                zs = [sb(pre, "zs%d" % i, [128, TB]) for i in range(2)]
                zs_rr = [0]
                zds = [sb(pre, "zd%d" % i, [128, TB]) for i in range(2)]

                zd0s = [sb(pre, "zdc%d" % i, [128, 2]) for i in range(2)]

                def shifted(P, ci, dst_ap, dst_tl, eng2="dve"):
                    z = zs[zs_rr[0] % 2]
                    zd = zds[zs_rr[0] % 2]
                    zd0 = zd0s[zs_rr[0] % 2]
                    zs_rr[0] += 1
                    S.op("act", lambda e: e.copy(out=z[:, 0:TB], in_=P[:, :]), r=[P], w=[z])
                    S.op("dve", lambda e: e.tensor_tensor(out=zd[:, 1:TB], in0=z[:, 0:TB - 1], in1=z[:, 1:TB], op=ALU.subtract), r=[z], w=[zd])
                    S.op("dve", lambda e: e.scalar_tensor_tensor(out=dst_ap[:, 1:TB], in0=zd[:, 1:TB], scalar=pc(l, MU + ci), in1=z[:, 1:TB],
                                                                 op0=ALU.mult, op1=ALU.add), r=[z, zd, pcol], w=[dst_tl])
                    S.op("pool", lambda e: e.tensor_tensor(out=zd0[:, 0:1], in0=carryA[:, l, ci:ci + 1], in1=z[:, 0:1], op=ALU.subtract), r=[carryA, z], w=[zd0])
                    S.op("pool", lambda e: e.tensor_copy(out=carryA[:, l, ci:ci + 1], in_=z[:, TB - 1:TB]), r=[z, zd0], w=[carryA])
                    S.op("pool", lambda e: e.tensor_scalar(out=dst_ap[:, 0:1], in0=zd0[:, 0:1], scalar1=pc(l, MU + ci), scalar2=z[:, 0:1],
                                                           op0=ALU.mult, op1=ALU.add), r=[zd0, z, pcol], w=[dst_tl])

                NT = 6
                tmp = [sb(pre, "rt%d" % i, [128, TB]) for i in range(NT)]
                wt = x_pref[0] if x_pref[0] is not None else load_w("in", l, G_X)
                x_pref[0] = None
                wt_next = load_w("in", l, G_A0)
                P = proj_chunk(wt, 0)
                lraw = tmp[0]
                shifted(P, 12, lraw[:, :], lraw)
                S.op("act", lambda e: e.activation(out=lora_t[0:64, :], in_=lraw[0:64, :], func=AF.Tanh), r=[lraw], w=[lora_t])
                S.op("dve", lambda e: e.tensor_copy(out=lora_t[64:128, :], in_=lraw[64:128, :]), r=[lraw], w=[lora_t])
                for g in range(2):
                    P = proj_chunk(wt, 1 + g)
                    S.op("act", lambda e, g=g, P=P: e.activation(out=kxm[:, l, 2 * g, 128:640], in_=P[:, :], func=AF.Identity,
                                                                 scale=cf[:, 512:513]), r=[P, cf], w=[kxm])
                    S.op("dve", lambda e, g=g, P=P: e.tensor_scalar(out=kxm[:, l, 2 * g + 1, 128:640], in0=P[:, :],
                                                                    scalar1=cf[:, 513:514], scalar2=None, op0=ALU.mult),
                         r=[P, cf], w=[kxm])
                P = proj_chunk(wt, 3)
                vvT = sb(pre, "vvT", [128, TB], BF16)
                S.op("act", lambda e, P=P: e.copy(out=vvT[:, :], in_=P[:, :]), r=[P], w=[vvT])
                Pt = PS[2]
                Ptb = Pt[:, :].bitcast(BF16)
                for i in range(4):
                    S.mm(lambda e, i=i: e.transpose(Ptb[:, i * 128:(i + 1) * 128], vvT[:, i * 128:(i + 1) * 128], C_("ident")),
                         r=[vvT, cb], w=[Pt], last=(i == 3))
                Ptv = Ptb[:, 0:512].rearrange("p (i c) -> p i c", i=4)
                for vi, (src0, dst0) in enumerate([(0, 0), (0, 64), (64, 0), (64, 64)]):
                    ew(lambda e, vi=vi, src0=src0, dst0=dst0: e.tensor_copy(out=vpad[:, l, 1:5, vi, dst0:dst0 + 64],
                                                                            in_=Ptv[:, :, src0:src0 + 64]),
                       r=[Pt], w=[vpad], engines=("dve",))

                ckpt("AX")
                tmp2 = [sb(pre, "ru%d" % i, [128, TB]) for i in range(6)]
                sqbs = [sb(pre, "sqb%d" % i, [128, TB], BF16) for i in range(2)]
                for hr in range(2):
                    pairs = [2 * hr, 2 * hr + 1]
                    lnw, lnb, AR, BT, KT, rkp, vSb, sgA, rho, sC = HRT[hr]
                    S.dma("sp", lnw[:], lnw_d[l:l + 1, hr * 256:(hr + 1) * 256].partition_broadcast(128), w=[lnw])
                    S.dma("sp", lnb[:], lnb_d[l:l + 1, hr * 256:(hr + 1) * 256].partition_broadcast(128), w=[lnb])
                    for pi, p in enumerate(pairs):
                        wt = wt_next
                        nxt = p + 1
                        wt_next = load_w("in", l, G_A0 + nxt) if nxt < 4 else load_w("in", l, G_Q)
                        rS, kS, sgw, cs, av, t5 = tmp if pi == 0 else tmp2
                        P = proj_chunk(wt, 0)
                        shifted(P, p, rS[:, :], rS)
                        P = proj_chunk(wt, 1)
                        shifted(P, 4 + p, kS[:, :], kS)
                        P = proj_chunk(wt, 2)
                        shifted(P, 8 + p, vSb[pi][:, :], vSb[pi])
                        P = proj_chunk(wt, 3)
                        S.op("act", lambda e, P=P, pi=pi: e.activation(out=sgA[pi][:, :], in_=P[:, :], func=AF.Silu),
                             r=[P], w=[sgA[pi]])
                        PA, PB = PS[2], PS[3]
                        S.mm(lambda e, p=p: e.matmul(PA[:, :], lhsT=wupl[:, p * 128:(p + 1) * 128], rhs=lora_t[:, :],
                                                     start=True, stop=True), r=[wupl, lora_t], w=[PA])
                        S.mm(lambda e, p=p: e.matmul(PB[:, :], lhsT=aupl[:, p * 128:(p + 1) * 128], rhs=lora_t[:, :],
                                                     start=True, stop=True), r=[aupl, lora_t], w=[PB])
                        S.op("act", lambda e, p=p: e.activation(out=sgw[:, :], in_=PA[:, :], func=AF.Sigmoid,
                                                                bias=pc(l, W0 + p)), r=[PA, pcol], w=[sgw])
                        S.op("act", lambda e, p=p: e.activation(out=av[:, :], in_=PB[:, :], func=AF.Sigmoid,
                                                                bias=pc(l, A0 + p)), r=[PB, pcol], w=[av])
                        S.op("dve", lambda e: e.tensor_tensor_scan(out=cs[:, :], data0=cf[:, 0:512], data1=sgw[:, :], initial=0.0,
                                                                   op0=ALU.mult, op1=ALU.add), r=[cf, sgw], w=[cs])
                        S.op("act", lambda e, pi=pi: e.activation(out=rho[pi][:, :], in_=cs[:, 63:512:128], func=AF.Exp, scale=C0),
                             r=[cs], w=[rho[pi]])
                        cc = t5
                        S.op("dve", lambda e: e.tensor_tensor(out=cc[:, :].rearrange("p (c t) -> p c t", c=4),
                                                              in0=cs[:, :].rearrange("p (c t) -> p c t", c=4),
                                                              in1=cs[:, 63:512:128].unsqueeze(2).to_broadcast([128, 4, 128]),
                                                              op=ALU.subtract), r=[cs], w=[cc])
                        S.op("act", lambda e, pi=pi: e.activation(out=sC[pi][:, :], in_=cc[:, 127:512:128], func=AF.Exp, scale=C0),
                             r=[cc], w=[sC[pi]])
                        S.op("pool", lambda e: e.tensor_tensor(out=sgw[:, :], in0=cc[:, :], in1=sgw[:, :], op=ALU.subtract),
                             r=[cc, sgw], w=[sgw])
                        S.op("act", lambda e: e.activation(out=sgw[:, :], in_=sgw[:, :], func=AF.Exp, scale=C0), r=[sgw], w=[sgw])
                        S.op("act", lambda e: e.activation(out=cs[:, :], in_=cc[:, :], func=AF.Exp, scale=C0), r=[cc], w=[cs])
                        S.op("act", lambda e: e.activation(out=cc[:, :], in_=cc[:, :], func=AF.Exp, scale=-C0), r=[cc], w=[cc])
                        eprev, epos, eneg = sgw, cs, cc
                        S.op("pool", lambda e, pi=pi: e.tensor_tensor(out=AR[pi][:, :, 1, :],
                                                                      in0=rS[:, :].rearrange("p (c t) -> p c t", c=4),
                                                                      in1=epos[:, :].rearrange("p (c t) -> p c t", c=4), op=ALU.mult),
                             r=[rS, epos], w=[AR[pi]])
                        S.op("dve", lambda e, p=p: e.tensor_scalar(out=cs[:, :], in0=av[:, :], scalar1=pc(l, KA + p),
                                                                   scalar2=omka[:, l * 4 + p:l * 4 + p + 1], op0=ALU.mult, op1=ALU.add),
                             r=[av, pcol, omka], w=[cs])
                        S.op("pool", lambda e: e.tensor_tensor(out=cs[:, :], in0=cs[:, :], in1=kS[:, :], op=ALU.mult), r=[cs, kS], w=[cs])
                        kmod = cs
                        S.op("dve", lambda e, p=p, pi=pi: e.scalar_tensor_tensor(out=rkp[pi][:, :], in0=rS[:, :], scalar=pc(l, RK + p),
                                                                                in1=kmod[:, :], op0=ALU.mult, op1=ALU.mult),
                             r=[rS, pcol, kmod], w=[rkp[pi]])
                        S.op("pool", lambda e, pi=pi: e.tensor_tensor(out=KT[pi][:, :], in0=kmod[:, :], in1=eneg[:, :], op=ALU.mult),
                             r=[kmod, eneg], w=[KT[pi]])
                        kraw = rS
                        S.op("dve", lambda e, p=p: e.tensor_scalar(out=kraw[:, :], in0=kS[:, :], scalar1=pc(l, KK + p), scalar2=None,
                                                                   op0=ALU.mult), r=[kS, pcol], w=[kraw])
                        S.op("act", lambda e, sqb=sqbs[pi]: e.activation(out=sqb[:, :], in_=kraw[:, :], func=AF.Square), r=[kraw], w=[sqbs[pi]])
                    for pi, p in enumerate(pairs):
                        rS, kS, sgw, cs, av, t5 = tmp if pi == 0 else tmp2
                        PA, PB = PS[2], PS[3]
                        cc = t5
                        eprev, epos, eneg = sgw, cs, cc
                        kmod = cs
                        kraw = rS
                        S.mm(lambda e, sqb=sqbs[pi]: e.matmul(PA[:, :], lhsT=C_("bones"), rhs=sqb[:, :], start=True, stop=True),
                             r=[cb, sqbs[pi]], w=[PA])
                        nrm = kS
                        S.op("act", lambda e: e.sqrt(out=nrm[:, :], in_=PA[:, :]), r=[PA], w=[nrm])
                        S.op("dve", lambda e: e.tensor_scalar(out=nrm[:, :], in0=nrm[:, :], scalar1=1e-12, scalar2=None, op0=ALU.max),
                             r=[nrm], w=[nrm])
                        S.op("dve", lambda e: e.reciprocal(out=nrm[:, :], in_=nrm[:, :]), r=[nrm], w=[nrm])
                        kk = kraw
                        S.op("pool", lambda e: e.tensor_tensor(out=kk[:, :], in0=kraw[:, :], in1=nrm[:, :], op=ALU.mult), r=[kraw, nrm], w=[kk])
                        S.op("dve", lambda e, pi=pi: e.scalar_tensor_tensor(out=AR[pi][:, :, 0, :],
                                                                           in0=kk[:, :].rearrange("p (c t) -> p c t", c=4), scalar=-1.0,
                                                                           in1=eprev[:, :].rearrange("p (c t) -> p c t", c=4),
                                                                           op0=ALU.mult, op1=ALU.mult), r=[kk, eprev], w=[AR[pi]])
                        S.op("pool", lambda e: e.tensor_tensor(out=av[:, :], in0=av[:, :], in1=eneg[:, :], op=ALU.mult), r=[av, eneg], w=[av])
                        S.op("dve", lambda e, pi=pi: e.tensor_tensor(out=BT[pi][:, :], in0=kk[:, :], in1=av[:, :], op=ALU.mult),
                             r=[kk, av], w=[BT[pi]])
                rel(pre)
            ckpt("Apre")
            wtQ = wt_next

            def chunk_loop(hr, bk):
                sh = st
                pairs = [2 * hr, 2 * hr + 1]
                lnw, lnb, AR, BT, KT, rkp, vSb, sgA, rho, sC = HRT[hr]
                BTm = [[sb(sh, "BTm%d%d" % (i, e_), [128, 128], BF16) for e_ in range(2)] for i in range(2)]
                KTm = [[sb(sh, "KTm%d%d" % (i, e_), [128, 128], BF16) for e_ in range(2)] for i in range(2)]
                ATm = [[sb(sh, "ATm%d%d" % (i, e_), [128, 128], BF16) for e_ in range(2)] for i in range(2)]
                LA = sb(sh, "LA", [128, 4, 2, 128], BF16)
                KA = sb(sh, "KA", [128, 4, 2, 128], BF16)
                ZL0 = sb(sh, "ZL0", [128, 4, 2, 128], BF16)
                ZLp = [[sb(sh, "ZLp%d%d" % (i, j), [128, 2, 2, 128], BF16) for j in range(2)] for i in range(2)]
                LTp = [[sb(sh, "LTp%d%d" % (i, j), [128, 2, 128], BF16) for j in range(2)] for i in range(2)]
                TK = sb(sh, "TK", [128, 2, 4, 128], BF16)
                Pp = sb(sh, "Pp", [128, 2, 128], BF16)
                Ul = sb(sh, "Ul", [128, 2, 128], BF16)
                GT = sb(sh, "GT", [128, 2, 128], BF16)
                MtT = sb(sh, "MtT", [128, 2, 128], BF16)
                Hb = sb(sh, "Hb", [128, 2, 128], BF16)
                gs = sb(sh, "gs", [128, 32])
                rkt = sb(sh, "rkt", [128, 4])
                sq = sb(sh, "sq", [128, 256])
                yn = sb(sh, "yn", [128, 256])
                yab = sb(sh, "yab", [128, 256], BF16)
                for ch in range(4):
                    csl = slice(ch * 128, (ch + 1) * 128)
                    for pi in range(2):
                        for e_ in range(2):
                            hmc = cf[:, 512 + e_:513 + e_]
                            S.op("act", lambda e: e.activation(out=BTm[pi][e_][:, :], in_=BT[pi][:, csl], func=AF.Identity, scale=hmc), r=[BT[pi], cf], w=[BTm[pi][e_]])
                            S.op("pool", lambda e: e.tensor_scalar(out=KTm[pi][e_][:, :], in0=KT[pi][:, csl], scalar1=hmc, scalar2=1.0, op0=ALU.mult, op1=ALU.mult), r=[KT[pi], cf], w=[KTm[pi][e_]])
                            S.op("act", lambda e: e.activation(out=ATm[pi][e_][:, :], in_=AR[pi][:, ch, 0, :], func=AF.Identity, scale=hmc), r=[AR[pi], cf], w=[ATm[pi][e_]])
                    PLA = [PS[bk[2]], PS[bk[3]]]
                    PKA = [PS[bk[4]], PS[bk[5]]]
                    PL = PS[bk[6]]
                    for hh in range(4):
                        pi, e_ = hh // 2, hh % 2
                        o2 = slice(e_ * 256, (e_ + 1) * 256)
                        hs = slice(hh * 128, (hh + 1) * 128)
                        rhs2 = AR[pi][:, ch, :, :].rearrange("p a t -> p (a t)")
                        S.mm(lambda e: e.matmul(PLA[pi][:, o2], lhsT=BTm[pi][e_][:, :], rhs=rhs2, start=True, stop=True), r=[BTm[pi][e_], AR[pi]], w=[PLA[pi]], last=(e_ == 1))
                        S.mm(lambda e: e.matmul(PKA[pi][:, o2], lhsT=KTm[pi][e_][:, :], rhs=rhs2, start=True, stop=True), r=[KTm[pi][e_], AR[pi]], w=[PKA[pi]], last=(e_ == 1))
                        S.mm(lambda e: e.matmul(PL[:, hs], lhsT=ATm[pi][e_][:, :], rhs=BT[pi][:, csl], start=True, stop=True), r=[ATm[pi][e_], BT[pi]], w=[PL], last=(hh == 3))
                    for pi in range(2):
                        o_la = LA[:, 2 * pi:2 * pi + 2, :, :].rearrange("p h a t -> p (h a t)")
                        o_ka = KA[:, 2 * pi:2 * pi + 2, :, :].rearrange("p h a t -> p (h a t)")
                        S.op("dve", lambda e: e.tensor_tensor(out=o_la, in0=PLA[pi][:, :], in1=C_("suui2"), op=ALU.mult), r=[PLA[pi], cb], w=[LA])
                        S.op("dve", lambda e: e.tensor_tensor(out=o_ka, in0=PKA[pi][:, :], in1=C_("suui2"), op=ALU.mult), r=[PKA[pi], cb], w=[KA])
                    S.op("dve", lambda e: e.tensor_tensor(out=ZL0[:, :, 1, :], in0=PL[:, :].rearrange("p (h t) -> p h t", h=4),
                                                          in1=C_("sl4").rearrange("p (h t) -> p h t", h=4), op=ALU.mult), r=[PL, cb], w=[ZL0])
                    yield
                    Pt = PS[bk[7]]
                    Ptb = Pt[:, :].bitcast(BF16)
                    for pi in range(2):
                        srcs = [(AR[pi], AR[pi][:, ch, 0, :]), (BT[pi], BT[pi][:, csl]), (KT[pi], KT[pi][:, csl]), (vSb[pi], vSb[pi][:, csl])]
                        for qi, (tl_, ap_) in enumerate(srcs):
                            o0 = (pi * 4 + qi) * 128
                            S.mm(lambda e, ap_=ap_, o0=o0: e.transpose(Ptb[:, o0:o0 + 128], ap_, C_("ident")), r=[tl_, cb], w=[Pt],
                                 last=(pi == 1 and qi == 3))
                    S.op("dve", lambda e: e.tensor_copy(out=TK[:].rearrange("p a q c -> p (a q c)"), in_=Ptb[:, 0:1024]), r=[Pt], w=[TK])
                    PZ = PS[bk[2]]
                    for hh in range(4):
                        pi, e_ = hh // 2, hh % 2
                        S.mm(lambda e, hh=hh, pi=pi, e_=e_: e.matmul(PZ[:, hh * 128 + 64: hh * 128 + 128], lhsT=KA[:, hh, 0, :],
                                                                      rhs=TK[:, pi, 3, e_ * 64:(e_ + 1) * 64], start=True, stop=True),
                             r=[KA, TK], w=[PZ], last=(hh == 3))
                    PZv = PZ[:, :].rearrange("p (h c) -> p h c", h=4)
                    S.op("act", lambda e: e.copy(out=ZL0[:, :, 0, 64:128], in_=PZv[:, :, 64:128]), r=[PZ], w=[ZL0])
                    S.op("dve", lambda e: e.tensor_copy(out=ZL0[:, :, 0, 0:64].rearrange("p (a b) c -> p a b c", a=2),
                                                         in_=TK[:, :, 0, :].rearrange("p a (b c) -> p a b c", b=2)), r=[TK], w=[ZL0])
                    yield
                    for step in range(7):
                        for pi in range(2):
                            PZL, PLTq = PS[bk[2 + pi]], PS[bk[4]]
                            lo_ = pi * 256
                            if step == 0:
                                zlT, ltT = ZL0, LA
                                zl = [ZL0[:, 2 * pi + e_, :, :] for e_ in range(2)]
                                ltA = [LA[:, 2 * pi + e_, 0, :] for e_ in range(2)]
                            else:
                                cur = (step - 1) % 2
                                zlT, ltT = ZLp[pi][cur], LTp[pi][cur]
                                zl = [zlT[:, e_, :, :] for e_ in range(2)]
                                ltA = [ltT[:, e_, :] for e_ in range(2)]
                            for e_ in range(2):
                                rhs_ = zl[e_].rearrange("p a t -> p (a t)")
                                S.mm(lambda e: e.matmul(PZL[:, e_ * 256:(e_ + 1) * 256], lhsT=ltA[e_], rhs=rhs_, start=True, stop=True), r=[ltT, zlT], w=[PZL], last=(e_ == 1))
                            if step < 6:
                                for e_ in range(2):
                                    lk_ = zl[e_][:, 1, :]
                                    S.mm(lambda e: e.matmul(PLTq[:, lo_ + e_ * 128:lo_ + (e_ + 1) * 128], lhsT=lk_, rhs=ltA[e_], start=True, stop=True), r=[zlT, ltT], w=[PLTq], last=(e_ == 1))
                            for f_ in range(NFILL):
                                Pf = PS[fill_rr[0] % 2]
                                fill_rr[0] += 1
                                S.mm(lambda e: e.matmul(Pf[:, :], lhsT=C_("ident"), rhs=hT[:, 0, :], start=True, stop=True), r=[cb, hT], w=[Pf], last=False)
                        Pq = PS[ch % 2]
                        kb = [0, 2, 3, 4, 5, 6, 7, 8, 9, 10, 11, 12, 13, 14, 16]
                        slot = step * 2 + hr
                        for k in range(kb[slot], kb[slot + 1]):
                            S.mm(lambda e: e.matmul(Pq[:, :], lhsT=wtQ[:, k, ch * 128:(ch + 1) * 128], rhs=hT[:, k, :], start=(k == 0), stop=(k == 15)),
                                 r=[wtQ, hT], w=[Pq], last=(k == 15))
                        if slot == 13:
                            S.op("act", lambda e: e.copy(out=qTp[ch][:, :], in_=Pq[:, :]), r=[Pq], w=[qTp[ch]])
                        for pi in range(2):
                            PZL, PLTq = PS[bk[2 + pi]], PS[bk[4]]
                            lo_ = pi * 256
                            nx = step % 2
                            PZLv = PZL[:, :].rearrange("p (e a t) -> p e a t", e=2, a=2)
                            if step == 0:
                                zprevT, zprev = ZL0, ZL0[:, 2 * pi:2 * pi + 2, 0, :]
                            else:
                                zprevT = ZLp[pi][(step - 1) % 2]
                                zprev = zprevT[:, :, 0, :]
                            if step < 6:
                                o_z = ZLp[pi][nx][:, :, 0, :]
                                o_l = ZLp[pi][nx][:, :, 1, :]
                                o_lt = LTp[pi][nx][:].rearrange("p h t -> p (h t)")
                                S.op("dve", lambda e: e.tensor_tensor(out=o_z, in0=PZLv[:, :, 0, :], in1=zprev, op=ALU.add), r=[PZL, zprevT], w=[ZLp[pi][nx]])
                                S.op("act", lambda e: e.copy(out=o_l, in_=PZLv[:, :, 1, :]), r=[PZL], w=[ZLp[pi][nx]])
                                S.op("dve", lambda e: e.tensor_copy(out=o_lt, in_=PLTq[:, lo_:lo_ + 256]), r=[PLTq], w=[LTp[pi][nx]])
                            else:
                                o_p = Pp[:, pi, :].rearrange("p (b c) -> p b c", b=2)
                                o_u = Ul[:, pi, :].rearrange("p (b c) -> p b c", b=2)
                                S.op("dve", lambda e: e.tensor_tensor(out=o_p, in0=PZLv[:, :, 0, 0:64], in1=zprev[:, :, 0:64], op=ALU.add), r=[PZL, zprevT], w=[Pp])
                                S.op("dve", lambda e: e.tensor_tensor(out=o_u, in0=PZLv[:, :, 0, 64:128], in1=zprev[:, :, 64:128], op=ALU.add), r=[PZL, zprevT], w=[Ul])
                        yield
                    yield
                    PG = PS[bk[2]]
                    for hh in range(4):
                        pi = hh // 2
                        S.mm(lambda e, hh=hh, pi=pi: e.matmul(PG[:, hh * 128:(hh + 1) * 128], lhsT=Pp[:, pi, :], rhs=LA[:, hh, 1, :], start=True, stop=True),
                             r=[Pp, LA], w=[PG], last=(hh == 3))
                    for pi in range(2):
                        for e_ in range(2):
                            hh = pi * 2 + e_
                            rows = slice(e_ * 64, (e_ + 1) * 64)
                            S.op("dve", lambda e, pi=pi, hh=hh, rows=rows: e.tensor_tensor(out=GT[rows, pi, :], in0=PG[rows, hh * 128:(hh + 1) * 128],
                                                                                          in1=AR[pi][rows, ch, 1, :], op=ALU.add), r=[PG, AR[pi]], w=[GT])
                    PM = PS[bk[6]]
                    for pi in range(2):
                        S.mm(lambda e, pi=pi: e.matmul(PM[:, pi * 128:(pi + 1) * 128], lhsT=Pp[:, pi, :], rhs=TK[:, pi, 1, :], start=True, stop=True),
                             r=[Pp, TK], w=[PM], last=(pi == 1))
                    S.op("dve", lambda e: e.tensor_tensor(out=MtT[:].rearrange("p a c -> p (a c)"), in0=PM[:, 0:256], in1=C_("bd2"), op=ALU.mult), r=[PM, cb], w=[MtT])
                    for pi in range(2):
                        p = pairs[pi]
                        S.op("act", lambda e, pi=pi, p=p: e.activation(out=Hb[:, pi, :], in_=Hc[:, l, p, :], func=AF.Identity, scale=rho[pi][:, ch:ch + 1]), r=[Hc, rho[pi]], w=[Hb])
                    PY = PS[bk[3]]
                    for pi in range(2):
                        S.mm(lambda e, pi=pi: e.matmul(PY[:, pi * 128:(pi + 1) * 128], lhsT=GT[:, pi, :], rhs=Hb[:, pi, :], start=True, stop=False),
                             r=[GT, Hb], w=[PY], last=False)
                        for e_ in range(2):
                            hh = pi * 2 + e_
                            cs_ = slice(pi * 128 + e_ * 64, pi * 128 + (e_ + 1) * 64)
                            S.mm(lambda e, pi=pi, e_=e_, hh=hh, cs_=cs_: e.matmul(PY[:, cs_], lhsT=LA[:, hh, 1, :], rhs=Ul[:, pi, e_ * 64:(e_ + 1) * 64], start=False, stop=False),
                                 r=[LA, Ul], w=[PY], last=False)
                            S.mm(lambda e, pi=pi, e_=e_, hh=hh, cs_=cs_: e.matmul(PY[:, cs_], lhsT=KA[:, hh, 1, :], rhs=TK[:, pi, 3, e_ * 64:(e_ + 1) * 64], start=False, stop=(e_ == 1)),
                                 r=[KA, TK], w=[PY], last=(pi == 1 and e_ == 1))
                    PC_ = PS[bk[4]]
                    for pi in range(2):
                        o_ = slice(pi * 128, (pi + 1) * 128)
                        S.mm(lambda e, pi=pi, o_=o_: e.matmul(PC_[:, o_], lhsT=MtT[:, pi, :], rhs=Hb[:, pi, :], start=True, stop=False), r=[MtT, Hb], w=[PC_], last=False)
                        S.mm(lambda e, pi=pi, o_=o_: e.matmul(PC_[:, o_], lhsT=C_("ident"), rhs=Hb[:, pi, :], start=False, stop=False), r=[cb, Hb], w=[PC_], last=False)
                        S.mm(lambda e, pi=pi, o_=o_: e.matmul(PC_[:, o_], lhsT=TK[:, pi, 1, :], rhs=Ul[:, pi, :], start=False, stop=False), r=[TK, Ul], w=[PC_], last=False)
                        S.mm(lambda e, pi=pi, o_=o_: e.matmul(PC_[:, o_], lhsT=TK[:, pi, 2, :], rhs=TK[:, pi, 3, :], start=False, stop=True), r=[TK], w=[PC_], last=(pi == 1))
                    for pi in range(2):
                        p = pairs[pi]
                        S.op("dve", lambda e, pi=pi, p=p: e.scalar_tensor_tensor(out=Hc[:, l, p, :], in0=PC_[:, pi * 128:(pi + 1) * 128], scalar=sC[pi][:, ch:ch + 1],
                                                                                in1=C_("bd2", 0, 128), op0=ALU.mult, op1=ALU.mult), r=[PC_, sC[pi], cb], w=[Hc])
                    pass
                    PR = PS[bk[5]]
                    for pi in range(2):
                        S.mm(lambda e, pi=pi: e.matmul(PR[:, pi * 2:(pi + 1) * 2], lhsT=rkp[pi][:, csl], rhs=C_("hsel"), start=True, stop=True), r=[rkp[pi], cb], w=[PR], last=(pi == 1))
                    PYv = PY[:, 0:256].rearrange("p (h i) -> p h i", h=4)
                    S.op("act", lambda e: e.activation(out=sq[:, :], in_=PY[:, 0:256], func=AF.Square), r=[PY], w=[sq])
                    S.op("act", lambda e: e.copy(out=rkt[:, 0:4], in_=PR[:, 0:4]), r=[PR], w=[rkt])
                    S.op("dve", lambda e: e.tensor_reduce(out=gs[:, 0:4], in_=PYv, axis=AX.X, op=ALU.add), r=[PY], w=[gs])
                    S.op("dve", lambda e: e.tensor_reduce(out=gs[:, 4:8], in_=sq[:, :].rearrange("p (h i) -> p h i", h=4), axis=AX.X, op=ALU.add), r=[sq], w=[gs])
                    S.op("dve", lambda e: e.tensor_scalar(out=gs[:, 8:12], in0=gs[:, 0:4], scalar1=1.0 / 64, scalar2=None, op0=ALU.mult), r=[gs], w=[gs])
                    S.op("dve", lambda e: e.tensor_tensor(out=gs[:, 12:16], in0=gs[:, 8:12], in1=gs[:, 8:12], op=ALU.mult), r=[gs], w=[gs])
                    S.op("dve", lambda e: e.scalar_tensor_tensor(out=gs[:, 16:20], in0=gs[:, 4:8], scalar=1.0 / 64, in1=gs[:, 12:16], op0=ALU.mult, op1=ALU.subtract), r=[gs], w=[gs])
                    S.op("dve", lambda e: e.tensor_scalar(out=gs[:, 16:20], in0=gs[:, 16:20], scalar1=GN_EPS, scalar2=None, op0=ALU.add), r=[gs], w=[gs])
                    S.op("act", lambda e: e.sqrt(out=gs[:, 20:24], in_=gs[:, 16:20]), r=[gs], w=[gs])
                    S.op("dve", lambda e: e.reciprocal(out=gs[:, 24:28], in_=gs[:, 20:24]), r=[gs], w=[gs])
                    ynv = yn[:, :].rearrange("p (h i) -> p h i", h=4)
                    S.op("dve", lambda e: e.tensor_tensor(out=ynv, in0=PYv, in1=gs[:, 8:12].unsqueeze(2).to_broadcast([128, 4, 64]), op=ALU.subtract), r=[PY, gs], w=[yn])
                    S.op("dve", lambda e: e.tensor_tensor(out=ynv, in0=ynv, in1=gs[:, 24:28].unsqueeze(2).to_broadcast([128, 4, 64]), op=ALU.mult), r=[yn, gs], w=[yn])
                    S.op("dve", lambda e: e.tensor_tensor(out=yn[:, :], in0=yn[:, :], in1=lnw[:, :], op=ALU.mult), r=[yn, lnw], w=[yn])
                    S.op("dve", lambda e: e.tensor_tensor(out=yn[:, :], in0=yn[:, :], in1=lnb[:, :], op=ALU.add), r=[yn, lnb], w=[yn])
                    S.op("dve", lambda e: e.tensor_tensor(out=sq[:, :].rearrange("p (a b c) -> p a b c", a=2, b=2),
                                                          in0=TK[:, :, 3, :].rearrange("p a (b c) -> p a b c", b=2),
                                                          in1=rkt[:, 0:4].rearrange("p (a b) -> p a b", a=2).unsqueeze(3).to_broadcast([128, 2, 2, 64]), op=ALU.mult),
                         r=[TK, rkt], w=[sq])
                    S.op("dve", lambda e: e.tensor_tensor(out=yab[:, :], in0=yn[:, :], in1=sq[:, :], op=ALU.add), r=[yn, sq], w=[yab])
                    Pt2 = PS[bk[6]]
                    Pt2b = Pt2[:, :].bitcast(BF16)
                    for pi in range(2):
                        S.mm(lambda e, pi=pi: e.transpose(Pt2b[:, pi * 128:(pi + 1) * 128], yab[:, pi * 128:(pi + 1) * 128], C_("ident")), r=[yab, cb], w=[Pt2], last=(pi == 1))
                    for pi in range(2):
                        p = pairs[pi]
                        S.op("dve", lambda e, pi=pi, p=p: e.tensor_tensor(out=yT[:, p, csl], in0=Pt2b[:, pi * 128:(pi + 1) * 128], in1=sgA[pi][:, csl], op=ALU.mult),
                             r=[Pt2, sgA[pi]], w=[yT])
                    yield

            gens = [chunk_loop(0, [0, 1, 2, 3, 4, 5, 6, 7]), chunk_loop(1, [0, 1, 5, 6, 7, 2, 3, 4])]
            live = [gens[0]]
            started = 1
            nyield = 0
            while live:
                for g_ in list(live):
                    try:
                        next(g_)
                    except StopIteration:
                        live.remove(g_)
                nyield += 1
                if started < 2 and (nyield >= STAGGER or not live):
                    live.append(gens[1])
                    started = 2
            rel(st)
        ckpt("A")

        with contextlib.ExitStack() as st:
            qT = qTp
            sgB = [sb(st, "sgB%d" % c, [128, TB], BF16) for c in range(4)]
            wt = wt_next
            wt_next = load_w("in", l, G_GB)
            wt = wt_next
            wt_next = load_w("in", l, G_C)
            for c in range(4):
                P = proj_chunk(wt, c)
                S.op("act", lambda e, c=c, P=P: e.activation(out=sgB[c][:, :], in_=P[:, :], func=AF.Silu), r=[P], w=[sgB[c]])
            Et = sb(st, "Et", [128, 2048], BF16)
            S.dma("sp", Et[:], cb_d[:, NCBS:NCBS + 2048], w=[Et])
            pexp = sb(st, "pexp", [128, 1024], BF16)
            pT = sb(st, "pT", [128, 8, 128], BF16)
            t1 = sb(st, "t1", [128, 256])
            t2 = sb(st, "t2", [128, 256])
            for i in range(4):
                tsl = slice(i * 128, (i + 1) * 128)
                for g in range(2):
                    Pa, Pb_ = (PS[2], PS[3]) if g == 0 else (PS[6], PS[7])
                    for hh in range(4):
                        par = hh % 2
                        qc = 2 * g + hh // 2
                        for pcur in range(2):
                            Pd = Pa if hh < 2 else Pb_
                            o0 = ((hh % 2) * 2 + pcur) * 128
                            ks = slice((i + pcur) * 128, (i + pcur + 1) * 128)
                            S.mm(lambda e, Pd=Pd, o0=o0, ks=ks, g=g, par=par, qc=qc: e.matmul(Pd[:, o0:o0 + 128], lhsT=kxm[:, l, 2 * g + par, ks], rhs=qT[qc][:, tsl],
                                                                                              start=True, stop=True), r=[kxm, qT[qc]], w=[Pd], last=(pcur == 1 and hh % 2 == 1))
                    S.op("act", lambda e: e.activation(out=pexp[:, 0:512], in_=Pa[:, :], func=AF.Exp, scale=0.125), r=[Pa], w=[pexp])
                    S.op("act", lambda e: e.activation(out=pexp[:, 512:1024], in_=Pb_[:, :], func=AF.Exp, scale=0.125), r=[Pb_], w=[pexp])
                    S.op("dve", lambda e, g=g: e.tensor_tensor(out=pT[:].rearrange("p a q -> p (a q)"), in0=pexp[:, :], in1=Et[:, g * 1024:(g + 1) * 1024], op=ALU.mult), r=[pexp, Et], w=[pT])
                    PN, PD_ = PS[4], PS[5]
                    skip_prev = first and i == 0
                    for cq in range(2):
                        o_ = slice(cq * 128, (cq + 1) * 128)
                        terms = []
                        for par in range(2):
                            hh = 2 * cq + par
                            terms.append((i + 1, 2 * g + par, hh * 2 + 1, "onesA" if par == 0 else "onesB"))
                            if not skip_prev:
                                terms.append((i, 2 * g + par, hh * 2 + 0, "onesA" if par == 0 else "onesB"))
                        for ti, (vt, vv_, pidx, on) in enumerate(terms):
                            S.mm(lambda e, vt=vt, vv_=vv_, pidx=pidx, o_=o_, ti=ti: e.matmul(PN[:, o_], lhsT=vpad[:, l, vt, vv_, :], rhs=pT[:, pidx, :], start=(ti == 0), stop=(ti == len(terms) - 1)),
                                 r=[vpad, pT], w=[PN], last=(ti == len(terms) - 1))
                        for ti, (vt, vv_, pidx, on) in enumerate(terms):
                            S.mm(lambda e, on=on, pidx=pidx, o_=o_, ti=ti: e.matmul(PD_[:, o_], lhsT=C_(on), rhs=pT[:, pidx, :], start=(ti == 0), stop=(ti == len(terms) - 1)),
                                 r=[cb, pT], w=[PD_], last=(ti == len(terms) - 1))
                    es = esink[:, l * 4 + 2 * g: l * 4 + 2 * g + 2]
                    S.op("dve", lambda e, es=es: e.tensor_tensor(out=t1[:, :].rearrange("p (a q) -> p a q", a=2), in0=PD_[:, 0:256].rearrange("p (a q) -> p a q", a=2),
                                                                 in1=es.unsqueeze(2).to_broadcast([128, 2, 128]), op=ALU.add), r=[PD_, esink], w=[t1])
                    S.op("dve", lambda e: e.reciprocal(out=t1[:, :], in_=t1[:, :]), r=[t1], w=[t1])
                    S.op("dve", lambda e: e.tensor_tensor(out=t2[:, :], in0=PN[:, 0:256], in1=t1[:, :], op=ALU.mult), r=[PN, t1], w=[t2])
                    for cq in range(2):
                        c = 2 * g + cq
                        S.op("pool", lambda e, c=c, cq=cq: e.tensor_tensor(out=yT[:, 4 + c, tsl], in0=t2[:, cq * 128:(cq + 1) * 128], in1=sgB[c][:, tsl], op=ALU.mult),
                             r=[t2, sgB[c]], w=[yT])
            S.op("pool", lambda e: e.tensor_copy(out=kxm[:, l, :, 0:128], in_=kxm[:, l, :, 512:640]), r=[kxm], w=[kxm])
            S.op("dve", lambda e: e.tensor_copy(out=vpad[:, l, 0, :, :], in_=vpad[:, l, 4, :, :]), r=[vpad], w=[vpad])
            rel(st)

        ckpt("B")
        with contextlib.ExitStack() as st:
            zc = sb(st, "zc", [128, 4, 528])
            ta = sb(st, "ta", [128, 528])
            tb_ = sb(st, "tb", [128, 528])
            sgC = [sb(st, "sgC%d" % c, [128, TB], BF16) for c in range(4)]
            pooled = sb(st, "pooled", [128, TB], BF16)
            poolwl = sb(st, "poolwl", [128, 4, 128], BF16)
            S.dma("sp", poolwl[:].rearrange("p g n -> p (g n)"), csc["poolw"][:, l * 512:(l + 1) * 512], r=[csc_b["poolw"]], w=[poolwl])
            if first:
                icn = sb(st, "icn", [128, 4, 512])
                S.dma("sp", icn[:].rearrange("p g t -> p (g t)"), ic_d[:, :], w=[icn])
            wt = wt_next
            wt_next = load_w("in", l, G_GC)
            S.op("pool", lambda e: e.tensor_copy(out=zc[:, :, 0:16], in_=poolh[:, l, :, :]), r=[poolh], w=[zc])
            for g in range(4):
                P = proj_chunk(wt, g)
                S.op("act", lambda e, g=g, P=P: e.copy(out=zc[:, g, 16:528], in_=P[:, :]), r=[P], w=[zc])
            S.op("pool", lambda e: e.tensor_copy(out=poolh[:, l, :, :], in_=zc[:, :, 512:528]), r=[zc], w=[poolh])
            wt = wt_next
            wt_next = load_w("in", l, G_U)
            for c in range(4):
                P = proj_chunk(wt, c)
                S.op("act", lambda e, c=c, P=P: e.activation(out=sgC[c][:, :], in_=P[:, :], func=AF.Silu), r=[P], w=[sgC[c]])
            for g, wdw in enumerate((2, 4, 8, 16)):
                src_t, src = zc, (lambda a, b, g=g: zc[:, g, a:b])
                sh_ = 1
                bufs = [ta, tb_]
                bi_ = 0
                while sh_ < wdw:
                    dst = bufs[bi_ % 2]
                    bi_ += 1
                    lo_ = 2 * sh_ - 1
                    S.op("pool", lambda e, dst=dst, src=src, sh_=sh_, lo_=lo_: e.tensor_tensor(out=dst[:, lo_:528], in0=src(lo_, 528), in1=src(lo_ - sh_, 528 - sh_), op=ALU.add),
                         r=[src_t], w=[dst])
                    src_t, src = dst, (lambda a, b, dst=dst: dst[:, a:b])
                    sh_ *= 2
                if first:
                    S.op("dve", lambda e, g=g, src=src: e.tensor_tensor(out=src(16, 528), in0=src(16, 528), in1=icn[:, g, :], op=ALU.mult), r=[src_t, icn], w=[src_t])
                    S.op("dve", lambda e, g=g, src=src: e.tensor_tensor(out=pooled[:, :], in0=src(16, 528), in1=zc[:, g, 16:528], op=ALU.subtract), r=[src_t, zc], w=[pooled])
                else:
                    S.op("dve", lambda e, g=g, src=src, wdw=wdw: e.scalar_tensor_tensor(out=pooled[:, :], in0=src(16, 528), scalar=1.0 / wdw, in1=zc[:, g, 16:528],
                                                                                       op0=ALU.mult, op1=ALU.subtract), r=[src_t, zc], w=[pooled])
                PA = PS[2 + g % 2]
                S.mm(lambda e, g=g, PA=PA: e.matmul(PA[:, :], lhsT=poolwl[:, g, :], rhs=pooled[:, :], start=True, stop=True), r=[poolwl, pooled], w=[PA])
                S.op("dve", lambda e, g=g, PA=PA: e.scalar_tensor_tensor(out=yT[:, 8 + g, :], in0=PA[:, :], scalar=pc(l, PSC + g), in1=sgC[g][:, :], op0=ALU.mult, op1=ALU.mult),
                     r=[PA, pcol, sgC[g]], w=[yT])
            rel(st)

        ckpt("C")
        with contextlib.ExitStack() as st:
            uT = [sb(st, "uT%d" % c, [128, TB], BF16) for c in range(4)]
            vT = [sb(st, "vT%d" % c, [128, TB]) for c in range(4)]
            vb = [sb(st, "vb%d" % c, [128, TB], BF16) for c in range(4)]
            vq = [sb(st, "vq%d" % c, [128, TB], BF16) for c in range(4)]
            sgD = [sb(st, "sgD%d" % c, [128, TB], BF16) for c in range(4)]
            mean = sb(st, "mean", [128, TB])
            rstd = sb(st, "rstd", [128, TB])
            vtok = sb(st, "vtok", [128, 4, 4, 128], BF16)
            brep = sb(st, "brep", [128, 4, 128])
            sT = sb(st, "sT", [128, TB])
            wsTl = sb(st, "wsTl", [128, 4, 128], BF16)
            S.dma("sp", wsTl[:].rearrange("p g n -> p (g n)"), csc["wsT"][:, l * 512:(l + 1) * 512], r=[csc_b["wsT"]], w=[wsTl])
            S.dma("sp", brep[:].rearrange("p g i -> p (g i)"), sgub_d[l:l + 1, :].partition_broadcast(128), w=[brep])
            wt = wt_next
            wt_next = load_w("in", l, G_VD)
            for c in range(4):
                P = proj_chunk(wt, c)
                S.op("act", lambda e, c=c, P=P: e.activation(out=uT[c][:, :], in_=P[:, :], func=AF.Gelu), r=[P], w=[uT[c]])
            wt = wt_next
            wt_next = load_w("in", l, G_GD)
            for c in range(4):
                P = proj_chunk(wt, c)
                S.op("act", lambda e, c=c, P=P: e.activation(out=vT[c][:, :], in_=P[:, :], func=AF.Gelu), r=[P], w=[vT[c]])
                S.op("dve", lambda e, c=c: e.tensor_copy(out=vb[c][:, :], in_=vT[c][:, :]), r=[vT[c]], w=[vb[c]])
                S.op("pool", lambda e, c=c: e.tensor_tensor(out=vq[c][:, :], in0=vT[c][:, :], in1=vT[c][:, :], op=ALU.mult), r=[vT[c]], w=[vq[c]])
            wt = wt_next
            wt_next = load_w("out", l, 0)
            for c in range(4):
                P = proj_chunk(wt, c)
                S.op("act", lambda e, c=c, P=P: e.activation(out=sgD[c][:, :], in_=P[:, :], func=AF.Silu), r=[P], w=[sgD[c]])
            PM, PQ = PS[2], PS[3]
            for c in range(4):
                S.mm(lambda e, c=c: e.matmul(PM[:, :], lhsT=C_("o512"), rhs=vb[c][:, :], start=(c == 0), stop=(c == 3)), r=[cb, vb[c]], w=[PM], last=(c == 3))
            for c in range(4):
                S.mm(lambda e, c=c: e.matmul(PQ[:, :], lhsT=C_("o512"), rhs=vq[c][:, :], start=(c == 0), stop=(c == 3)), r=[cb, vq[c]], w=[PQ], last=(c == 3))
            S.op("act", lambda e: e.copy(out=mean[:, :], in_=PM[:, :]), r=[PM], w=[mean])
            S.op("pool", lambda e: e.tensor_tensor(out=rstd[:, :], in0=mean[:, :], in1=mean[:, :], op=ALU.mult), r=[mean], w=[rstd])
            S.op("dve", lambda e: e.tensor_tensor(out=rstd[:, :], in0=PQ[:, :], in1=rstd[:, :], op=ALU.subtract), r=[PQ, rstd], w=[rstd])
            S.op("dve", lambda e: e.tensor_scalar(out=rstd[:, :], in0=rstd[:, :], scalar1=LN_EPS, scalar2=None, op0=ALU.add), r=[rstd], w=[rstd])
            S.op("act", lambda e: e.sqrt(out=rstd[:, :], in_=rstd[:, :]), r=[rstd], w=[rstd])
            S.op("dve", lambda e: e.reciprocal(out=rstd[:, :], in_=rstd[:, :]), r=[rstd], w=[rstd])
            for c in range(4):
                S.op("pool", lambda e, c=c: e.tensor_tensor(out=vT[c][:, :], in0=vT[c][:, :], in1=mean[:, :], op=ALU.subtract), r=[vT[c], mean], w=[vT[c]])
                S.op("dve", lambda e, c=c: e.tensor_tensor(out=vb[c][:, :], in0=vT[c][:, :], in1=rstd[:, :], op=ALU.mult), r=[vT[c], rstd], w=[vb[c]])
            for i in range(4):
                Pt = PS[4 + i % 2]
                Ptb = Pt[:, :].bitcast(BF16)
                for c in range(4):
                    S.mm(lambda e, i=i, c=c, Ptb=Ptb: e.transpose(Ptb[:, c * 128:(c + 1) * 128], vb[c][:, i * 128:(i + 1) * 128], C_("ident")), r=[vb[c], cb], w=[Pt], last=(c == 3))
                S.op("act", lambda e, i=i, Ptb=Ptb: e.copy(out=vtok[:, i, :, :].rearrange("p c d -> p (c d)"), in_=Ptb[:, 0:512]), r=[Pt], w=[vtok])
            for g in range(4):
                Pg = PS[2 + g % 2]
                for i in range(4):
                    S.mm(lambda e, g=g, i=i, Pg=Pg: e.matmul(Pg[:, i * 128:(i + 1) * 128], lhsT=vtok[:, i, g, :], rhs=wsTl[:, g, :], start=True, stop=True),
                         r=[vtok, wsTl], w=[Pg], last=(i == 3))
                S.op("dve", lambda e, g=g, Pg=Pg: e.scalar_tensor_tensor(out=sT[:, :].rearrange("p (i t) -> p i t", i=4), in0=Pg[:, :].rearrange("p (i t) -> p i t", i=4),
                                                                        scalar=pc(l, SNW + g), in1=brep[:, g:g + 1, :].to_broadcast([128, 4, 128]), op0=ALU.mult, op1=ALU.add),
                     r=[Pg, pcol, brep], w=[sT])
                S.op("pool", lambda e, g=g: e.tensor_tensor(out=sT[:, :], in0=sT[:, :], in1=uT[g][:, :], op=ALU.mult), r=[sT, uT[g]], w=[sT])
                S.op("dve", lambda e, g=g: e.tensor_tensor(out=yT[:, 12 + g, :], in0=sT[:, :], in1=sgD[g][:, :], op=ALU.mult), r=[sT, sgD[g]], w=[yT])
            rel(st)

        ckpt("D")
        with contextlib.ExitStack() as st:
            o = sb(st, "o", [128, 4, D])
            postw = sb(st, "postw", [128, D])
            junk = sb(st, "junk", [128, 512], BF16)
            ssq = sb(st, "ssq", [128, 16])
            sm = sb(st, "smo", [128, 32])
            S.dma("sp", postw[:], postw_d[l:l + 1, :].partition_broadcast(128), w=[postw])
            for mb in range(4):
                wt = wt_next
                if mb < 3:
                    wt_next = load_w("out", l, mb + 1)
                elif not (bi == n_blocks - 1 and l == n_layers - 1):
                    x_pref[0] = load_w("in", (l + 1) if l + 1 < n_layers else 0, G_X)
                for i in range(4):
                    P = PS[pj_rr[0] % 2]
                    pj_rr[0] += 1
                    for k in range(16):
                        S.mm(lambda e, k=k, P=P, i=i, wt=wt: e.matmul(P[:, :], lhsT=yT[:, k, i * 128:(i + 1) * 128], rhs=wt[:, k, :], start=(k == 0), stop=(k == 15)),
                             r=[yT, wt], w=[P], last=(k == 15))
                    S.op("act", lambda e, i=i, mb=mb, P=P: e.copy(out=o[:, i, mb * 512:(mb + 1) * 512], in_=P[:, :]), r=[P], w=[o])
                    S.op("act", lambda e, i=i, mb=mb: e.activation(out=junk[:, :], in_=o[:, i, mb * 512:(mb + 1) * 512], func=AF.Square, accum_out=ssq[:, i * 4 + mb:i * 4 + mb + 1]),
                         r=[o], w=[junk, ssq])
            for i in range(4):
                S.op("dve", lambda e, i=i: e.tensor_reduce(out=sm[:, i * 8:i * 8 + 1], in_=ssq[:, i * 4:(i + 1) * 4], axis=AX.X, op=ALU.add), r=[ssq], w=[sm])
                S.op("dve", lambda e, i=i: e.tensor_scalar(out=sm[:, i * 8 + 1:i * 8 + 2], in0=sm[:, i * 8:i * 8 + 1], scalar1=1.0 / D, scalar2=NORM_EPS, op0=ALU.mult, op1=ALU.add), r=[sm], w=[sm])
                S.op("act", lambda e, i=i: e.sqrt(out=sm[:, i * 8 + 2:i * 8 + 3], in_=sm[:, i * 8 + 1:i * 8 + 2]), r=[sm], w=[sm])
                S.op("dve", lambda e, i=i: e.reciprocal(out=sm[:, i * 8 + 3:i * 8 + 4], in_=sm[:, i * 8 + 2:i * 8 + 3]), r=[sm], w=[sm])
                S.op("dve", lambda e, i=i: e.scalar_tensor_tensor(out=o[:, i, :], in0=o[:, i, :], scalar=sm[:, i * 8 + 3:i * 8 + 4], in1=postw[:, :], op0=ALU.mult, op1=ALU.mult), r=[o, sm, postw], w=[o])
                S.op("dve", lambda e, i=i: e.tensor_tensor(out=xb[i][:, :], in0=xb[i][:, :], in1=o[:, i, :], op=ALU.add), r=[xb[i], o], w=[xb[i]])
                if last_layer:
                    S.dma("sp", out_d[bi * TB + i * 128: bi * TB + (i + 1) * 128, :], xb[i][:, :], r=[xb[i]])
            rel(st)

    try:
        for bi in range(n_blocks):
            for l in range(n_layers):
                layer_block(bi, l, l == n_layers - 1)
    except _Stop:
        S.barrier()
        S.emit()
        return nc
    S.barrier()
    S.emit()
    stack.close()
    return nc


_CACHE = {}


def host_layout(inputs):
    f = lambda a: np.ascontiguousarray(np.asarray(a, dtype=np.float32))
    L = 2
    shared = {}
    shared["w_in_p"] = f(np.asarray(inputs["w_in"])[:, :, perm_index()])
    shared["w_out"] = f(inputs["w_out"])
    pcol = np.zeros((128, L * NPC), np.float32)
    for l in range(L):
        cols = []
        cols.append(np.asarray(inputs["pre_norm_w"])[l].reshape(16, 128).T)
        cols.append(np.asarray(inputs["shift_mu"])[l].reshape(13, 128).T)
        for nm in ("rwkv_w0", "rwkv_a0", "rwkv_k_k", "rwkv_k_a"):
            cols.append(np.asarray(inputs[nm])[l].reshape(4, 128).T)
        cols.append(np.asarray(inputs["rwkv_r_k"])[l].reshape(4, 128).T)
        cols.append(np.asarray(inputs["pool_scale"])[l].reshape(4, 128).T)
        cols.append(np.asarray(inputs["sgu_norm_w"])[l].reshape(4, 128).T)
        cols.append(np.repeat(np.asarray(inputs["attn_sinks"])[l], 64).reshape(4, 128).T)
        pc_ = np.concatenate(cols, 1)
        assert pc_.shape[1] == NPC
        pcol[:, l * NPC:(l + 1) * NPC] = pc_
    shared["pcol"] = pcol
    shared["post_norm_w"] = f(inputs["post_norm_w"])
    shared["pre_norm_w"] = f(inputs["pre_norm_w"])
    shared["rwkv_ln_w"] = f(inputs["rwkv_ln_w"])
    shared["rwkv_ln_b"] = f(inputs["rwkv_ln_b"])
    shared["sgu_b"] = f(np.asarray(inputs["sgu_b"]).reshape(L, 512))
    shared["rwkv_w_up"] = f(inputs["rwkv_w_up"])
    shared["rwkv_a_up"] = f(inputs["rwkv_a_up"])
    shared["pool_w"] = f(inputs["pool_w"])
    shared["sgu_wT"] = f(np.asarray(inputs["sgu_w"]).transpose(0, 1, 3, 2))
    cbf, cf32, ic = make_consts()
    shared["cbf"] = cbf
    shared["cf32"] = cf32
    shared["invcnt"] = ic
    return shared


def kernel(**inputs):
    x = np.asarray(inputs["x"], dtype=np.float32)
    shared = host_layout(inputs)
    if "nc" not in _CACHE:
        _CACHE["nc"] = build()
    nc = _CACHE["nc"]
    in_maps = []
    for c in range(8):
        m = dict(shared)
        m["x"] = np.ascontiguousarray(x[c % 4])
        in_maps.append(m)
    res = run_bass_kernel_spmd(nc, in_maps, core_ids=list(range(8)))
    out = np.stack([res.results[c]["out"] for c in range(4)], 0)
    return out.astype(np.float32)
```

```python
import contextlib
import types
import numpy as np
import ml_dtypes
import concourse.bass as bass
import concourse.mybir as mybir
from concourse.bass_utils import run_bass_kernel_spmd

F32 = mybir.dt.float32
BF16 = mybir.dt.bfloat16
AF = mybir.ActivationFunctionType
ALU = mybir.AluOpType
AX = mybir.AxisListType

D = 2048
SEQ = 2048
TB = 512
NCOL = 6144
C0 = -0.6065306597126334
GN_EPS = 64e-5
STAGGER = 0
NFILL = 0
LN_EPS = 1e-5
NORM_EPS = 1e-6

G_X, G_A0, G_Q, G_GB, G_C, G_GC, G_U, G_VD, G_GD = 0, 1, 5, 6, 7, 8, 9, 10, 11


def perm_index():
    idx = []
    rng = lambda a, n: list(range(a, a + n))
    idx += rng(1536, 128)
    idx += rng(2176, 64) + rng(2176, 64)
    idx += rng(2240, 64) + rng(2240, 64)
    idx += rng(2304, 128)
    for p in range(4):
        idx += rng(128 * p, 128) + rng(512 + 128 * p, 128) + rng(1024 + 128 * p, 128) + rng(3968 + 128 * p, 128)
    idx += rng(1664, 512)
    idx += rng(3968 + 512, 512)
    idx += rng(2432, 512)
    idx += rng(3968 + 1024, 512)
    idx += rng(2944, 512)
    idx += rng(3456, 512)
    idx += rng(3968 + 1536, 512)
    assert len(idx) == NCOL
    return np.array(idx)


CB = {}
_o = 0
for _n, _w in [("ident", 128), ("sl4", 512), ("suui2", 512), ("bd2", 256), ("bones", 128),
               ("onesA", 128), ("onesB", 128), ("o512", 128), ("hsel", 2), ("E0", 1024), ("E1", 1024), ("ui4", 512)]:
    CB[_n] = (_o, _o + _w)
    _o += _w
NCB = _o
NCBS = CB["E0"][0]
NPC = 61


def make_consts():
    cb = np.zeros((128, NCB), np.float32)
    p = np.arange(128)[:, None]
    f = np.arange(128)[None, :]
    cb[:, slice(*CB["ident"])] = (p == f)
    cb[:, slice(*CB["sl4"])] = np.tile((f < p), (1, 4))
    cb[:, slice(*CB["suui2"])] = np.tile(np.concatenate([(p < f), (p <= f)], 1), (1, 2))
    cb[:, slice(*CB["ui4"])] = np.tile((p <= f), (1, 4))
    cb[:, slice(*CB["bd2"])] = np.tile(((p // 64) == (f // 64)), (1, 2))
    cb[:, slice(*CB["bones"])] = ((p // 64) == (f // 64))
    slopes = 2.0 ** (-(np.arange(8) + 1.0))
    for g in range(2):
        E = np.zeros((128, 8, 128), np.float64)
        for hh in range(4):
            sl = slopes[4 * g + hh]
            dist_cur = (f - p).astype(np.float64)
            E[:, hh * 2 + 1, :] = np.where(dist_cur >= 0, np.exp(-sl * np.maximum(dist_cur, 0)), 0.0)
            dist_prev = dist_cur + 128
            E[:, hh * 2 + 0, :] = np.where(dist_prev < 128, np.exp(-sl * dist_prev), 0.0)
        cb[:, slice(*CB["E%d" % g])] = E.reshape(128, 1024)
    cb[:, CB["onesA"][0]:CB["onesA"][0] + 64] = 1.0
    cb[:, CB["onesB"][0] + 64:CB["onesB"][1]] = 1.0
    cb[:, slice(*CB["o512"])] = 1.0 / 512.0
    cb[0:64, CB["hsel"][0]] = 1.0
    cb[64:128, CB["hsel"][0] + 1] = 1.0
    cf = np.zeros((128, 512 + 2), np.float32)
    rm = np.ones(512, np.float32)
    rm[::128] = 0.0
    cf[:, 0:512] = rm[None]
    cf[0:64, 512] = 1.0
    cf[64:128, 513] = 1.0
    pos = np.arange(512)
    ic = np.zeros((128, 4, 512), np.float32)
    for g, w in enumerate((2, 4, 8, 16)):
        ic[:, g, :] = (1.0 / np.minimum(pos + 1, w))[None]
    return cb.astype(ml_dtypes.bfloat16), cf, ic.reshape(128, 2048)


def freeze(fn):
    if fn.__closure__ is None:
        return fn
    cells = []
    for c in fn.__closure__:
        try:
            cells.append(types.CellType(c.cell_contents))
        except ValueError:
            cells.append(c)
    g = types.FunctionType(fn.__code__, fn.__globals__, fn.__name__, fn.__defaults__, tuple(cells))
    g.__kwdefaults__ = fn.__kwdefaults__
    return g


class Buf:
    __slots__ = ("name", "w", "r")

    def __init__(self, name):
        self.name = name
        self.w = None
        self.r = []


class Tl:
    def __init__(self, t, name, psum=False):
        self.t = t
        self.b = Buf(name)
        self.psum = psum

    def __getitem__(self, k):
        return self.t[k]


class Sched:
    ENG = ["pe", "act", "dve", "pool", "sp"]

    def __init__(self, nc, stack, n_dma_sems=16):
        self.nc = nc
        self.stack = stack
        self.sem = {}
        self.cnt = {}
        for k in ["pe", "act", "dve", "pool"]:
            self.sem[k] = stack.enter_context(nc.semaphore("s_" + k))
            self.cnt[k] = 0
        self.dma_sems = []
        for i in range(n_dma_sems):
            key = "dma%d" % i
            self.sem[key] = stack.enter_context(nc.semaphore("s_" + key))
            self.cnt[key] = 0
            self.dma_sems.append(key)
        self.dma_rr = 0
        self.seen = {}
        self.prog = {e: [] for e in self.ENG}
        self.uid = 0
        self.pending = {}
        self.scopes = {}
        self.sw_sems = []

    def _wait(self, e, tok):
        if tok is None:
            return
        k, v = tok
        if e == "pe" and k == "pe":
            return
        if self.seen.get((e, k), 0) >= v:
            return
        self.prog[e].append(("wait", k, v))
        self.seen[(e, k)] = v

    def deps(self, e, reads, writes):
        for t in reads:
            self._wait(e, t.b.w)
            if t.psum:
                for tok in t.b.r:
                    if tok[0] != e:
                        self._wait(e, tok)
        for t in writes:
            self._wait(e, t.b.w)
            for tok in t.b.r:
                self._wait(e, tok)

    def commit(self, tok, reads, writes):
        for t in reads:
            if t.psum:
                t.b.r = [tok]
            else:
                t.b.r.append(tok)
        for t in writes:
            t.b.w = tok
            t.b.r = []

    def op(self, e, fn, r=(), w=()):
        self.deps(e, r, w)
        fn = freeze(fn)
        self.cnt[e] += 1
        self.prog[e].append(("ins", fn, e, 1))
        tok = (e, self.cnt[e])
        self.commit(tok, r, w)
        return tok

    def mm(self, fn, r=(), w=(), last=True):
        e = "pe"
        self.deps(e, r, w)
        fn = freeze(fn)
        if last:
            self.cnt[e] += 1
            self.prog[e].append(("ins", fn, e, 1))
            tok = (e, self.cnt[e])
        else:
            self.prog[e].append(("ins", fn, None, 0))
            tok = (e, self.cnt[e] + 1)
        self.commit(tok, r, w)
        return tok

    def dma(self, q, out, in_, r=(), w=(), fresh=False, **kw):
        if fresh:
            key = "swd%d" % len(self.sem)
            self.sem[key] = self.stack.enter_context(self.nc.semaphore("s_" + key))
            self.cnt[key] = 0
            self.sw_sems.append(key)
        else:
            key = self.dma_sems[self.dma_rr % len(self.dma_sems)]
            self.dma_rr += 1
        self._wait(q, (key, self.cnt[key]) if self.cnt[key] else None)
        self.deps(q, r, w)
        fn = lambda eng, out=out, in_=in_, kw=kw: eng.dma_start(out=out, in_=in_, **kw)
        self.cnt[key] += 16
        self.prog[q].append(("ins", fn, key, 16))
        tok = (key, self.cnt[key])
        self.commit(tok, r, w)
        return tok

    def release(self, tiles):
        for t in tiles:
            for tok in ([t.b.w] if t.b.w else []) + list(t.b.r):
                k, v = tok
                if self.pending.get(k, 0) < v:
                    self.pending[k] = v

    def barrier(self):
        for e in self.ENG:
            for k in ["pe", "act", "dve", "pool"] + self.dma_sems + self.sw_sems:
                if self.cnt[k]:
                    self._wait(e, (k, self.cnt[k]))

    def emit(self):
        nc = self.nc

        def replay(name):
            def run(eng):
                for it in self.prog[name]:
                    if it[0] == "wait":
                        eng.wait_ge(self.sem[it[1]], it[2])
                    else:
                        ins = it[1](eng)
                        if it[2] is not None:
                            ins.then_inc(self.sem[it[2]], it[3])
            return run

        with nc.Block() as block:
            block.tensor(replay("pe"))
            block.scalar(replay("act"))
            block.vector(replay("dve"))
            block.gpsimd(replay("pool"))
            block.sync(replay("sp"))


class _Stop(Exception):
    pass


def build(n_blocks=4, n_layers=2, dbg=None, stop_after=None):
    nc = bass.Bass("TRN2", target_bir_lowering=False)
    stack = contextlib.ExitStack()
    S = Sched(nc, stack)
    L = 2

    def din(name, shape, dt=F32):
        return nc.dram_tensor(name, list(shape), dt, kind="ExternalInput").ap()

    x_d = din("x", [SEQ, D])
    win_d = din("w_in_p", [L, D, NCOL])
    wout_d = din("w_out", [L, D, D])
    pcol_d = din("pcol", [128, L * NPC])
    postw_d = din("post_norm_w", [L, D])
    prew_d = din("pre_norm_w", [L, D])
    lnw_d = din("rwkv_ln_w", [L, 512])
    lnb_d = din("rwkv_ln_b", [L, 512])
    sgub_d = din("sgu_b", [L, 512])
    wup_d = din("rwkv_w_up", [L, 64, 512])
    aup_d = din("rwkv_a_up", [L, 64, 512])
    poolw_d = din("pool_w", [L, 4, 128, 128])
    sguwT_d = din("sgu_wT", [L, 4, 128, 128])
    cb_d = din("cbf", [128, NCB], BF16)
    cf_d = din("cf32", [128, 514])
    ic_d = din("invcnt", [128, 2048])
    out_d = nc.dram_tensor("out", [SEQ, D], F32, kind="ExternalOutput").ap()
    dbg_d = {}
    if dbg:
        for nm, shp in dbg.items():
            dbg_d[nm] = nc.dram_tensor("dbg_" + nm, list(shp), F32, kind="ExternalOutput").ap()
    wsc_in = nc.dram_tensor("wsc_in", [L, 12, 128, 16 * 512], BF16).ap()
    wsc_out = nc.dram_tensor("wsc_out", [L, 4, 128, 16 * 512], BF16).ap()
    wsc_in_b = [[Tl(None, "wsi%d_%d" % (l, g)) for g in range(12)] for l in range(L)]
    wsc_out_b = [[Tl(None, "wso%d_%d" % (l, g)) for g in range(4)] for l in range(L)]

    def sb(stk, name, shape, dt=F32):
        S.uid += 1
        nm = "%s_%d" % (name, S.uid)
        t = Tl(stk.enter_context(nc.sbuf_tensor(nm, list(shape), dt)), nm)
        t.b.r = list(S.pending.items())
        S.scopes.setdefault(id(stk), []).append(t)
        return t

    def rel(stk):
        S.release(S.scopes.pop(id(stk), []))

    def ps(stk, name, shape, dt=F32):
        S.uid += 1
        nm = "%s_%d" % (name, S.uid)
        return Tl(stk.enter_context(nc.psum_tensor(nm, list(shape), dt)), nm, psum=True)

    def ckpt(name):
        if stop_after == name:
            raise _Stop()

    top = stack
    xb = [sb(top, "xblk%d" % i, [128, D]) for i in range(4)]
    hT = sb(top, "hT", [128, 16, TB], BF16)
    yT = sb(top, "yT", [128, 16, TB], BF16)
    wbuf = [sb(top, "wbuf%d" % i, [128, 16, 512], BF16) for i in range(2)]
    cb = sb(top, "cb", [128, NCBS], BF16)
    cf = sb(top, "cf", [128, 514])
    pcol = sb(top, "pcol", [128, L * NPC])
    omka = sb(top, "omka", [128, L * 4])
    esink = sb(top, "esink", [128, L * 4])
    csc = {nm: nc.dram_tensor("csc_" + nm, [128, L * 512], BF16).ap() for nm in ("wup", "aup", "poolw", "wsT")}
    csc_b = {nm: Tl(None, "csc_" + nm) for nm in csc}
    Hc = sb(top, "Hc", [128, L, 4, 128])
    carryA = sb(top, "carryA", [128, L, 13])
    kxm = sb(top, "kxm", [128, L, 4, 640], BF16)
    vpad = sb(top, "vpad", [128, L, 5, 4, 128], BF16)
    poolh = sb(top, "poolh", [128, L, 4, 16])
    qTp = [sb(top, "qTp%d" % c, [128, TB], BF16) for c in range(4)]
    PS = [ps(top, "ps%d" % i, [128, 512]) for i in range(8)]

    def C_(name, a=None, b=None):
        lo, hi = CB[name]
        if a is None:
            return cb[:, lo:hi]
        return cb[:, lo + a:lo + b]

    def pc(l, off, n=1):
        return pcol[:, l * NPC + off: l * NPC + off + n]

    PRE, MU, W0, A0, KK, KA, RK, PSC, SNW, SNK = 0, 16, 29, 33, 37, 41, 45, 49, 53, 57

    eng_rr = [0]

    def ew(fn, r, w, engines=("dve", "pool")):
        e = engines[eng_rr[0] % len(engines)]
        eng_rr[0] += 1
        return S.op(e, fn, r, w)

    S.dma("sp", cb[:], cb_d[:, 0:NCBS], w=[cb])
    S.dma("sp", cf[:], cf_d[:, :], w=[cf])
    S.dma("sp", pcol[:], pcol_d[:, :], w=[pcol])
    for t in (Hc, carryA, kxm, vpad, poolh):
        S.op("pool", lambda e, t=t: e.memset(t[:], 0.0), w=[t])
    with contextlib.ExitStack() as st:
        wup = sb(st, "wup", [128, L, 512], BF16)
        aup = sb(st, "aup", [128, L, 512], BF16)
        poolw = sb(st, "poolw", [128, L, 4, 128], BF16)
        wsT = sb(st, "wsT", [128, L, 4, 128], BF16)
        stg = sb(st, "stg", [128, L, 512])
        stg2 = sb(st, "stg2", [128, L, 512])
        S.op("pool", lambda e: e.memset(stg[:], 0.0), w=[stg])
        S.op("pool", lambda e: e.memset(stg2[:], 0.0), w=[stg2])
        for l in range(L):
            S.dma("sp", stg[0:64, l, :], wup_d[l, :, :], w=[stg])
            S.dma("sp", stg2[64:128, l, :], aup_d[l, :, :], w=[stg2])
        S.op("dve", lambda e: e.tensor_copy(out=wup[:], in_=stg[:]), r=[stg], w=[wup])
        S.op("dve", lambda e: e.tensor_copy(out=aup[:], in_=stg2[:]), r=[stg2], w=[aup])
        stg3 = sb(st, "stg3", [128, L, 4, 128])
        stg4 = sb(st, "stg4", [128, L, 4, 128])
        for l in range(L):
            S.dma("sp", stg3[:, l, :, :], poolw_d[l].rearrange("g c d -> c g d"), w=[stg3])
            S.dma("sp", stg4[:, l, :, :], sguwT_d[l].rearrange("g j i -> j g i"), w=[stg4])
        S.op("dve", lambda e: e.tensor_copy(out=poolw[:], in_=stg3[:]), r=[stg3], w=[poolw])
        ui4t = sb(st, "ui4t", [128, 512], BF16)
        S.dma("sp", ui4t[:], cb_d[:, CB["ui4"][0]:CB["ui4"][1]], w=[ui4t])
        for l in range(L):
            S.op("dve", lambda e, l=l: e.tensor_tensor(out=wsT[:, l, :, :].rearrange("p g i -> p (g i)"),
                                                       in0=stg4[:, l, :, :].rearrange("p g i -> p (g i)"),
                                                       in1=ui4t[:, :], op=ALU.mult), r=[stg4, ui4t], w=[wsT])
        S.dma("sp", csc["wup"], wup[:].rearrange("p l n -> p (l n)"), r=[wup], w=[csc_b["wup"]])
        S.dma("sp", csc["aup"], aup[:].rearrange("p l n -> p (l n)"), r=[aup], w=[csc_b["aup"]])
        S.dma("sp", csc["poolw"], poolw[:].rearrange("p l g n -> p (l g n)"), r=[poolw], w=[csc_b["poolw"]])
        S.dma("sp", csc["wsT"], wsT[:].rearrange("p l g n -> p (l g n)"), r=[wsT], w=[csc_b["wsT"]])
        for l in range(L):
            S.op("dve", lambda e, l=l: e.tensor_scalar(out=omka[:, l * 4:(l + 1) * 4], in0=pc(l, KA, 4), scalar1=-1.0,
                                                       scalar2=1.0, op0=ALU.mult, op1=ALU.add), r=[pcol], w=[omka])
            S.op("act", lambda e, l=l: e.activation(out=esink[:, l * 4:(l + 1) * 4], in_=pc(l, SNK, 4), func=AF.Exp),
                 r=[pcol], w=[esink])
        rel(st)

    try:
        ckpt("setup")
    except _Stop:
        S.barrier(); S.emit(); stack.close(); return nc
    def prepass(layers):
        HW = NCOL // 2
        with contextlib.ExitStack() as st:
            f32s = [sb(st, "pf%d" % i, [128, HW]) for i in range(2)]
            b16s = [sb(st, "pb%d" % i, [128, HW], BF16) for i in range(2)]
            it = 0
            for l in layers:
                for k in range(16):
                    for hf in range(2):
                        f, b_ = f32s[it % 2], b16s[it % 2]
                        it += 1
                        S.dma("sp", f[:], win_d[l, k * 128:(k + 1) * 128, hf * HW:(hf + 1) * HW], w=[f])
                        S.op("dve", lambda e, f=f, b_=b_: e.tensor_copy(out=b_[:, 0:1024], in_=f[:, 0:1024]), r=[f], w=[b_])
                        S.op("pool", lambda e, f=f, b_=b_: e.tensor_copy(out=b_[:, 1024:2048], in_=f[:, 1024:2048]), r=[f], w=[b_])
                        S.op("act", lambda e, f=f, b_=b_: e.copy(out=b_[:, 2048:HW], in_=f[:, 2048:HW]), r=[f], w=[b_])
                        S.dma("sp", wsc_in[l, hf * 6:(hf + 1) * 6, :, k * 512:(k + 1) * 512].rearrange("g p n -> p g n"),
                              b_[:].rearrange("p (g n) -> p g n", g=6), r=[b_], w=wsc_in_b[l][hf * 6:(hf + 1) * 6])
                for k in range(16):
                    f, b_ = f32s[it % 2], b16s[it % 2]
                    it += 1
                    S.dma("sp", f[:, 0:D], wout_d[l, k * 128:(k + 1) * 128, :], w=[f])
                    S.op("dve", lambda e, f=f, b_=b_: e.tensor_copy(out=b_[:, 0:1024], in_=f[:, 0:1024]), r=[f], w=[b_])
                    S.op("pool", lambda e, f=f, b_=b_: e.tensor_copy(out=b_[:, 1024:2048], in_=f[:, 1024:2048]), r=[f], w=[b_])
                    S.dma("sp", wsc_out[l, :, :, k * 512:(k + 1) * 512].rearrange("g p n -> p g n"),
                          b_[:, 0:D].rearrange("p (g n) -> p g n", g=4), r=[b_], w=wsc_out_b[l])
            S.barrier()

    try:
        ckpt("prepass")
    except _Stop:
        S.barrier(); S.emit(); stack.close(); return nc

    wslot = [0]

    converted = set()

    def load_w(kind, l, g):
        t = wbuf[wslot[0] % 2]
        wslot[0] += 1
        tl = (wsc_in_b if kind == "in" else wsc_out_b)[l][g]
        scr = (wsc_in if kind == "in" else wsc_out)[l, g, :, :]
        if (kind, l, g) not in converted:
            converted.add((kind, l, g))
            srcw = win_d if kind == "in" else wout_d
            src = srcw[l].rearrange("(k p) n -> p k n", p=128)[:, :, g * 512:(g + 1) * 512]
            S.dma("pool", t[:], src, w=[t], fresh=True)
            S.dma("sp", scr, t[:].rearrange("p k n -> p (k n)"), r=[t], w=[tl])
        else:
            S.dma("sp", t[:].rearrange("p k n -> p (k n)"), scr, r=[tl], w=[t])
        return t

    pj_rr = [0]

    def proj_chunk(wt, c):
        P = PS[pj_rr[0] % 2]
        pj_rr[0] += 1
        for k in range(16):
            S.mm(lambda e, k=k, P=P: e.matmul(P[:, :], lhsT=wt[:, k, c * 128:(c + 1) * 128], rhs=hT[:, k, :],
                                             start=(k == 0), stop=(k == 15)), r=[wt, hT], w=[P], last=(k == 15))
        return P

    def dump(name, tile_ap, tl):
        if name in dbg_d:
            S.dma("sp", dbg_d[name], tile_ap, r=[tl])

    wt_next = None
    x_pref = [None]
    fill_rr = [0]

    def layer_block(bi, l, last_layer):
        nonlocal wt_next
        first = (bi == 0)
        with contextlib.ExitStack() as st:
            prep = sb(st, "prep", [128, D])
            xn = sb(st, "xn", [128, 2, D], BF16)
            sm = sb(st, "sm", [128, 32])
            S.dma("sp", prep[:], prew_d[l:l + 1, :].partition_broadcast(128), w=[prep])
            for i in range(4):
                if l == 0:
                    S.dma("sp", xb[i][:, :], x_d[bi * TB + i * 128: bi * TB + (i + 1) * 128, :], w=[xb[i]])
                xv = xn[:, i % 2, :]
                S.op("act", lambda e, i=i, xv=xv: e.activation(out=xv, in_=xb[i][:, :], func=AF.Square,
                                                               accum_out=sm[:, i * 8:i * 8 + 1]), r=[xb[i]], w=[xn, sm])
                S.op("dve", lambda e, i=i: e.tensor_scalar(out=sm[:, i * 8 + 1:i * 8 + 2], in0=sm[:, i * 8:i * 8 + 1], scalar1=1.0 / D, scalar2=NORM_EPS,
                                                      op0=ALU.mult, op1=ALU.add), r=[sm], w=[sm])
                S.op("act", lambda e, i=i: e.sqrt(out=sm[:, i * 8 + 2:i * 8 + 3], in_=sm[:, i * 8 + 1:i * 8 + 2]), r=[sm], w=[sm])
                S.op("dve", lambda e, i=i: e.reciprocal(out=sm[:, i * 8 + 3:i * 8 + 4], in_=sm[:, i * 8 + 2:i * 8 + 3]), r=[sm], w=[sm])
                S.op("dve", lambda e, i=i, xv=xv: e.scalar_tensor_tensor(out=xv, in0=xb[i][:, :], scalar=sm[:, i * 8 + 3:i * 8 + 4],
                                                                         in1=prep[:], op0=ALU.mult, op1=ALU.mult),
                     r=[xb[i], sm, prep], w=[xn])
                for half in range(2):
                    P = PS[2 + half + 2 * (i % 2)]
                    Pb = P[:, :].bitcast(BF16)
                    for kk in range(8):
                        k = half * 8 + kk
                        S.mm(lambda e, k=k, kk=kk, Pb=Pb, xv=xv: e.transpose(Pb[:, kk * 128:(kk + 1) * 128],
                                                                          xv[:, k * 128:(k + 1) * 128], C_("ident")),
                             r=[xn, cb], w=[P], last=(kk == 7))
                    S.op("act", lambda e, i=i, half=half, Pb=Pb: e.copy(
                        out=hT[:, half * 8:(half + 1) * 8, i * 128:(i + 1) * 128],
                        in_=Pb.rearrange("p (k t) -> p k t", k=8)), r=[P], w=[hT])
            rel(st)
        ckpt("N")

        with contextlib.ExitStack() as st:
            HRT = []
            for hr in range(2):
                sh = st
                lnw = sb(sh, "lnw", [128, 256])
                lnb = sb(sh, "lnb", [128, 256])
                AR = [sb(sh, "AR%d" % i, [128, 4, 2, 128], BF16) for i in range(2)]
                BT = [sb(sh, "BT%d" % i, [128, TB], BF16) for i in range(2)]
                KT = [sb(sh, "KT%d" % i, [128, TB], BF16) for i in range(2)]
                rkp = [sb(sh, "rkp%d" % i, [128, TB], BF16) for i in range(2)]
                vSb = [sb(sh, "vSb%d" % i, [128, TB], BF16) for i in range(2)]
                sgA = [sb(sh, "sgA%d" % i, [128, TB], BF16) for i in range(2)]
                rho = [sb(sh, "rho%d" % i, [128, 4]) for i in range(2)]
                sC = [sb(sh, "sC%d" % i, [128, 4]) for i in range(2)]
                HRT.append((lnw, lnb, AR, BT, KT, rkp, vSb, sgA, rho, sC))
            with contextlib.ExitStack() as pre:
                wupl = sb(pre, "wupl", [128, 512], BF16)
                aupl = sb(pre, "aupl", [128, 512], BF16)
                S.dma("sp", wupl[:], csc["wup"][:, l * 512:(l + 1) * 512], r=[csc_b["wup"]], w=[wupl])
                S.dma("sp", aupl[:], csc["aup"][:, l * 512:(l + 1) * 512], r=[csc_b["aup"]], w=[aupl])
                lora_t = sb(pre, "lora", [128, TB], BF16)
                zs = [sb(pre, "zs%d" % i, [128, TB]) for i in range(2)]
                zs_rr = [0]
                zds = [sb(pre, "zd%d" % i, [128, TB]) for i in range(2)]

                zd0s = [sb(pre, "zdc%d" % i, [128, 2]) for i in range(2)]

                def shifted(P, ci, dst_ap, dst_tl, eng2="dve"):
                    z = zs[zs_rr[0] % 2]
                    zd = zds[zs_rr[0] % 2]
                    zd0 = zd0s[zs_rr[0] % 2]
                    zs_rr[0] += 1
                    S.op("act", lambda e: e.copy(out=z[:, 0:TB], in_=P[:, :]), r=[P], w=[z])
                    S.op("dve", lambda e: e.tensor_tensor(out=zd[:, 1:TB], in0=z[:, 0:TB - 1], in1=z[:, 1:TB], op=ALU.subtract), r=[z], w=[zd])
                    S.op("dve", lambda e: e.scalar_tensor_tensor(out=dst_ap[:, 1:TB], in0=zd[:, 1:TB], scalar=pc(l, MU + ci), in1=z[:, 1:TB],
                                                                 op0=ALU.mult, op1=ALU.add), r=[z, zd, pcol], w=[dst_tl])
                    S.op("pool", lambda e: e.tensor_tensor(out=zd0[:, 0:1], in0=carryA[:, l, ci:ci + 1], in1=z[:, 0:1], op=ALU.subtract), r=[carryA, z], w=[zd0])
                    S.op("pool", lambda e: e.tensor_copy(out=carryA[:, l, ci:ci + 1], in_=z[:, TB - 1:TB]), r=[z, zd0], w=[carryA])
                    S.op("pool", lambda e: e.tensor_scalar(out=dst_ap[:, 0:1], in0=zd0[:, 0:1], scalar1=pc(l, MU + ci), scalar2=z[:, 0:1],
                                                           op0=ALU.mult, op1=ALU.add), r=[zd0, z, pcol], w=[dst_tl])

                NT = 6
                tmp = [sb(pre, "rt%d" % i, [128, TB]) for i in range(NT)]
                wt = x_pref[0] if x_pref[0] is not None else load_w("in", l, G_X)
                x_pref[0] = None
                wt_next = load_w("in", l, G_A0)
                P = proj_chunk(wt, 0)
                lraw = tmp[0]
                shifted(P, 12, lraw[:, :], lraw)
                S.op("act", lambda e: e.activation(out=lora_t[0:64, :], in_=lraw[0:64, :], func=AF.Tanh), r=[lraw], w=[lora_t])
                S.op("dve", lambda e: e.tensor_copy(out=lora_t[64:128, :], in_=lraw[64:128, :]), r=[lraw], w=[lora_t])
                for g in range(2):
                    P = proj_chunk(wt, 1 + g)
                    S.op("act", lambda e, g=g, P=P: e.activation(out=kxm[:, l, 2 * g, 128:640], in_=P[:, :], func=AF.Identity,
                                                                 scale=cf[:, 512:513]), r=[P, cf], w=[kxm])
                    S.op("dve", lambda e, g=g, P=P: e.tensor_scalar(out=kxm[:, l, 2 * g + 1, 128:640], in0=P[:, :],
                                                                    scalar1=cf[:, 513:514], scalar2=None, op0=ALU.mult),
                         r=[P, cf], w=[kxm])
                P = proj_chunk(wt, 3)
                vvT = sb(pre, "vvT", [128, TB], BF16)
                S.op("act", lambda e, P=P: e.copy(out=vvT[:, :], in_=P[:, :]), r=[P], w=[vvT])
                Pt = PS[2]
                Ptb = Pt[:, :].bitcast(BF16)
                for i in range(4):
                    S.mm(lambda e, i=i: e.transpose(Ptb[:, i * 128:(i + 1) * 128], vvT[:, i * 128:(i + 1) * 128], C_("ident")),
                         r=[vvT, cb], w=[Pt], last=(i == 3))
                Ptv = Ptb[:, 0:512].rearrange("p (i c) -> p i c", i=4)
                for vi, (src0, dst0) in enumerate([(0, 0), (0, 64), (64, 0), (64, 64)]):
                    ew(lambda e, vi=vi, src0=src0, dst0=dst0: e.tensor_copy(out=vpad[:, l, 1:5, vi, dst0:dst0 + 64],
                                                                            in_=Ptv[:, :, src0:src0 + 64]),
                       r=[Pt], w=[vpad], engines=("dve",))

                ckpt("AX")
                tmp2 = [sb(pre, "ru%d" % i, [128, TB]) for i in range(6)]
                sqbs = [sb(pre, "sqb%d" % i, [128, TB], BF16) for i in range(2)]
                for hr in range(2):
                    pairs = [2 * hr, 2 * hr + 1]
                    lnw, lnb, AR, BT, KT, rkp, vSb, sgA, rho, sC = HRT[hr]
                    S.dma("sp", lnw[:], lnw_d[l:l + 1, hr * 256:(hr + 1) * 256].partition_broadcast(128), w=[lnw])
                    S.dma("sp", lnb[:], lnb_d[l:l + 1, hr * 256:(hr + 1) * 256].partition_broadcast(128), w=[lnb])
                    for pi, p in enumerate(pairs):
                        wt = wt_next
                        nxt = p + 1
                        wt_next = load_w("in", l, G_A0 + nxt) if nxt < 4 else load_w("in", l, G_Q)
                        rS, kS, sgw, cs, av, t5 = tmp if pi == 0 else tmp2
                        P = proj_chunk(wt, 0)
                        shifted(P, p, rS[:, :], rS)
                        P = proj_chunk(wt, 1)
                        shifted(P, 4 + p, kS[:, :], kS)
                        P = proj_chunk(wt, 2)
                        shifted(P, 8 + p, vSb[pi][:, :], vSb[pi])
                        P = proj_chunk(wt, 3)
                        S.op("act", lambda e, P=P, pi=pi: e.activation(out=sgA[pi][:, :], in_=P[:, :], func=AF.Silu),
                             r=[P], w=[sgA[pi]])
                        PA, PB = PS[2], PS[3]
                        S.mm(lambda e, p=p: e.matmul(PA[:, :], lhsT=wupl[:, p * 128:(p + 1) * 128], rhs=lora_t[:, :],
                                                     start=True, stop=True), r=[wupl, lora_t], w=[PA])
                        S.mm(lambda e, p=p: e.matmul(PB[:, :], lhsT=aupl[:, p * 128:(p + 1) * 128], rhs=lora_t[:, :],
                                                     start=True, stop=True), r=[aupl, lora_t], w=[PB])
                        S.op("act", lambda e, p=p: e.activation(out=sgw[:, :], in_=PA[:, :], func=AF.Sigmoid,
                                                                bias=pc(l, W0 + p)), r=[PA, pcol], w=[sgw])
                        S.op("act", lambda e, p=p: e.activation(out=av[:, :], in_=PB[:, :], func=AF.Sigmoid,
                                                                bias=pc(l, A0 + p)), r=[PB, pcol], w=[av])
                        S.op("dve", lambda e: e.tensor_tensor_scan(out=cs[:, :], data0=cf[:, 0:512], data1=sgw[:, :], initial=0.0,
                                                                   op0=ALU.mult, op1=ALU.add), r=[cf, sgw], w=[cs])
                        S.op("act", lambda e, pi=pi: e.activation(out=rho[pi][:, :], in_=cs[:, 63:512:128], func=AF.Exp, scale=C0),
                             r=[cs], w=[rho[pi]])
                        cc = t5
                        S.op("dve", lambda e: e.tensor_tensor(out=cc[:, :].rearrange("p (c t) -> p c t", c=4),
                                                              in0=cs[:, :].rearrange("p (c t) -> p c t", c=4),
                                                              in1=cs[:, 63:512:128].unsqueeze(2).to_broadcast([128, 4, 128]),
                                                              op=ALU.subtract), r=[cs], w=[cc])
                        S.op("act", lambda e, pi=pi: e.activation(out=sC[pi][:, :], in_=cc[:, 127:512:128], func=AF.Exp, scale=C0),
                             r=[cc], w=[sC[pi]])
                        S.op("pool", lambda e: e.tensor_tensor(out=sgw[:, :], in0=cc[:, :], in1=sgw[:, :], op=ALU.subtract),
                             r=[cc, sgw], w=[sgw])
                        S.op("act", lambda e: e.activation(out=sgw[:, :], in_=sgw[:, :], func=AF.Exp, scale=C0), r=[sgw], w=[sgw])
                        S.op("act", lambda e: e.activation(out=cs[:, :], in_=cc[:, :], func=AF.Exp, scale=C0), r=[cc], w=[cs])
                        S.op("act", lambda e: e.activation(out=cc[:, :], in_=cc[:, :], func=AF.Exp, scale=-C0), r=[cc], w=[cc])
                        eprev, epos, eneg = sgw, cs, cc
                        S.op("pool", lambda e, pi=pi: e.tensor_tensor(out=AR[pi][:, :, 1, :],
                                                                      in0=rS[:, :].rearrange("p (c t) -> p c t", c=4),
                                                                      in1=epos[:, :].rearrange("p (c t) -> p c t", c=4), op=ALU.mult),
                             r=[rS, epos], w=[AR[pi]])
                        S.op("dve", lambda e, p=p: e.tensor_scalar(out=cs[:, :], in0=av[:, :], scalar1=pc(l, KA + p),
                                                                   scalar2=omka[:, l * 4 + p:l * 4 + p + 1], op0=ALU.mult, op1=ALU.add),
                             r=[av, pcol, omka], w=[cs])
                        S.op("pool", lambda e: e.tensor_tensor(out=cs[:, :], in0=cs[:, :], in1=kS[:, :], op=ALU.mult), r=[cs, kS], w=[cs])
                        kmod = cs
                        S.op("dve", lambda e, p=p, pi=pi: e.scalar_tensor_tensor(out=rkp[pi][:, :], in0=rS[:, :], scalar=pc(l, RK + p),
                                                                                in1=kmod[:, :], op0=ALU.mult, op1=ALU.mult),
                             r=[rS, pcol, kmod], w=[rkp[pi]])
                        S.op("pool", lambda e, pi=pi: e.tensor_tensor(out=KT[pi][:, :], in0=kmod[:, :], in1=eneg[:, :], op=ALU.mult),
                             r=[kmod, eneg], w=[KT[pi]])
                        kraw = rS
                        S.op("dve", lambda e, p=p: e.tensor_scalar(out=kraw[:, :], in0=kS[:, :], scalar1=pc(l, KK + p), scalar2=None,
                                                                   op0=ALU.mult), r=[kS, pcol], w=[kraw])
                        S.op("act", lambda e, sqb=sqbs[pi]: e.activation(out=sqb[:, :], in_=kraw[:, :], func=AF.Square), r=[kraw], w=[sqbs[pi]])
                    for pi, p in enumerate(pairs):
                        rS, kS, sgw, cs, av, t5 = tmp if pi == 0 else tmp2
                        PA, PB = PS[2], PS[3]
                        cc = t5
                        eprev, epos, eneg = sgw, cs, cc
                        kmod = cs
                        kraw = rS
                        S.mm(lambda e, sqb=sqbs[pi]: e.matmul(PA[:, :], lhsT=C_("bones"), rhs=sqb[:, :], start=True, stop=True),
                             r=[cb, sqbs[pi]], w=[PA])
                        nrm = kS
                        S.op("act", lambda e: e.sqrt(out=nrm[:, :], in_=PA[:, :]), r=[PA], w=[nrm])
                        S.op("dve", lambda e: e.tensor_scalar(out=nrm[:, :], in0=nrm[:, :], scalar1=1e-12, scalar2=None, op0=ALU.max),
                             r=[nrm], w=[nrm])
                        S.op("dve", lambda e: e.reciprocal(out=nrm[:, :], in_=nrm[:, :]), r=[nrm], w=[nrm])
                        kk = kraw
                        S.op("pool", lambda e: e.tensor_tensor(out=kk[:, :], in0=kraw[:, :], in1=nrm[:, :], op=ALU.mult), r=[kraw, nrm], w=[kk])
                        S.op("dve", lambda e, pi=pi: e.scalar_tensor_tensor(out=AR[pi][:, :, 0, :],
                                                                           in0=kk[:, :].rearrange("p (c t) -> p c t", c=4), scalar=-1.0,
                                                                           in1=eprev[:, :].rearrange("p (c t) -> p c t", c=4),
                                                                           op0=ALU.mult, op1=ALU.mult), r=[kk, eprev], w=[AR[pi]])
                        S.op("pool", lambda e: e.tensor_tensor(out=av[:, :], in0=av[:, :], in1=eneg[:, :], op=ALU.mult), r=[av, eneg], w=[av])
                        S.op("dve", lambda e, pi=pi: e.tensor_tensor(out=BT[pi][:, :], in0=kk[:, :], in1=av[:, :], op=ALU.mult),
                             r=[kk, av], w=[BT[pi]])
                rel(pre)
            ckpt("Apre")
            wtQ = wt_next

            def chunk_loop(hr, bk):
                sh = st
                pairs = [2 * hr, 2 * hr + 1]
                lnw, lnb, AR, BT, KT, rkp, vSb, sgA, rho, sC = HRT[hr]
                BTm = [[sb(sh, "BTm%d%d" % (i, e_), [128, 128], BF16) for e_ in range(2)] for i in range(2)]
                KTm = [[sb(sh, "KTm%d%d" % (i, e_), [128, 128], BF16) for e_ in range(2)] for i in range(2)]
                ATm = [[sb(sh, "ATm%d%d" % (i, e_), [128, 128], BF16) for e_ in range(2)] for i in range(2)]
                LA = sb(sh, "LA", [128, 4, 2, 128], BF16)
                KA = sb(sh, "KA", [128, 4, 2, 128], BF16)
                ZL0 = sb(sh, "ZL0", [128, 4, 2, 128], BF16)
                ZLp = [[sb(sh, "ZLp%d%d" % (i, j), [128, 2, 2, 128], BF16) for j in range(2)] for i in range(2)]
                LTp = [[sb(sh, "LTp%d%d" % (i, j), [128, 2, 128], BF16) for j in range(2)] for i in range(2)]
                TK = sb(sh, "TK", [128, 2, 4, 128], BF16)
                Pp = sb(sh, "Pp", [128, 2, 128], BF16)
                Ul = sb(sh, "Ul", [128, 2, 128], BF16)
                GT = sb(sh, "GT", [128, 2, 128], BF16)
                MtT = sb(sh, "MtT", [128, 2, 128], BF16)
                Hb = sb(sh, "Hb", [128, 2, 128], BF16)
                gs = sb(sh, "gs", [128, 32])
                rkt = sb(sh, "rkt", [128, 4])
                sq = sb(sh, "sq", [128, 256])
                yn = sb(sh, "yn", [128, 256])
                yab = sb(sh, "yab", [128, 256], BF16)
                def dense(slot):
                    Pq = PS[ch % 2]
                    if slot < 16:
                        k = slot
                        S.mm(lambda e: e.matmul(Pq[:, :], lhsT=wtQ[:, k, ch * 128:(ch + 1) * 128], rhs=hT[:, k, :], start=(k == 0), stop=(k == 15)),
                             r=[wtQ, hT], w=[Pq], last=(k == 15))
                    if slot == 15:
                        S.op("act", lambda e: e.copy(out=qTp[ch][:, :], in_=Pq[:, :]), r=[Pq], w=[qTp[ch]])

                for ch in range(4):
                    csl = slice(ch * 128, (ch + 1) * 128)
                    for pi in range(2):
                        for e_ in range(2):
                            hmc = cf[:, 512 + e_:513 + e_]
                            S.op("act", lambda e: e.activation(out=BTm[pi][e_][:, :], in_=BT[pi][:, csl], func=AF.Identity, scale=hmc), r=[BT[pi], cf], w=[BTm[pi][e_]])
                            S.op("pool", lambda e: e.tensor_scalar(out=KTm[pi][e_][:, :], in0=KT[pi][:, csl], scalar1=hmc, scalar2=1.0, op0=ALU.mult, op1=ALU.mult), r=[KT[pi], cf], w=[KTm[pi][e_]])
                            S.op("act", lambda e: e.activation(out=ATm[pi][e_][:, :], in_=AR[pi][:, ch, 0, :], func=AF.Identity, scale=hmc), r=[AR[pi], cf], w=[ATm[pi][e_]])
                    PLA = [PS[bk[2]], PS[bk[3]]]
                    PKA = [PS[bk[4]], PS[bk[5]]]
                    PL = PS[bk[6]]
                    for hh in range(4):
                        pi, e_ = hh // 2, hh % 2
                        o2 = slice(e_ * 256, (e_ + 1) * 256)
                        hs = slice(hh * 128, (hh + 1) * 128)
                        rhs2 = AR[pi][:, ch, :, :].rearrange("p a t -> p (a t)")
                        S.mm(lambda e: e.matmul(PLA[pi][:, o2], lhsT=BTm[pi][e_][:, :], rhs=rhs2, start=True, stop=True), r=[BTm[pi][e_], AR[pi]], w=[PLA[pi]], last=(e_ == 1))
                        S.mm(lambda e: e.matmul(PKA[pi][:, o2], lhsT=KTm[pi][e_][:, :], rhs=rhs2, start=True, stop=True), r=[KTm[pi][e_], AR[pi]], w=[PKA[pi]], last=(e_ == 1))
                        S.mm(lambda e: e.matmul(PL[:, hs], lhsT=ATm[pi][e_][:, :], rhs=BT[pi][:, csl], start=True, stop=True), r=[ATm[pi][e_], BT[pi]], w=[PL], last=(hh == 3))
                    dense(0 + hr)
                    for pi in range(2):
                        o_la = LA[:, 2 * pi:2 * pi + 2, :, :].rearrange("p h a t -> p (h a t)")
                        o_ka = KA[:, 2 * pi:2 * pi + 2, :, :].rearrange("p h a t -> p (h a t)")
                        S.op("dve", lambda e: e.tensor_tensor(out=o_la, in0=PLA[pi][:, :], in1=C_("suui2"), op=ALU.mult), r=[PLA[pi], cb], w=[LA])
                        S.op("dve", lambda e: e.tensor_tensor(out=o_ka, in0=PKA[pi][:, :], in1=C_("suui2"), op=ALU.mult), r=[PKA[pi], cb], w=[KA])
                    S.op("dve", lambda e: e.tensor_tensor(out=ZL0[:, :, 1, :], in0=PL[:, :].rearrange("p (h t) -> p h t", h=4),
                                                          in1=C_("sl4").rearrange("p (h t) -> p h t", h=4), op=ALU.mult), r=[PL, cb], w=[ZL0])
                    yield
                    Pt = PS[bk[7]]
                    Ptb = Pt[:, :].bitcast(BF16)
                    for pi in range(2):
                        srcs = [(AR[pi], AR[pi][:, ch, 0, :]), (BT[pi], BT[pi][:, csl]), (KT[pi], KT[pi][:, csl]), (vSb[pi], vSb[pi][:, csl])]
                        for qi, (tl_, ap_) in enumerate(srcs):
                            o0 = (pi * 4 + qi) * 128
                            S.mm(lambda e, ap_=ap_, o0=o0: e.transpose(Ptb[:, o0:o0 + 128], ap_, C_("ident")), r=[tl_, cb], w=[Pt],
                                 last=(pi == 1 and qi == 3))
                    dense(2 + hr)
                    S.op("dve", lambda e: e.tensor_copy(out=TK[:].rearrange("p a q c -> p (a q c)"), in_=Ptb[:, 0:1024]), r=[Pt], w=[TK])
                    PZ = PS[bk[2]]
                    for hh in range(4):
                        pi, e_ = hh // 2, hh % 2
                        S.mm(lambda e, hh=hh, pi=pi, e_=e_: e.matmul(PZ[:, hh * 128 + 64: hh * 128 + 128], lhsT=KA[:, hh, 0, :],
                                                                      rhs=TK[:, pi, 3, e_ * 64:(e_ + 1) * 64], start=True, stop=True),
                             r=[KA, TK], w=[PZ], last=(hh == 3))
                    PZv = PZ[:, :].rearrange("p (h c) -> p h c", h=4)
                    S.op("act", lambda e: e.copy(out=ZL0[:, :, 0, 64:128], in_=PZv[:, :, 64:128]), r=[PZ], w=[ZL0])
                    S.op("dve", lambda e: e.tensor_copy(out=ZL0[:, :, 0, 0:64].rearrange("p (a b) c -> p a b c", a=2),
                                                         in_=TK[:, :, 0, :].rearrange("p a (b c) -> p a b c", b=2)), r=[TK], w=[ZL0])
                    yield
                    for step in range(7):
                        for pi in range(2):
                            PZL, PLTq = PS[bk[2 + pi]], PS[bk[4]]
                            lo_ = pi * 256
                            if step == 0:
                                zlT, ltT = ZL0, LA
                                zl = [ZL0[:, 2 * pi + e_, :, :] for e_ in range(2)]
                                ltA = [LA[:, 2 * pi + e_, 0, :] for e_ in range(2)]
                            else:
                                cur = (step - 1) % 2
                                zlT, ltT = ZLp[pi][cur], LTp[pi][cur]
                                zl = [zlT[:, e_, :, :] for e_ in range(2)]
                                ltA = [ltT[:, e_, :] for e_ in range(2)]
                            for e_ in range(2):
                                rhs_ = zl[e_].rearrange("p a t -> p (a t)")
                                S.mm(lambda e: e.matmul(PZL[:, e_ * 256:(e_ + 1) * 256], lhsT=ltA[e_], rhs=rhs_, start=True, stop=True), r=[ltT, zlT], w=[PZL], last=(e_ == 1))
                            if step < 6:
                                for e_ in range(2):
                                    lk_ = zl[e_][:, 1, :]
                                    S.mm(lambda e: e.matmul(PLTq[:, lo_ + e_ * 128:lo_ + (e_ + 1) * 128], lhsT=lk_, rhs=ltA[e_], start=True, stop=True), r=[zlT, ltT], w=[PLTq], last=(e_ == 1))
                            for f_ in range(NFILL):
                                Pf = PS[fill_rr[0] % 2]
                                fill_rr[0] += 1
                                S.mm(lambda e: e.matmul(Pf[:, :], lhsT=C_("ident"), rhs=hT[:, 0, :], start=True, stop=True), r=[cb, hT], w=[Pf], last=False)
                        dense(4 + 2 * step + hr)
                        for pi in range(2):
                            PZL, PLTq = PS[bk[2 + pi]], PS[bk[4]]
                            lo_ = pi * 256
                            nx = step % 2
                            PZLv = PZL[:, :].rearrange("p (e a t) -> p e a t", e=2, a=2)
                            if step == 0:
                                zprevT, zprev = ZL0, ZL0[:, 2 * pi:2 * pi + 2, 0, :]
                            else:
                                zprevT = ZLp[pi][(step - 1) % 2]
                                zprev = zprevT[:, :, 0, :]
                            if step < 6:
                                o_z = ZLp[pi][nx][:, :, 0, :]
                                o_l = ZLp[pi][nx][:, :, 1, :]
                                o_lt = LTp[pi][nx][:].rearrange("p h t -> p (h t)")
                                S.op("dve", lambda e: e.tensor_tensor(out=o_z, in0=PZLv[:, :, 0, :], in1=zprev, op=ALU.add), r=[PZL, zprevT], w=[ZLp[pi][nx]])
                                S.op("act", lambda e: e.copy(out=o_l, in_=PZLv[:, :, 1, :]), r=[PZL], w=[ZLp[pi][nx]])
                                S.op("dve", lambda e: e.tensor_copy(out=o_lt, in_=PLTq[:, lo_:lo_ + 256]), r=[PLTq], w=[LTp[pi][nx]])
                            else:
                                o_p = Pp[:, pi, :].rearrange("p (b c) -> p b c", b=2)
                                o_u = Ul[:, pi, :].rearrange("p (b c) -> p b c", b=2)
                                S.op("dve", lambda e: e.tensor_tensor(out=o_p, in0=PZLv[:, :, 0, 0:64], in1=zprev[:, :, 0:64], op=ALU.add), r=[PZL, zprevT], w=[Pp])
                                S.op("dve", lambda e: e.tensor_tensor(out=o_u, in0=PZLv[:, :, 0, 64:128], in1=zprev[:, :, 64:128], op=ALU.add), r=[PZL, zprevT], w=[Ul])
                        yield
                    yield
                    PG = PS[bk[2]]
                    for hh in range(4):
                        pi = hh // 2
                        S.mm(lambda e, hh=hh, pi=pi: e.matmul(PG[:, hh * 128:(hh + 1) * 128], lhsT=Pp[:, pi, :], rhs=LA[:, hh, 1, :], start=True, stop=True),
                             r=[Pp, LA], w=[PG], last=(hh == 3))
                    for pi in range(2):
                        for e_ in range(2):
                            hh = pi * 2 + e_
                            rows = slice(e_ * 64, (e_ + 1) * 64)
                            S.op("dve", lambda e, pi=pi, hh=hh, rows=rows: e.tensor_tensor(out=GT[rows, pi, :], in0=PG[rows, hh * 128:(hh + 1) * 128],
                                                                                          in1=AR[pi][rows, ch, 1, :], op=ALU.add), r=[PG, AR[pi]], w=[GT])
                    PM = PS[bk[6]]
                    for pi in range(2):
                        S.mm(lambda e, pi=pi: e.matmul(PM[:, pi * 128:(pi + 1) * 128], lhsT=Pp[:, pi, :], rhs=TK[:, pi, 1, :], start=True, stop=True),
                             r=[Pp, TK], w=[PM], last=(pi == 1))
                    S.op("dve", lambda e: e.tensor_tensor(out=MtT[:].rearrange("p a c -> p (a c)"), in0=PM[:, 0:256], in1=C_("bd2"), op=ALU.mult), r=[PM, cb], w=[MtT])
                    for pi in range(2):
                        p = pairs[pi]
                        S.op("act", lambda e, pi=pi, p=p: e.activation(out=Hb[:, pi, :], in_=Hc[:, l, p, :], func=AF.Identity, scale=rho[pi][:, ch:ch + 1]), r=[Hc, rho[pi]], w=[Hb])
                    PY = PS[bk[3]]
                    for pi in range(2):
                        S.mm(lambda e, pi=pi: e.matmul(PY[:, pi * 128:(pi + 1) * 128], lhsT=GT[:, pi, :], rhs=Hb[:, pi, :], start=True, stop=False),
                             r=[GT, Hb], w=[PY], last=False)
                        for e_ in range(2):
                            hh = pi * 2 + e_
                            cs_ = slice(pi * 128 + e_ * 64, pi * 128 + (e_ + 1) * 64)
                            S.mm(lambda e, pi=pi, e_=e_, hh=hh, cs_=cs_: e.matmul(PY[:, cs_], lhsT=LA[:, hh, 1, :], rhs=Ul[:, pi, e_ * 64:(e_ + 1) * 64], start=False, stop=False),
                                 r=[LA, Ul], w=[PY], last=False)
                            S.mm(lambda e, pi=pi, e_=e_, hh=hh, cs_=cs_: e.matmul(PY[:, cs_], lhsT=KA[:, hh, 1, :], rhs=TK[:, pi, 3, e_ * 64:(e_ + 1) * 64], start=False, stop=(e_ == 1)),
                                 r=[KA, TK], w=[PY], last=(pi == 1 and e_ == 1))
                    PC_ = PS[bk[4]]
                    for pi in range(2):
                        o_ = slice(pi * 128, (pi + 1) * 128)
                        S.mm(lambda e, pi=pi, o_=o_: e.matmul(PC_[:, o_], lhsT=MtT[:, pi, :], rhs=Hb[:, pi, :], start=True, stop=False), r=[MtT, Hb], w=[PC_], last=False)
                        S.mm(lambda e, pi=pi, o_=o_: e.matmul(PC_[:, o_], lhsT=C_("ident"), rhs=Hb[:, pi, :], start=False, stop=False), r=[cb, Hb], w=[PC_], last=False)
                        S.mm(lambda e, pi=pi, o_=o_: e.matmul(PC_[:, o_], lhsT=TK[:, pi, 1, :], rhs=Ul[:, pi, :], start=False, stop=False), r=[TK, Ul], w=[PC_], last=False)
                        S.mm(lambda e, pi=pi, o_=o_: e.matmul(PC_[:, o_], lhsT=TK[:, pi, 2, :], rhs=TK[:, pi, 3, :], start=False, stop=True), r=[TK], w=[PC_], last=(pi == 1))
                    for pi in range(2):
                        p = pairs[pi]
                        S.op("dve", lambda e, pi=pi, p=p: e.scalar_tensor_tensor(out=Hc[:, l, p, :], in0=PC_[:, pi * 128:(pi + 1) * 128], scalar=sC[pi][:, ch:ch + 1],
                                                                                in1=C_("bd2", 0, 128), op0=ALU.mult, op1=ALU.mult), r=[PC_, sC[pi], cb], w=[Hc])
                    pass
                    PR = PS[bk[5]]
                    for pi in range(2):
                        S.mm(lambda e, pi=pi: e.matmul(PR[:, pi * 2:(pi + 1) * 2], lhsT=rkp[pi][:, csl], rhs=C_("hsel"), start=True, stop=True), r=[rkp[pi], cb], w=[PR], last=(pi == 1))
                    PYv = PY[:, 0:256].rearrange("p (h i) -> p h i", h=4)
                    S.op("act", lambda e: e.activation(out=sq[:, :], in_=PY[:, 0:256], func=AF.Square), r=[PY], w=[sq])
                    S.op("act", lambda e: e.copy(out=rkt[:, 0:4], in_=PR[:, 0:4]), r=[PR], w=[rkt])
                    S.op("dve", lambda e: e.tensor_reduce(out=gs[:, 0:4], in_=PYv, axis=AX.X, op=ALU.add), r=[PY], w=[gs])
                    S.op("dve", lambda e: e.tensor_reduce(out=gs[:, 4:8], in_=sq[:, :].rearrange("p (h i) -> p h i", h=4), axis=AX.X, op=ALU.add), r=[sq], w=[gs])
                    S.op("dve", lambda e: e.tensor_scalar(out=gs[:, 8:12], in0=gs[:, 0:4], scalar1=1.0 / 64, scalar2=None, op0=ALU.mult), r=[gs], w=[gs])
                    S.op("dve", lambda e: e.tensor_tensor(out=gs[:, 12:16], in0=gs[:, 8:12], in1=gs[:, 8:12], op=ALU.mult), r=[gs], w=[gs])
                    S.op("dve", lambda e: e.scalar_tensor_tensor(out=gs[:, 16:20], in0=gs[:, 4:8], scalar=1.0 / 64, in1=gs[:, 12:16], op0=ALU.mult, op1=ALU.subtract), r=[gs], w=[gs])
                    S.op("dve", lambda e: e.tensor_scalar(out=gs[:, 16:20], in0=gs[:, 16:20], scalar1=GN_EPS, scalar2=None, op0=ALU.add), r=[gs], w=[gs])
                    S.op("act", lambda e: e.sqrt(out=gs[:, 20:24], in_=gs[:, 16:20]), r=[gs], w=[gs])
                    S.op("dve", lambda e: e.reciprocal(out=gs[:, 24:28], in_=gs[:, 20:24]), r=[gs], w=[gs])
                    ynv = yn[:, :].rearrange("p (h i) -> p h i", h=4)
                    S.op("dve", lambda e: e.tensor_tensor(out=ynv, in0=PYv, in1=gs[:, 8:12].unsqueeze(2).to_broadcast([128, 4, 64]), op=ALU.subtract), r=[PY, gs], w=[yn])
                    S.op("dve", lambda e: e.tensor_tensor(out=ynv, in0=ynv, in1=gs[:, 24:28].unsqueeze(2).to_broadcast([128, 4, 64]), op=ALU.mult), r=[yn, gs], w=[yn])
                    S.op("dve", lambda e: e.tensor_tensor(out=yn[:, :], in0=yn[:, :], in1=lnw[:, :], op=ALU.mult), r=[yn, lnw], w=[yn])
                    S.op("dve", lambda e: e.tensor_tensor(out=yn[:, :], in0=yn[:, :], in1=lnb[:, :], op=ALU.add), r=[yn, lnb], w=[yn])
                    S.op("dve", lambda e: e.tensor_tensor(out=sq[:, :].rearrange("p (a b c) -> p a b c", a=2, b=2),
                                                          in0=TK[:, :, 3, :].rearrange("p a (b c) -> p a b c", b=2),
                                                          in1=rkt[:, 0:4].rearrange("p (a b) -> p a b", a=2).unsqueeze(3).to_broadcast([128, 2, 2, 64]), op=ALU.mult),
                         r=[TK, rkt], w=[sq])
                    S.op("dve", lambda e: e.tensor_tensor(out=yab[:, :], in0=yn[:, :], in1=sq[:, :], op=ALU.add), r=[yn, sq], w=[yab])
                    Pt2 = PS[bk[6]]
                    Pt2b = Pt2[:, :].bitcast(BF16)
                    for pi in range(2):
                        S.mm(lambda e, pi=pi: e.transpose(Pt2b[:, pi * 128:(pi + 1) * 128], yab[:, pi * 128:(pi + 1) * 128], C_("ident")), r=[yab, cb], w=[Pt2], last=(pi == 1))
                    for pi in range(2):
                        p = pairs[pi]
                        S.op("dve", lambda e, pi=pi, p=p: e.tensor_tensor(out=yT[:, p, csl], in0=Pt2b[:, pi * 128:(pi + 1) * 128], in1=sgA[pi][:, csl], op=ALU.mult),
                             r=[Pt2, sgA[pi]], w=[yT])
                    yield

            gens = [chunk_loop(0, [0, 1, 2, 3, 4, 5, 6, 7]), chunk_loop(1, [0, 1, 5, 6, 7, 2, 3, 4])]
            live = [gens[0]]
            started = 1
            nyield = 0
            while live:
                for g_ in list(live):
                    try:
                        next(g_)
                    except StopIteration:
                        live.remove(g_)
                nyield += 1
                if started < 2 and (nyield >= STAGGER or not live):
                    live.append(gens[1])
                    started = 2
            rel(st)
        ckpt("A")

        with contextlib.ExitStack() as st:
            qT = qTp
            sgB = [sb(st, "sgB%d" % c, [128, TB], BF16) for c in range(4)]
            wt = wt_next
            wt_next = load_w("in", l, G_GB)
            wt = wt_next
            wt_next = load_w("in", l, G_C)
            for c in range(4):
                P = proj_chunk(wt, c)
                S.op("act", lambda e, c=c, P=P: e.activation(out=sgB[c][:, :], in_=P[:, :], func=AF.Silu), r=[P], w=[sgB[c]])
            Et = sb(st, "Et", [128, 2048], BF16)
            S.dma("sp", Et[:], cb_d[:, NCBS:NCBS + 2048], w=[Et])
            pexp = sb(st, "pexp", [128, 1024], BF16)
            pT = sb(st, "pT", [128, 8, 128], BF16)
            t1 = sb(st, "t1", [128, 256])
            t2 = sb(st, "t2", [128, 256])
            for i in range(4):
                tsl = slice(i * 128, (i + 1) * 128)
                for g in range(2):
                    Pa, Pb_ = (PS[2], PS[3]) if g == 0 else (PS[6], PS[7])
                    for hh in range(4):
                        par = hh % 2
                        qc = 2 * g + hh // 2
                        for pcur in range(2):
                            Pd = Pa if hh < 2 else Pb_
                            o0 = ((hh % 2) * 2 + pcur) * 128
                            ks = slice((i + pcur) * 128, (i + pcur + 1) * 128)
                            S.mm(lambda e, Pd=Pd, o0=o0, ks=ks, g=g, par=par, qc=qc: e.matmul(Pd[:, o0:o0 + 128], lhsT=kxm[:, l, 2 * g + par, ks], rhs=qT[qc][:, tsl],
                                                                                              start=True, stop=True), r=[kxm, qT[qc]], w=[Pd], last=(pcur == 1 and hh % 2 == 1))
                    S.op("act", lambda e: e.activation(out=pexp[:, 0:512], in_=Pa[:, :], func=AF.Exp, scale=0.125), r=[Pa], w=[pexp])
                    S.op("act", lambda e: e.activation(out=pexp[:, 512:1024], in_=Pb_[:, :], func=AF.Exp, scale=0.125), r=[Pb_], w=[pexp])
                    S.op("dve", lambda e, g=g: e.tensor_tensor(out=pT[:].rearrange("p a q -> p (a q)"), in0=pexp[:, :], in1=Et[:, g * 1024:(g + 1) * 1024], op=ALU.mult), r=[pexp, Et], w=[pT])
                    PN, PD_ = PS[4], PS[5]
                    skip_prev = first and i == 0
                    for cq in range(2):
                        o_ = slice(cq * 128, (cq + 1) * 128)
                        terms = []
                        for par in range(2):
                            hh = 2 * cq + par
                            terms.append((i + 1, 2 * g + par, hh * 2 + 1, "onesA" if par == 0 else "onesB"))
                            if not skip_prev:
                                terms.append((i, 2 * g + par, hh * 2 + 0, "onesA" if par == 0 else "onesB"))
                        for ti, (vt, vv_, pidx, on) in enumerate(terms):
                            S.mm(lambda e, vt=vt, vv_=vv_, pidx=pidx, o_=o_, ti=ti: e.matmul(PN[:, o_], lhsT=vpad[:, l, vt, vv_, :], rhs=pT[:, pidx, :], start=(ti == 0), stop=(ti == len(terms) - 1)),
                                 r=[vpad, pT], w=[PN], last=(ti == len(terms) - 1))
                        for ti, (vt, vv_, pidx, on) in enumerate(terms):
                            S.mm(lambda e, on=on, pidx=pidx, o_=o_, ti=ti: e.matmul(PD_[:, o_], lhsT=C_(on), rhs=pT[:, pidx, :], start=(ti == 0), stop=(ti == len(terms) - 1)),
                                 r=[cb, pT], w=[PD_], last=(ti == len(terms) - 1))
                    es = esink[:, l * 4 + 2 * g: l * 4 + 2 * g + 2]
                    S.op("dve", lambda e, es=es: e.tensor_tensor(out=t1[:, :].rearrange("p (a q) -> p a q", a=2), in0=PD_[:, 0:256].rearrange("p (a q) -> p a q", a=2),
                                                                 in1=es.unsqueeze(2).to_broadcast([128, 2, 128]), op=ALU.add), r=[PD_, esink], w=[t1])
                    S.op("dve", lambda e: e.reciprocal(out=t1[:, :], in_=t1[:, :]), r=[t1], w=[t1])
                    S.op("dve", lambda e: e.tensor_tensor(out=t2[:, :], in0=PN[:, 0:256], in1=t1[:, :], op=ALU.mult), r=[PN, t1], w=[t2])
                    for cq in range(2):
                        c = 2 * g + cq
                        S.op("pool", lambda e, c=c, cq=cq: e.tensor_tensor(out=yT[:, 4 + c, tsl], in0=t2[:, cq * 128:(cq + 1) * 128], in1=sgB[c][:, tsl], op=ALU.mult),
                             r=[t2, sgB[c]], w=[yT])
            S.op("pool", lambda e: e.tensor_copy(out=kxm[:, l, :, 0:128], in_=kxm[:, l, :, 512:640]), r=[kxm], w=[kxm])
            S.op("dve", lambda e: e.tensor_copy(out=vpad[:, l, 0, :, :], in_=vpad[:, l, 4, :, :]), r=[vpad], w=[vpad])
            rel(st)

        ckpt("B")
        with contextlib.ExitStack() as st:
            zc = sb(st, "zc", [128, 4, 528])
            ta = sb(st, "ta", [128, 528])
            tb_ = sb(st, "tb", [128, 528])
            sgC = [sb(st, "sgC%d" % c, [128, TB], BF16) for c in range(4)]
            pooled = sb(st, "pooled", [128, TB], BF16)
            poolwl = sb(st, "poolwl", [128, 4, 128], BF16)
            S.dma("sp", poolwl[:].rearrange("p g n -> p (g n)"), csc["poolw"][:, l * 512:(l + 1) * 512], r=[csc_b["poolw"]], w=[poolwl])
            if first:
                icn = sb(st, "icn", [128, 4, 512])
                S.dma("sp", icn[:].rearrange("p g t -> p (g t)"), ic_d[:, :], w=[icn])
            wt = wt_next
            wt_next = load_w("in", l, G_GC)
            S.op("pool", lambda e: e.tensor_copy(out=zc[:, :, 0:16], in_=poolh[:, l, :, :]), r=[poolh], w=[zc])
            for g in range(4):
                P = proj_chunk(wt, g)
                S.op("act", lambda e, g=g, P=P: e.copy(out=zc[:, g, 16:528], in_=P[:, :]), r=[P], w=[zc])
            S.op("pool", lambda e: e.tensor_copy(out=poolh[:, l, :, :], in_=zc[:, :, 512:528]), r=[zc], w=[poolh])
            wt = wt_next
            wt_next = load_w("in", l, G_U)
            for c in range(4):
                P = proj_chunk(wt, c)
                S.op("act", lambda e, c=c, P=P: e.activation(out=sgC[c][:, :], in_=P[:, :], func=AF.Silu), r=[P], w=[sgC[c]])
            for g, wdw in enumerate((2, 4, 8, 16)):
                src_t, src = zc, (lambda a, b, g=g: zc[:, g, a:b])
                sh_ = 1
                bufs = [ta, tb_]
                bi_ = 0
                while sh_ < wdw:
                    dst = bufs[bi_ % 2]
                    bi_ += 1
                    lo_ = 2 * sh_ - 1
                    S.op("pool", lambda e, dst=dst, src=src, sh_=sh_, lo_=lo_: e.tensor_tensor(out=dst[:, lo_:528], in0=src(lo_, 528), in1=src(lo_ - sh_, 528 - sh_), op=ALU.add),
                         r=[src_t], w=[dst])
                    src_t, src = dst, (lambda a, b, dst=dst: dst[:, a:b])
                    sh_ *= 2
                if first:
                    S.op("dve", lambda e, g=g, src=src: e.tensor_tensor(out=src(16, 528), in0=src(16, 528), in1=icn[:, g, :], op=ALU.mult), r=[src_t, icn], w=[src_t])
                    S.op("dve", lambda e, g=g, src=src: e.tensor_tensor(out=pooled[:, :], in0=src(16, 528), in1=zc[:, g, 16:528], op=ALU.subtract), r=[src_t, zc], w=[pooled])
                else:
                    S.op("dve", lambda e, g=g, src=src, wdw=wdw: e.scalar_tensor_tensor(out=pooled[:, :], in0=src(16, 528), scalar=1.0 / wdw, in1=zc[:, g, 16:528],
                                                                                       op0=ALU.mult, op1=ALU.subtract), r=[src_t, zc], w=[pooled])
                PA = PS[2 + g % 2]
                S.mm(lambda e, g=g, PA=PA: e.matmul(PA[:, :], lhsT=poolwl[:, g, :], rhs=pooled[:, :], start=True, stop=True), r=[poolwl, pooled], w=[PA])
                S.op("dve", lambda e, g=g, PA=PA: e.scalar_tensor_tensor(out=yT[:, 8 + g, :], in0=PA[:, :], scalar=pc(l, PSC + g), in1=sgC[g][:, :], op0=ALU.mult, op1=ALU.mult),
                     r=[PA, pcol, sgC[g]], w=[yT])
            rel(st)

        ckpt("C")
        with contextlib.ExitStack() as st:
            uT = [sb(st, "uT%d" % c, [128, TB], BF16) for c in range(4)]
            vT = [sb(st, "vT%d" % c, [128, TB]) for c in range(4)]
            vb = [sb(st, "vb%d" % c, [128, TB], BF16) for c in range(4)]
            vq = [sb(st, "vq%d" % c, [128, TB], BF16) for c in range(4)]
            sgD = [sb(st, "sgD%d" % c, [128, TB], BF16) for c in range(4)]
            mean = sb(st, "mean", [128, TB])
            rstd = sb(st, "rstd", [128, TB])
            vtok = sb(st, "vtok", [128, 4, 4, 128], BF16)
            brep = sb(st, "brep", [128, 4, 128])
            sT = sb(st, "sT", [128, TB])
            wsTl = sb(st, "wsTl", [128, 4, 128], BF16)
            S.dma("sp", wsTl[:].rearrange("p g n -> p (g n)"), csc["wsT"][:, l * 512:(l + 1) * 512], r=[csc_b["wsT"]], w=[wsTl])
            S.dma("sp", brep[:].rearrange("p g i -> p (g i)"), sgub_d[l:l + 1, :].partition_broadcast(128), w=[brep])
            wt = wt_next
            wt_next = load_w("in", l, G_VD)
            for c in range(4):
                P = proj_chunk(wt, c)
                S.op("act", lambda e, c=c, P=P: e.activation(out=uT[c][:, :], in_=P[:, :], func=AF.Gelu), r=[P], w=[uT[c]])
            wt = wt_next
            wt_next = load_w("in", l, G_GD)
            for c in range(4):
                P = proj_chunk(wt, c)
                S.op("act", lambda e, c=c, P=P: e.activation(out=vT[c][:, :], in_=P[:, :], func=AF.Gelu), r=[P], w=[vT[c]])
                S.op("dve", lambda e, c=c: e.tensor_copy(out=vb[c][:, :], in_=vT[c][:, :]), r=[vT[c]], w=[vb[c]])
                S.op("pool", lambda e, c=c: e.tensor_tensor(out=vq[c][:, :], in0=vT[c][:, :], in1=vT[c][:, :], op=ALU.mult), r=[vT[c]], w=[vq[c]])
            wt = wt_next
            wt_next = load_w("out", l, 0)
            for c in range(4):
                P = proj_chunk(wt, c)
                S.op("act", lambda e, c=c, P=P: e.activation(out=sgD[c][:, :], in_=P[:, :], func=AF.Silu), r=[P], w=[sgD[c]])
            PM, PQ = PS[2], PS[3]
            for c in range(4):
                S.mm(lambda e, c=c: e.matmul(PM[:, :], lhsT=C_("o512"), rhs=vb[c][:, :], start=(c == 0), stop=(c == 3)), r=[cb, vb[c]], w=[PM], last=(c == 3))
            for c in range(4):
                S.mm(lambda e, c=c: e.matmul(PQ[:, :], lhsT=C_("o512"), rhs=vq[c][:, :], start=(c == 0), stop=(c == 3)), r=[cb, vq[c]], w=[PQ], last=(c == 3))
            S.op("act", lambda e: e.copy(out=mean[:, :], in_=PM[:, :]), r=[PM], w=[mean])
            S.op("pool", lambda e: e.tensor_tensor(out=rstd[:, :], in0=mean[:, :], in1=mean[:, :], op=ALU.mult), r=[mean], w=[rstd])
            S.op("dve", lambda e: e.tensor_tensor(out=rstd[:, :], in0=PQ[:, :], in1=rstd[:, :], op=ALU.subtract), r=[PQ, rstd], w=[rstd])
            S.op("dve", lambda e: e.tensor_scalar(out=rstd[:, :], in0=rstd[:, :], scalar1=LN_EPS, scalar2=None, op0=ALU.add), r=[rstd], w=[rstd])
            S.op("act", lambda e: e.sqrt(out=rstd[:, :], in_=rstd[:, :]), r=[rstd], w=[rstd])
            S.op("dve", lambda e: e.reciprocal(out=rstd[:, :], in_=rstd[:, :]), r=[rstd], w=[rstd])
            for c in range(4):
                S.op("pool", lambda e, c=c: e.tensor_tensor(out=vT[c][:, :], in0=vT[c][:, :], in1=mean[:, :], op=ALU.subtract), r=[vT[c], mean], w=[vT[c]])
                S.op("dve", lambda e, c=c: e.tensor_tensor(out=vb[c][:, :], in0=vT[c][:, :], in1=rstd[:, :], op=ALU.mult), r=[vT[c], rstd], w=[vb[c]])
            for i in range(4):
                Pt = PS[4 + i % 2]
                Ptb = Pt[:, :].bitcast(BF16)
                for c in range(4):
                    S.mm(lambda e, i=i, c=c, Ptb=Ptb: e.transpose(Ptb[:, c * 128:(c + 1) * 128], vb[c][:, i * 128:(i + 1) * 128], C_("ident")), r=[vb[c], cb], w=[Pt], last=(c == 3))
                S.op("act", lambda e, i=i, Ptb=Ptb: e.copy(out=vtok[:, i, :, :].rearrange("p c d -> p (c d)"), in_=Ptb[:, 0:512]), r=[Pt], w=[vtok])
            for g in range(4):
                Pg = PS[2 + g % 2]
                for i in range(4):
                    S.mm(lambda e, g=g, i=i, Pg=Pg: e.matmul(Pg[:, i * 128:(i + 1) * 128], lhsT=vtok[:, i, g, :], rhs=wsTl[:, g, :], start=True, stop=True),
                         r=[vtok, wsTl], w=[Pg], last=(i == 3))
                S.op("dve", lambda e, g=g, Pg=Pg: e.scalar_tensor_tensor(out=sT[:, :].rearrange("p (i t) -> p i t", i=4), in0=Pg[:, :].rearrange("p (i t) -> p i t", i=4),
                                                                        scalar=pc(l, SNW + g), in1=brep[:, g:g + 1, :].to_broadcast([128, 4, 128]), op0=ALU.mult, op1=ALU.add),
                     r=[Pg, pcol, brep], w=[sT])
                S.op("pool", lambda e, g=g: e.tensor_tensor(out=sT[:, :], in0=sT[:, :], in1=uT[g][:, :], op=ALU.mult), r=[sT, uT[g]], w=[sT])
                S.op("dve", lambda e, g=g: e.tensor_tensor(out=yT[:, 12 + g, :], in0=sT[:, :], in1=sgD[g][:, :], op=ALU.mult), r=[sT, sgD[g]], w=[yT])
            rel(st)

        ckpt("D")
        with contextlib.ExitStack() as st:
            o = sb(st, "o", [128, 4, D])
            postw = sb(st, "postw", [128, D])
            junk = sb(st, "junk", [128, 512], BF16)
            ssq = sb(st, "ssq", [128, 16])
            sm = sb(st, "smo", [128, 32])
            S.dma("sp", postw[:], postw_d[l:l + 1, :].partition_broadcast(128), w=[postw])
            for mb in range(4):
                wt = wt_next
                if mb < 3:
                    wt_next = load_w("out", l, mb + 1)
                elif not (bi == n_blocks - 1 and l == n_layers - 1):
                    x_pref[0] = load_w("in", (l + 1) if l + 1 < n_layers else 0, G_X)
                for i in range(4):
                    P = PS[pj_rr[0] % 2]
                    pj_rr[0] += 1
                    for k in range(16):
                        S.mm(lambda e, k=k, P=P, i=i, wt=wt: e.matmul(P[:, :], lhsT=yT[:, k, i * 128:(i + 1) * 128], rhs=wt[:, k, :], start=(k == 0), stop=(k == 15)),
                             r=[yT, wt], w=[P], last=(k == 15))
                    S.op("act", lambda e, i=i, mb=mb, P=P: e.copy(out=o[:, i, mb * 512:(mb + 1) * 512], in_=P[:, :]), r=[P], w=[o])
                    S.op("act", lambda e, i=i, mb=mb: e.activation(out=junk[:, :], in_=o[:, i, mb * 512:(mb + 1) * 512], func=AF.Square, accum_out=ssq[:, i * 4 + mb:i * 4 + mb + 1]),
                         r=[o], w=[junk, ssq])
            for i in range(4):
                S.op("dve", lambda e, i=i: e.tensor_reduce(out=sm[:, i * 8:i * 8 + 1], in_=ssq[:, i * 4:(i + 1) * 4], axis=AX.X, op=ALU.add), r=[ssq], w=[sm])
                S.op("dve", lambda e, i=i: e.tensor_scalar(out=sm[:, i * 8 + 1:i * 8 + 2], in0=sm[:, i * 8:i * 8 + 1], scalar1=1.0 / D, scalar2=NORM_EPS, op0=ALU.mult, op1=ALU.add), r=[sm], w=[sm])
                S.op("act", lambda e, i=i: e.sqrt(out=sm[:, i * 8 + 2:i * 8 + 3], in_=sm[:, i * 8 + 1:i * 8 + 2]), r=[sm], w=[sm])
                S.op("dve", lambda e, i=i: e.reciprocal(out=sm[:, i * 8 + 3:i * 8 + 4], in_=sm[:, i * 8 + 2:i * 8 + 3]), r=[sm], w=[sm])
                S.op("dve", lambda e, i=i: e.scalar_tensor_tensor(out=o[:, i, :], in0=o[:, i, :], scalar=sm[:, i * 8 + 3:i * 8 + 4], in1=postw[:, :], op0=ALU.mult, op1=ALU.mult), r=[o, sm, postw], w=[o])
                S.op("dve", lambda e, i=i: e.tensor_tensor(out=xb[i][:, :], in0=xb[i][:, :], in1=o[:, i, :], op=ALU.add), r=[xb[i], o], w=[xb[i]])
                if last_layer:
                    S.dma("sp", out_d[bi * TB + i * 128: bi * TB + (i + 1) * 128, :], xb[i][:, :], r=[xb[i]])
            rel(st)

    try:
        for bi in range(n_blocks):
            for l in range(n_layers):
                layer_block(bi, l, l == n_layers - 1)
    except _Stop:
        S.barrier()
        S.emit()
        return nc
    S.barrier()
    S.emit()
    stack.close()
    return nc


_CACHE = {}


def host_layout(inputs):
    f = lambda a: np.ascontiguousarray(np.asarray(a, dtype=np.float32))
    L = 2
    shared = {}
    shared["w_in_p"] = f(np.asarray(inputs["w_in"])[:, :, perm_index()])
    shared["w_out"] = f(inputs["w_out"])
    pcol = np.zeros((128, L * NPC), np.float32)
    for l in range(L):
        cols = []
        cols.append(np.asarray(inputs["pre_norm_w"])[l].reshape(16, 128).T)
        cols.append(np.asarray(inputs["shift_mu"])[l].reshape(13, 128).T)
        for nm in ("rwkv_w0", "rwkv_a0", "rwkv_k_k", "rwkv_k_a"):
            cols.append(np.asarray(inputs[nm])[l].reshape(4, 128).T)
        cols.append(np.asarray(inputs["rwkv_r_k"])[l].reshape(4, 128).T)
        cols.append(np.asarray(inputs["pool_scale"])[l].reshape(4, 128).T)
        cols.append(np.asarray(inputs["sgu_norm_w"])[l].reshape(4, 128).T)
        cols.append(np.repeat(np.asarray(inputs["attn_sinks"])[l], 64).reshape(4, 128).T)
        pc_ = np.concatenate(cols, 1)
        assert pc_.shape[1] == NPC
        pcol[:, l * NPC:(l + 1) * NPC] = pc_
    shared["pcol"] = pcol
    shared["post_norm_w"] = f(inputs["post_norm_w"])
    shared["pre_norm_w"] = f(inputs["pre_norm_w"])
    shared["rwkv_ln_w"] = f(inputs["rwkv_ln_w"])
    shared["rwkv_ln_b"] = f(inputs["rwkv_ln_b"])
    shared["sgu_b"] = f(np.asarray(inputs["sgu_b"]).reshape(L, 512))
    shared["rwkv_w_up"] = f(inputs["rwkv_w_up"])
    shared["rwkv_a_up"] = f(inputs["rwkv_a_up"])
    shared["pool_w"] = f(inputs["pool_w"])
    shared["sgu_wT"] = f(np.asarray(inputs["sgu_w"]).transpose(0, 1, 3, 2))
    cbf, cf32, ic = make_consts()
    shared["cbf"] = cbf
    shared["cf32"] = cf32
    shared["invcnt"] = ic
    return shared


def kernel(**inputs):
    x = np.asarray(inputs["x"], dtype=np.float32)
    shared = host_layout(inputs)
    if "nc" not in _CACHE:
        _CACHE["nc"] = build()
    nc = _CACHE["nc"]
    in_maps = []
    for c in range(8):
        m = dict(shared)
        m["x"] = np.ascontiguousarray(x[c % 4])
        in_maps.append(m)
    res = run_bass_kernel_spmd(nc, in_maps, core_ids=list(range(8)))
    out = np.stack([res.results[c]["out"] for c in range(4)], 0)
    return out.astype(np.float32)
```
